# Optimizing a Trainium2 kernel written in Bass

```python
import math
import jax, jax.numpy as jnp
from jax import lax
import numpy as np

D_MODEL = 1024
BATCH = 8
SEQ = 4096
DEPTH = 1
DEC_BATCH = 32
DEC_SEQ = 1
PAST_LEN = 16384
PAGE_SIZE = 128

DK_A = 128
DV_A = 128
H_A = D_MODEL // 256
CONV_W = 4
CONV_CH = H_A * (2 * DK_A + DV_A)
CHUNK = 64
HD_B = 64
H_B = D_MODEL // 128
N_KV_B = 2
H_IDX = 8
D_IDX = 64
TOPK_MAX = 256
Q_BLOCK = 128
ROPE_THETA = 10000.0
D_FF = 4 * D_MODEL
EPS = 1e-6
POOL_NUM = 5
POOL_DEN = 4

IN_SIZES = (H_A * DK_A, H_A * DK_A, H_A * DV_A, H_A * DV_A, H_A, H_A,
            H_B * HD_B, N_KV_B * HD_B, N_KV_B * HD_B, H_IDX * D_IDX, D_IDX, H_IDX,
            D_MODEL, D_MODEL)
N_IN = sum(IN_SIZES)

kernel_name = 'hybrid_gdn_dsa_parallel_decode_step'


def rms_norm(x, g):
    xf = x.astype(jnp.float32)
    y = xf * lax.rsqrt(jnp.mean(xf * xf, axis=-1, keepdims=True) + EPS)
    return (y * g.astype(jnp.float32)).astype(x.dtype)


def l2_norm(x):
    return x * lax.rsqrt(jnp.sum(x * x, axis=-1, keepdims=True) + EPS)


def rope(x, pos):
    half = x.shape[-1] // 2
    inv = ROPE_THETA ** (-jnp.arange(half, dtype=jnp.float32) / half)
    ang = pos.astype(jnp.float32)[:, None] * inv[None, :]
    cos = jnp.cos(ang)[None, :, None, :]
    sin = jnp.sin(ang)[None, :, None, :]
    xf = x.astype(jnp.float32)
    x1, x2 = xf[..., :half], xf[..., half:]
    return jnp.concatenate([x1 * cos - x2 * sin, x2 * cos + x1 * sin], axis=-1).astype(x.dtype)


def causal_conv(u, prev, conv_w):
    t = u.shape[1]
    up = jnp.concatenate([prev.astype(u.dtype), u], axis=1)
    acc = up[:, 0:t] * conv_w[0]
    for j in range(1, CONV_W):
        acc = acc + up[:, j:j + t] * conv_w[j]
    return jax.nn.silu(acc), up[:, t:]


def gated_delta_chunked(q, k, v, g, beta, s0):
    b, t, h, _ = q.shape
    dv = v.shape[-1]
    nc = t // CHUNK

    def chunks(a):
        return a.reshape(b, nc, CHUNK, *a.shape[2:]).swapaxes(2, 3)

    q, k, v, g, beta = (chunks(a) for a in (q, k, v, g, beta))
    gc = jnp.cumsum(g, axis=-1)
    diff = gc[..., :, None] - gc[..., None, :]
    strict = jnp.tril(jnp.ones((CHUNK, CHUNK), dtype=bool), -1)
    incl = jnp.tril(jnp.ones((CHUNK, CHUNK), dtype=bool))
    dec_strict = jnp.where(strict, jnp.exp(jnp.where(strict, diff, 0.0)), 0.0)
    dec_incl = jnp.where(incl, jnp.exp(jnp.where(incl, diff, 0.0)), 0.0)
    kb = k * beta[..., None]
    a_mat = jnp.einsum('bnhid,bnhjd->bnhij', kb, k) * dec_strict + jnp.eye(CHUNK, dtype=jnp.float32)
    rhs = jnp.concatenate([v * beta[..., None], kb * jnp.exp(gc)[..., None]], axis=-1)
    sol = lax.linalg.triangular_solve(a_mat, rhs, left_side=True, lower=True, unit_diagonal=True)
    u_c, w_c = sol[..., :dv], sol[..., dv:]
    qk = jnp.einsum('bnhid,bnhjd->bnhij', q, k) * dec_incl
    g_last = gc[..., -1:]
    q_dec = q * jnp.exp(gc)[..., None]
    k_dec = k * jnp.exp(g_last - gc)[..., None]
    c_dec = jnp.exp(g_last[..., 0])

    def step(s, inp):
        uu, ww, qd, kd, qkc, cd = inp
        vn = uu - jnp.einsum('bhcd,bhde->bhce', ww, s)
        o = jnp.einsum('bhcd,bhde->bhce', qd, s) + jnp.einsum('bhij,bhje->bhie', qkc, vn)
        s = cd[..., None, None] * s + jnp.einsum('bhcd,bhce->bhde', kd, vn)
        return s, o

    xs = tuple(jnp.moveaxis(a, 1, 0) for a in (u_c, w_c, q_dec, k_dec, qk, c_dec))
    s_fin, o = lax.scan(step, s0.astype(jnp.float32), xs)
    o = o.transpose(1, 0, 3, 2, 4).reshape(b, t, h, dv)
    return o, s_fin


def gated_delta_recurrent(q, k, v, g, beta, s0):
    def step(s, inp):
        qt, kt, vt, gt, bt = inp
        s = s * jnp.exp(gt)[..., None, None]
        vn = bt[..., None] * (vt - jnp.einsum('bhd,bhde->bhe', kt, s))
        s = s + jnp.einsum('bhd,bhe->bhde', kt, vn)
        return s, jnp.einsum('bhd,bhde->bhe', qt, s)

    xs = tuple(a.swapaxes(0, 1) for a in (q, k, v, g, beta))
    s_fin, o = lax.scan(step, s0.astype(jnp.float32), xs)
    return o.swapaxes(0, 1), s_fin


def dsa_attend(q, qi, wi, q_pos, k, v, ki, k_top):
    b, tq = q.shape[:2]
    k_pos = jnp.arange(k.shape[1])
    causal = k_pos[None, :] <= q_pos[:, None]
    s = jnp.einsum('bqhd,bsd->bqhs', qi, ki, preferred_element_type=jnp.float32) * (D_IDX ** -0.5)
    score = jnp.einsum('bqhs,bqh->bqs', jax.nn.relu(s), wi.astype(jnp.float32))
    score = jnp.where(causal[None], score, -jnp.inf)
    _, idx = lax.top_k(score, k_top)
    valid = idx <= q_pos[None, :, None]
    gather = jax.vmap(lambda rows, ii: rows[ii])
    k_sel = gather(k, idx)
    v_sel = gather(v, idx)
    qg = q.reshape(b, tq, N_KV_B, H_B // N_KV_B, HD_B)
    logits = jnp.einsum('bqngd,bqknd->bqngk', qg, k_sel, preferred_element_type=jnp.float32) * (HD_B ** -0.5)
    logits = jnp.where(valid[:, :, None, None, :], logits, -jnp.inf)
    p = jax.nn.softmax(logits, axis=-1)
    o = jnp.einsum('bqngk,bqknd->bqngd', p.astype(v.dtype), v_sel)
    return o.reshape(b, tq, H_B * HD_B)


def hybrid_layer(x, pos, ssm0, conv0, past_kv, past_ki, norm1_g, w_in, conv_w, a_log, dt_bias,
                 gdn_norm_g, q_norm_g, k_norm_g, w_out_a, w_out_b, w_o, norm2_g, w_up, w_down):
    b, t, _ = x.shape
    f32 = jnp.float32
    xn = rms_norm(x, norm1_g)
    split_pts = np.cumsum(IN_SIZES)[:-1].tolist()
    (aq, ak, av, az, ab, aa, bq, bk, bv, iq, ik, iw, ga, gb) = jnp.split(xn @ w_in, split_pts, axis=-1)

    conv_out, conv_new = causal_conv(jnp.concatenate([aq, ak, av], axis=-1), conv0, conv_w)
    cq, ck, cv = jnp.split(conv_out.astype(f32), [H_A * DK_A, 2 * H_A * DK_A], axis=-1)
    qa = l2_norm(cq.reshape(b, t, H_A, DK_A)) * (DK_A ** -0.5)
    ka = l2_norm(ck.reshape(b, t, H_A, DK_A))
    va = cv.reshape(b, t, H_A, DV_A)
    beta = jax.nn.sigmoid(ab.astype(f32))
    g = -jnp.exp(a_log.astype(f32)) * jax.nn.softplus(aa.astype(f32) + dt_bias.astype(f32))
    if past_kv is None:
        oa, ssm_new = gated_delta_chunked(qa, ka, va, g, beta, ssm0)
    else:
        oa, ssm_new = gated_delta_recurrent(qa, ka, va, g, beta, ssm0)
    oa = rms_norm(oa, gdn_norm_g) * jax.nn.silu(az.reshape(b, t, H_A, DV_A).astype(f32))
    oa = oa.astype(x.dtype).reshape(b, t, H_A * DV_A)

    qb = rope(rms_norm(bq.reshape(b, t, H_B, HD_B), q_norm_g), pos)
    kb = rope(rms_norm(bk.reshape(b, t, N_KV_B, HD_B), k_norm_g), pos)
    vb = bv.reshape(b, t, N_KV_B, HD_B)
    qi = rope(iq.reshape(b, t, H_IDX, D_IDX), pos)
    ki = rope(ik.reshape(b, t, 1, D_IDX), pos)[:, :, 0]
    wi = iw * (H_IDX ** -0.5)
    kv_new = jnp.stack([kb, vb], axis=2)
    if past_kv is None:
        k_top = min(TOPK_MAX, t // 4)
        nb = t // Q_BLOCK

        def blk(a):
            return a.reshape(b, nb, Q_BLOCK, *a.shape[2:]).swapaxes(0, 1)

        ob = lax.map(lambda a: dsa_attend(a[0], a[1], a[2], a[3], kb, vb, ki, k_top),
                     (blk(qb), blk(qi), blk(wi), pos.reshape(nb, Q_BLOCK)))
        ob = ob.swapaxes(0, 1).reshape(b, t, H_B * HD_B)
    else:
        k_all = jnp.concatenate([past_kv[:, :, 0].astype(kb.dtype), kb], axis=1)
        v_all = jnp.concatenate([past_kv[:, :, 1].astype(vb.dtype), vb], axis=1)
        ki_all = jnp.concatenate([past_ki.astype(ki.dtype), ki], axis=1)
        k_top = min(TOPK_MAX, k_all.shape[1] // 4)
        ob = dsa_attend(qb, qi, wi, pos, k_all, v_all, ki_all, k_top)

    mix = jax.nn.sigmoid(ga) * (oa @ w_out_a) + jax.nn.sigmoid(gb) * (ob @ w_out_b)
    h = x + mix @ w_o
    hn = rms_norm(h, norm2_g)
    y = h + jnp.square(jax.nn.relu(hn @ w_up)) @ w_down
    return y, kv_new, ki, ssm_new.astype(ssm0.dtype), conv_new


def setup_inputs(seed: int = 0) -> dict:
    key = jax.random.key(seed)
    ks = jax.random.split(key, 24)
    f32 = jnp.float32
    n_pages = PAST_LEN // PAGE_SIZE
    n_used = DEC_BATCH * n_pages
    n_pool = (n_used * POOL_NUM) // POOL_DEN

    def nrm(k, shape, scale):
        return jax.random.normal(k, shape, f32) * scale

    def gain(k, shape):
        return 1.0 + 0.02 * jax.random.normal(k, shape, f32)

    page_table = jax.random.permutation(ks[4], n_pool)[:n_used].reshape(DEC_BATCH, n_pages).astype(jnp.int32)
    dt = jnp.exp(jax.random.uniform(ks[10], (DEPTH, H_A), f32, math.log(1e-3), math.log(1e-1)))
    dt_bias = dt + jnp.log(-jnp.expm1(-dt))
    a_log = jnp.log(jax.random.uniform(ks[9], (DEPTH, H_A), f32, 1.0, 16.0))
    return {
        'x_prompt': nrm(ks[0], (BATCH, SEQ, D_MODEL), 1.0),
        'x_sample': nrm(ks[1], (DEC_BATCH, DEC_SEQ, D_MODEL), 1.0),
        'cache_kv': nrm(ks[2], (DEPTH, n_pool, PAGE_SIZE, 2, N_KV_B, HD_B), 1.0),
        'cache_kidx': nrm(ks[3], (DEPTH, n_pool, PAGE_SIZE, D_IDX), 1.0),
        'page_table': page_table,
        'state_ssm': nrm(ks[5], (DEPTH, DEC_BATCH, H_A, DK_A, DV_A), 0.3),
        'state_conv': nrm(ks[6], (DEPTH, DEC_BATCH, CONV_W - 1, CONV_CH), 1.0),
        'norm1_g': gain(ks[7], (DEPTH, D_MODEL)),
        'w_in': nrm(ks[8], (DEPTH, D_MODEL, N_IN), D_MODEL ** -0.5),
        'conv_w': nrm(ks[11], (DEPTH, CONV_W, CONV_CH), CONV_W ** -0.5),
        'a_log': a_log,
        'dt_bias': dt_bias,
        'gdn_norm_g': gain(ks[12], (DEPTH, DV_A)),
        'q_norm_g': gain(ks[13], (DEPTH, HD_B)),
        'k_norm_g': gain(ks[14], (DEPTH, HD_B)),
        'w_out_a': nrm(ks[15], (DEPTH, H_A * DV_A, D_MODEL), (H_A * DV_A) ** -0.5),
        'w_out_b': nrm(ks[16], (DEPTH, H_B * HD_B, D_MODEL), (H_B * HD_B) ** -0.5),
        'w_o': nrm(ks[17], (DEPTH, D_MODEL, D_MODEL), D_MODEL ** -0.5),
        'norm2_g': gain(ks[18], (DEPTH, D_MODEL)),
        'w_up': nrm(ks[19], (DEPTH, D_MODEL, D_FF), D_MODEL ** -0.5),
        'w_down': nrm(ks[20], (DEPTH, D_FF, D_MODEL), D_FF ** -0.5),
    }


def reference(x_prompt, x_sample, cache_kv, cache_kidx, page_table, state_ssm, state_conv,
              norm1_g, w_in, conv_w, a_log, dt_bias, gdn_norm_g, q_norm_g, k_norm_g,
              w_out_a, w_out_b, w_o, norm2_g, w_up, w_down):
    b_p, t_p, _ = x_prompt.shape
    b_s, t_s, _ = x_sample.shape
    n_pages = page_table.shape[1]
    past_len = n_pages * PAGE_SIZE
    pos_p = jnp.arange(t_p)
    pos_s = past_len + jnp.arange(t_s)
    hp, hs = x_prompt, x_sample
    outs_p, outs_s = [], []
    for l in range(DEPTH):
        w = (norm1_g[l], w_in[l], conv_w[l], a_log[l], dt_bias[l], gdn_norm_g[l], q_norm_g[l],
             k_norm_g[l], w_out_a[l], w_out_b[l], w_o[l], norm2_g[l], w_up[l], w_down[l])
        ssm0_p = jnp.zeros((b_p, H_A, DK_A, DV_A), x_prompt.dtype)
        conv0_p = jnp.zeros((b_p, CONV_W - 1, CONV_CH), x_prompt.dtype)
        hp, kv_p, ki_p, ssm_p, cv_p = hybrid_layer(hp, pos_p, ssm0_p, conv0_p, None, None, *w)
        past_kv = cache_kv[l][page_table].reshape(b_s, past_len, 2, N_KV_B, HD_B)
        past_ki = cache_kidx[l][page_table].reshape(b_s, past_len, D_IDX)
        hs, kv_s, ki_s, ssm_s, cv_s = hybrid_layer(hs, pos_s, state_ssm[l], state_conv[l], past_kv, past_ki, *w)
        outs_p.append((kv_p, ki_p, ssm_p, cv_p))
        outs_s.append((kv_s, ki_s, ssm_s, cv_s))
    kv_prompt, kidx_prompt, ssm_prompt, conv_prompt = (jnp.stack(a) for a in zip(*outs_p))
    kv_sample, kidx_sample, ssm_sample, conv_sample = (jnp.stack(a) for a in zip(*outs_s))
    return (hp, hs, kv_prompt, kidx_prompt, ssm_prompt, conv_prompt, kv_sample, kidx_sample, ssm_sample, conv_sample)
```

```python
import contextlib
import numpy as np
import concourse.bass as bass
import concourse.mybir as mybir
from concourse.bass_utils import run_bass_kernel_spmd

F32 = mybir.dt.float32
BF16 = mybir.dt.bfloat16
I32 = mybir.dt.int32
AF = mybir.ActivationFunctionType
ALU = mybir.AluOpType
AX = mybir.AxisListType

NCORES = 8
NCR = 8
T = 4096
NB = 33
TT = NB * 128
D = 1024
NIN = 5456
EPS = 1e-6
PAST = 16384
NPOOL = 5120
P2_BLOCKS = 32
P1_ENABLE = True
P2_STAGE = 99
DBG = 0
P2_SAMPLE = True
P3_BLOCKS = 32
P4A_TILES = 9
DEBUG_OUT = False
_DBG = {}
P3S_ENABLE = True
P3_STAGE = 99
P4B_TILES = 17


class Buf:
    def __init__(self, t, name):
        self.t = t
        self.name = name
        self.w = {}
        self.r = {}
        self.excl = False


class Kern:
    STREAMS = ('pe', 'act', 'dve', 'pool', 'sp')
    NROT = {'sp': 24, 'pool': 8}
    SEMS = ('pe', 'act', 'dve', 'pool') + tuple(f'sp_dma{i}' for i in range(24)) + tuple(f'pool_dma{i}' for i in range(8))

    def __init__(self, nc):
        self.nc = nc
        self.stack = [contextlib.ExitStack()]
        self.rec = {s: [] for s in self.STREAMS}
        self.cnt = {}
        self.seen = {s: {} for s in self.STREAMS}
        self.sems = {}
        self.rr = {'sp': 0, 'pool': 0}

    def __enter__(self):
        self.stack[0].__enter__()
        for k in self.SEMS:
            self.sems[k] = self.stack[0].enter_context(self.nc.semaphore("s_" + k))
            self.cnt[k] = 0
        return self

    def __exit__(self, *a):
        return self.stack[0].__exit__(*a)

    @contextlib.contextmanager
    def scope(self):
        es = contextlib.ExitStack()
        es.__enter__()
        self.stack.append(es)
        try:
            yield
        finally:
            self.barrier()
            self.stack.pop()
            es.__exit__(None, None, None)

    def push_scope(self):
        es = contextlib.ExitStack()
        es.__enter__()
        self.stack.append(es)

    def pop_scope(self):
        self.barrier()
        es = self.stack.pop()
        es.__exit__(None, None, None)

    def sb(self, name, shape, dt):
        return Buf(self.stack[-1].enter_context(self.nc.sbuf_tensor("sb_" + name, list(shape), dt)), name)

    def ps(self, name, shape, dt):
        n = 2048 // (2 if dt == BF16 else 4)
        h = self.stack[-1].enter_context(self.nc.psum_tensor("ps_" + name, [128, n], dt))
        free = 1
        for d in shape[1:]:
            free *= d
        ap = h[0:shape[0], 0:free]
        if len(shape) == 3:
            ap = ap.rearrange("p (a b) -> p a b", a=shape[1])
        elif len(shape) == 4:
            ap = ap.rearrange("p (a b c) -> p a b c", a=shape[1], b=shape[2])
        b = Buf(ap, name)
        b.excl = True
        return b

    def dram(self, name, shape, dt, kind="Internal"):
        return Buf(self.nc.dram_tensor(name, list(shape), dt, kind=kind), name)

    def _wait(self, stream, k, v):
        if self.seen[stream].get(k, 0) < v:
            self.seen[stream][k] = v
            self.rec[stream].append(('wait', k, v))

    def _deps(self, stream, reads, writes, self_sync=True):
        deps = {}

        def add(k, v):
            if k == 'pe' and stream == 'pe':
                return
            if k == stream and not self_sync:
                return
            if deps.get(k, 0) < v:
                deps[k] = v
        for b in reads:
            for k, v in b.w.items():
                add(k, v)
            if b.excl:
                for k, v in b.r.items():
                    if k != stream:
                        add(k, v)
        for b in writes:
            for k, v in b.w.items():
                add(k, v)
            for k, v in b.r.items():
                add(k, v)
        for k, v in deps.items():
            self._wait(stream, k, v)

    def _mark(self, key, val, reads, writes):
        for b in writes:
            b.w[key] = val
        for b in reads:
            b.r[key] = val

    def op(self, stream, fn, reads=(), writes=(), self_sync=True):
        reads, writes = list(reads), list(writes)
        self._deps(stream, reads, writes, self_sync)
        self.cnt[stream] += 1
        self.rec[stream].append(('op', fn, stream, 1))
        self._mark(stream, self.cnt[stream], reads, writes)

    def dma(self, stream, out, in_, reads=(), writes=(), fn=None, **kw):
        reads, writes = list(reads), list(writes)
        self._deps(stream, reads, writes)
        key = f"{stream}_dma{self.rr[stream] % self.NROT[stream]}"
        self.rr[stream] += 1
        self.cnt[key] += 16
        if fn is None:
            fn = (lambda e, out=out, in_=in_, kw=kw: e.dma_start(out=out, in_=in_, **kw))
        self.rec[stream].append(('op', fn, key, 16))
        self._mark(key, self.cnt[key], reads, writes)

    def barrier(self):
        for s in self.STREAMS:
            for k in self.SEMS:
                if self.cnt[k] > 0:
                    self._wait(s, k, self.cnt[k])

    def finish(self):
        self.barrier()
        nc, rec, sems = self.nc, self.rec, self.sems

        def play(eng, items):
            for it in items:
                if it[0] == 'wait':
                    eng.wait_ge(sems[it[1]], it[2])
                else:
                    it[1](eng).then_inc(sems[it[2]], it[3])

        with nc.Block() as block:
            @block.tensor
            def _(e):
                play(e, rec['pe'])

            @block.scalar
            def _(e):
                play(e, rec['act'])

            @block.vector
            def _(e):
                play(e, rec['dve'])

            @block.gpsimd
            def _(e):
                play(e, rec['pool'])

            @block.sync
            def _(e):
                play(e, rec['sp'])


CST_NAMES = ['ident', 'ones', 'U', 'BD', 'CM0', 'CM1', 'MsT', 'Ms', 'MiT', 'C0', 'C1', 'CA']


def make_consts():
    i = np.arange(128)
    same = (i[:, None] // 64) == (i[None, :] // 64)
    c = {}
    c['ident'] = np.eye(128)
    c['ones'] = np.ones((128, 128))
    c['U'] = same & (i[:, None] <= i[None, :])
    c['BD'] = same
    c['CM0'] = np.broadcast_to((i[:, None] < 64), (128, 128))
    c['CM1'] = np.broadcast_to((i[:, None] >= 64), (128, 128))
    c['MsT'] = same & (i[None, :] > i[:, None])
    c['Ms'] = same & (i[:, None] > i[None, :])
    c['MiT'] = same & (i[None, :] >= i[:, None])
    c['C0'] = np.broadcast_to((i[None, :] < 64), (128, 128))
    c['C1'] = np.broadcast_to((i[None, :] >= 64), (128, 128))
    c['CA'] = np.where(i[None, :] <= i[:, None], 0.0, -1e30)
    return np.stack([np.asarray(c[n], dtype=np.float32) for n in CST_NAMES], axis=1)


def rope_table():
    half = 32
    inv = (np.float32(10000.0) ** (-np.arange(half, dtype=np.float32) / np.float32(half))).astype(np.float32)
    pos = np.zeros(TT, dtype=np.float32)
    pos[:T] = np.arange(T, dtype=np.float32)
    pos[T:T + 4] = np.float32(PAST)
    ang = (pos[:, None] * inv[None, :]).astype(np.float32)
    return np.concatenate([np.cos(ang), np.sin(ang)], axis=1).astype(np.float32)


def build_program():
    nc = bass.Bass("TRN2", target_bir_lowering=False)

    def din(name, shape, dt=F32):
        return nc.dram_tensor(name, list(shape), dt, kind="ExternalInput").ap()

    def dout(name, shape, dt=F32):
        return nc.dram_tensor(name, list(shape), dt, kind="ExternalOutput").ap()

    x_in = din("x", [TT, D])
    cs_in = din("cs", [TT, 64])
    cst_in = din("cst", [128, len(CST_NAMES), 128])
    w_in = din("w_in", [D, NIN])
    norm1 = din("norm1_g", [D])
    conv_w = din("conv_w", [4, 1536])
    a_log = din("a_log", [4])
    dt_bias = din("dt_bias", [4])
    gdn_g = din("gdn_norm_g", [128])
    q_g = din("q_norm_g", [64])
    k_g = din("k_norm_g", [64])
    st_conv = din("state_conv", [4, 3, 1536])
    st_ssm = din("state_ssm", [4, 4, 128, 128])
    cache_kidx = din("cache_kidx", [NPOOL, 128 * 64])
    cache_kv = din("cache_kv", [NPOOL, 128 * 256])
    page_tab = din("page_tab", [4, 128], I32)
    w_oa = din("w_out_a", [512, D])
    w_ob = din("w_out_b", [512, D])
    w_oo = din("w_o", [D, D])
    norm2 = din("norm2_g", [D])
    w_up_in = din("w_up", [D, 4 * D])
    w_dn_in = din("w_down", [4 * D, D])

    y_out = dout("y", [TT, D])
    kv_out = dout("kv", [TT, 256])
    kidx_out = dout("kidx", [TT, 64])
    convp_out = dout("conv_p", [3, 1536])
    convs_out = dout("conv_s", [4, 3, 1536])
    ssmp_out = dout("ssm_p", [4, 128, 128])
    ssms_out = dout("ssm_s", [4, 4, 128, 128])

    K = Kern(nc)
    with K:
        cst = K.sb("cst", [128, len(CST_NAMES), 128], F32)
        K.dma('sp', cst.t[:], cst_in[:, :, :], writes=[cst])

        def C(name):
            return cst.t[:, CST_NAMES.index(name), :]
        identb = K.sb("identb", [128, 128], BF16)
        K.op('dve', lambda e: e.tensor_copy(identb.t[:], C('ident')), reads=[cst], writes=[identb])

        K.push_scope()
        kbT2 = K.sb("kbT2", [128, 2, TT], BF16)
        kiT2 = K.sb("kiT2", [128, TT], BF16)
        v65 = K.sb("v65", [128, NB, 2, 65], BF16)
        gtm = K.sb("gtm", [128, NB, 4], F32)
        btm = K.sb("btm", [128, NB, 4], F32)
        wab = K.sb("wab", [128, NB, 8], F32)
        wsg = K.sb("wsg", [128, NB, 8], F32)
        K.op('pool', lambda e: e.memset(v65.t[:], 1.0), writes=[v65])

        prm = K.sb("prm", [128, 4 + 4 + 128 + 64 + 64], F32)
        K.dma('sp', prm.t[:, 0:4], a_log.partition_broadcast(128), writes=[prm])
        K.dma('sp', prm.t[:, 4:8], dt_bias.partition_broadcast(128), writes=[prm])
        K.dma('sp', prm.t[:, 8:136], gdn_g.partition_broadcast(128), writes=[prm])
        K.dma('sp', prm.t[:, 136:200], q_g.partition_broadcast(128), writes=[prm])
        K.dma('sp', prm.t[:, 200:264], k_g.partition_broadcast(128), writes=[prm])
        negA = K.sb("negA", [128, 4], F32)
        K.op('act', lambda e: e.activation(negA.t[:], prm.t[:, 0:4], AF.Exp), reads=[prm], writes=[negA])
        K.op('dve', lambda e: e.tensor_scalar(negA.t[:], negA.t[:], -1.0, None, op0=ALU.mult), reads=[negA], writes=[negA])
        g1T = K.sb("g1T", [128, 8], F32)
        K.dma('sp', g1T.t[:], norm1.rearrange("(k p) -> p k", p=128), writes=[g1T], allow_slow_non_contiguous=True)
        cw = K.sb("cw", [128, 12, 4], F32)
        for j in range(4):
            K.dma('sp', cw.t[:, :, j], conv_w[j].rearrange("(c p) -> p c", p=128), writes=[cw], allow_slow_non_contiguous=True)

        cT = K.dram("cT_s", [12, 128, TT], F32)
        zT = K.dram("zT_s", [4, 128, TT], F32)
        sgT = K.dram("sgT_s", [16, 128, TT], F32)
        qsT = K.dram("qsT_s", [NB, 128, 8, 128], BF16)

        with (K.scope() if P1_ENABLE else contextlib.nullcontext()):
          if P1_ENABLE:
            xnT = K.sb("xnT", [128, 8, TT], BF16)
            with K.scope():
                xt = [K.sb(f"xt{i}", [128, D], F32) for i in range(2)]
                junk = K.sb("junk", [128, D], F32)
                ssq = [K.sb(f"ssq{i}", [128, 2], F32) for i in range(2)]
                xs = [K.sb(f"xs{i}", [128, D], BF16) for i in range(2)]
                pT = [K.ps(f"pT{i}", [128, 8, 128], BF16) for i in range(2)]
                for b in range(NB):
                    i = b % 2
                    K.dma('sp', xt[i].t[:], x_in[b * 128:(b + 1) * 128, :], writes=[xt[i]])
                    K.op('act', lambda e, i=i: e.activation(junk.t[:], xt[i].t[:], AF.Square, accum_out=ssq[i].t[:, 0:1]),
                         reads=[xt[i]], writes=[junk, ssq[i]])
                    K.op('act', lambda e, i=i: e.activation(ssq[i].t[:, 1:2], ssq[i].t[:, 0:1], AF.Sqrt, bias=EPS, scale=1.0 / D),
                         reads=[ssq[i]], writes=[ssq[i]])
                    K.op('dve', lambda e, i=i: e.reciprocal(ssq[i].t[:, 0:1], ssq[i].t[:, 1:2]), reads=[ssq[i]], writes=[ssq[i]])
                    K.op('dve', lambda e, i=i: e.tensor_scalar(xs[i].t[:], xt[i].t[:], ssq[i].t[:, 0:1], None, op0=ALU.mult),
                         reads=[xt[i], ssq[i]], writes=[xs[i]])
                    for k in range(8):
                        K.op('pe', lambda e, i=i, k=k: e.transpose(pT[i].t[:, k, :], xs[i].t[:, k * 128:(k + 1) * 128], identb.t[:]),
                             reads=[xs[i], identb], writes=[pT[i]])
                    K.op('dve', lambda e, i=i, b=b: e.tensor_tensor(
                        xnT.t[:, :, b * 128:(b + 1) * 128], pT[i].t[:], g1T.t[:, :].unsqueeze(2).to_broadcast([128, 8, 128]), ALU.mult),
                        reads=[pT[i], g1T], writes=[xnT])

            with K.scope():
                wtm = K.sb("wtm", [128, 8, 1360], BF16)
                wst = [K.sb(f"wst{i}", [128, 1360], F32) for i in range(2)]
                for k in range(8):
                    i = k % 2
                    K.dma('sp', wst[i].t[:], w_in[k * 128:(k + 1) * 128, 2048:3408], writes=[wst[i]])
                    K.op('act', lambda e, i=i, k=k: e.copy(wtm.t[:, k, :], wst[i].t[:]), reads=[wst[i]], writes=[wtm])
                pSm = K.ps("pSm", [128, 512], F32)
                pQ = K.ps("pQ", [128, 512], F32)
                pKV = K.ps("pKV", [128, 512], F32)
                pI = K.ps("pI", [128, 512], F32)
                pTq = K.ps("pTq", [128, 8, 128], BF16)
                pTk = K.ps("pTk", [128, 4, 128], BF16)
                cs = [K.sb(f"cs{i}", [128, 64], F32) for i in range(2)]
                sm = K.sb("sm", [128, 16], F32)
                sq = K.sb("sq", [128, 512], F32)
                qn = K.sb("qn", [128, 512], F32)
                rn = K.sb("rn", [128, 16], F32)
                t1 = K.sb("t1", [128, 8, 32], F32)
                t2 = K.sb("t2", [128, 8, 32], F32)
                qr = K.sb("qr", [128, 512], BF16)
                qir = K.sb("qir", [128, 512], BF16)
                kvo = [K.sb(f"kvo{i}", [128, 256], F32) for i in range(2)]
                kio = [K.sb(f"kio{i}", [128, 64], F32) for i in range(2)]
                kdup = K.sb("kdup", [128, 3, 128], BF16)
                qst = [K.sb(f"qst{i}", [128, 8, 128], BF16) for i in range(2)]

                def rope(src3, dst3, nh, cst_b, srcbufs, dstbufs):
                    cosb = cst_b.t[:, 0:32].unsqueeze(1).to_broadcast([128, nh, 32])
                    sinb = cst_b.t[:, 32:64].unsqueeze(1).to_broadcast([128, nh, 32])
                    a1, a2 = t1.t[:, 0:nh, :], t2.t[:, 0:nh, :]
                    x1, x2 = src3[:, :, 0:32], src3[:, :, 32:64]
                    K.op('dve', lambda e: e.tensor_tensor(a1, x1, cosb, ALU.mult), reads=srcbufs + [cst_b], writes=[t1])
                    K.op('dve', lambda e: e.tensor_tensor(a2, x2, sinb, ALU.mult), reads=srcbufs + [cst_b], writes=[t2])
                    K.op('dve', lambda e: e.tensor_tensor(dst3[:, :, 0:32], a1, a2, ALU.subtract), reads=[t1, t2], writes=dstbufs)
                    K.op('dve', lambda e: e.tensor_tensor(a1, x2, cosb, ALU.mult), reads=srcbufs + [cst_b], writes=[t1])
                    K.op('dve', lambda e: e.tensor_tensor(a2, x1, sinb, ALU.mult), reads=srcbufs + [cst_b], writes=[t2])
                    K.op('dve', lambda e: e.tensor_tensor(dst3[:, :, 32:64], a1, a2, ALU.add), reads=[t1, t2], writes=dstbufs)

                def qknorm(psrc, nh, gofs, psbuf):
                    n = nh * 64
                    K.op('act', lambda e: e.activation(sq.t[:, 0:n], psrc, AF.Square), reads=[psbuf], writes=[sq])
                    K.op('dve', lambda e: e.tensor_reduce(rn.t[:, 0:nh], sq.t[:, 0:n].rearrange("p (h d) -> p h d", d=64), AX.X, ALU.add),
                         reads=[sq], writes=[rn])
                    K.op('act', lambda e: e.activation(rn.t[:, 8:8 + nh], rn.t[:, 0:nh], AF.Sqrt, bias=EPS, scale=1.0 / 64),
                         reads=[rn], writes=[rn])
                    K.op('dve', lambda e: e.reciprocal(rn.t[:, 0:nh], rn.t[:, 8:8 + nh]), reads=[rn], writes=[rn])
                    q3 = qn.t[:, 0:n].rearrange("p (h d) -> p h d", d=64)
                    K.op('dve', lambda e: e.tensor_tensor(q3, psrc.rearrange("p (h d) -> p h d", d=64),
                                                          rn.t[:, 0:nh].unsqueeze(2).to_broadcast([128, nh, 64]), ALU.mult),
                         reads=[psbuf, rn], writes=[qn])
                    K.op('dve', lambda e: e.tensor_tensor(q3, q3, prm.t[:, gofs:gofs + 64].unsqueeze(1).to_broadcast([128, nh, 64]), ALU.mult),
                         reads=[qn, prm], writes=[qn])
                    return q3

                for b in range(NB):
                    i = b % 2
                    blk = slice(b * 128, (b + 1) * 128)
                    K.dma('sp', cs[i].t[:], cs_in[blk, :], writes=[cs[i]])
                    for (pt, c0, c1, o0) in ((pSm, 0, 8, 0), (pQ, 8, 520, 0), (pKV, 520, 776, 0), (pI, 776, 1288, 0), (pSm, 1288, 1360, 8)):
                        for k in range(8):
                            K.op('pe', lambda e, pt=pt, c0=c0, c1=c1, o0=o0, k=k, blk=blk: e.matmul(
                                pt.t[:, o0:o0 + (c1 - c0)], xnT.t[:, k, blk], wtm.t[:, k, c0:c1], start=(k == 0), stop=(k == 7)),
                                reads=[xnT, wtm], writes=[pt])
                    K.op('act', lambda e, b=b: e.activation(btm.t[:, b, :], pSm.t[:, 0:4], AF.Sigmoid), reads=[pSm], writes=[btm])
                    K.op('dve', lambda e: e.tensor_tensor(sm.t[:, 0:4], pSm.t[:, 4:8], prm.t[:, 4:8], ALU.add), reads=[pSm, prm], writes=[sm])
                    K.op('act', lambda e: e.activation(sm.t[:, 4:8], sm.t[:, 0:4], AF.Exp), reads=[sm], writes=[sm])
                    K.op('act', lambda e: e.activation(sm.t[:, 8:12], sm.t[:, 4:8], AF.Ln, bias=1.0), reads=[sm], writes=[sm])
                    K.op('dve', lambda e, b=b: e.tensor_tensor(gtm.t[:, b, :], sm.t[:, 8:12], negA.t[:], ALU.mult), reads=[sm, negA], writes=[gtm])
                    K.op('act', lambda e, b=b: e.activation(wab.t[:, b, :], pSm.t[:, 72:80], AF.Abs, scale=8.0 ** -0.5 / 8.0),
                         reads=[pSm], writes=[wab])
                    K.op('act', lambda e, b=b: e.sign(wsg.t[:, b, :], pSm.t[:, 72:80]), reads=[pSm], writes=[wsg])
                    q3 = qknorm(pQ.t[:, 0:512], 8, 136, pQ)
                    rope(q3, qr.t[:].rearrange("p (h d) -> p h d", d=64), 8, cs[i], [qn], [qr])
                    k3 = qknorm(pKV.t[:, 0:128], 2, 200, pKV)
                    rope(k3, kvo[i].t[:, 0:128].rearrange("p (h d) -> p h d", d=64), 2, cs[i], [qn], [kvo[i]])
                    K.op('act', lambda e, i=i: e.copy(kvo[i].t[:, 128:256], pKV.t[:, 128:256]), reads=[pKV], writes=[kvo[i]])
                    K.dma('sp', kv_out[blk, :], kvo[i].t[:], reads=[kvo[i]])
                    K.op('act', lambda e, i=i, b=b: e.copy(v65.t[:, b, :, 0:64], kvo[i].t[:, 128:256].rearrange("p (n d) -> p n d", d=64)),
                         reads=[kvo[i]], writes=[v65])
                    for n in range(2):
                        K.op('dve', lambda e, i=i, n=n: e.tensor_copy(
                            kdup.t[:, n, :].rearrange("p (r d) -> p r d", d=64),
                            kvo[i].t[:, n * 64:(n + 1) * 64].unsqueeze(1).to_broadcast([128, 2, 64])), reads=[kvo[i]], writes=[kdup])
                    rope(pI.t[:, 0:512].rearrange("p (h d) -> p h d", d=64), qir.t[:].rearrange("p (h d) -> p h d", d=64), 8, cs[i], [pI], [qir])
                    rope(pSm.t[:, 8:72].rearrange("p (h d) -> p h d", d=64), kio[i].t[:].rearrange("p (h d) -> p h d", d=64), 1, cs[i], [pSm], [kio[i]])
                    K.dma('sp', kidx_out[blk, :], kio[i].t[:], reads=[kio[i]])
                    K.op('dve', lambda e, i=i: e.tensor_copy(kdup.t[:, 2, :].rearrange("p (r d) -> p r d", d=64),
                                                             kio[i].t[:].unsqueeze(1).to_broadcast([128, 2, 64])), reads=[kio[i]], writes=[kdup])
                    for pr in range(4):
                        K.op('pe', lambda e, pr=pr: e.transpose(pTq.t[:, pr, :], qr.t[:, pr * 128:(pr + 1) * 128], identb.t[:]),
                             reads=[qr, identb], writes=[pTq])
                        K.op('pe', lambda e, pr=pr: e.transpose(pTq.t[:, 4 + pr, :], qir.t[:, pr * 128:(pr + 1) * 128], identb.t[:]),
                             reads=[qir, identb], writes=[pTq])
                    K.op('act', lambda e, i=i: e.copy(qst[i].t[:], pTq.t[:]), reads=[pTq], writes=[qst[i]])
                    K.dma('sp', qsT.t.ap()[b], qst[i].t[:], reads=[qst[i]], writes=[qsT])
                    for n in range(3):
                        K.op('pe', lambda e, n=n: e.transpose(pTk.t[:, n, :], kdup.t[:, n, :], identb.t[:]), reads=[kdup, identb], writes=[pTk])
                    K.op('act', lambda e, blk=blk: e.copy(kbT2.t[:, :, blk], pTk.t[:, 0:2, :]), reads=[pTk], writes=[kbT2])
                    K.op('act', lambda e, blk=blk: e.copy(kiT2.t[:, blk], pTk.t[:, 2, :]), reads=[pTk], writes=[kiT2])

            with K.scope():
                uT = K.sb("uT", [128, 3 + TT], BF16)
                dg = K.sb("dg", [128, 12, 4, 128], BF16)
                cwb = K.sb("cwb", [128, 12, 4], F32)
                for ch in range(12):
                    for j in range(4):
                        K.op('dve', lambda e, ch=ch, j=j: e.tensor_scalar(dg.t[:, ch, j, :], C('ident'), cw.t[:, ch, j:j + 1], None, op0=ALU.mult),
                             reads=[cst, cw], writes=[dg])
                K.op('pool', lambda e: e.memset(uT.t[:, 0:3], 0.0), writes=[uT])
                stT = K.sb("stT", [128, 12, 3, 4], F32)
                for s in range(4):
                    for j in range(3):
                        K.dma('sp', stT.t[:, :, j, s], st_conv[s, j].rearrange("(c p) -> p c", p=128), writes=[stT], allow_slow_non_contiguous=True)
                cpre = K.sb("cpre", [128, 12, 3], F32)
                cso = K.sb("cso", [128, 12, 4, 3], F32)
                us = K.sb("us", [128, 4], F32)
                acc4 = K.sb("acc4", [128, 4], F32)
                wfs = [K.sb(f"wfs{i}", [128, 8, 128], F32) for i in range(2)]
                wfb = [K.sb(f"wfb{i}", [128, 8, 128], BF16) for i in range(2)]
                pF = [K.ps(f"pF{i}", [128, 512], F32) for i in range(2)]
                pC = [K.ps(f"pC{i}", [128, 512], F32) for i in range(2)]
                pN = K.ps("pN", [128, 512], F32)
                cv = [K.sb(f"cv{i}", [128, 512], F32) for i in range(2)]
                sq2 = K.sb("sq2", [128, 512], F32)
                rr = K.sb("rr", [128, 512], F32)
                co = [K.sb(f"co{i}", [128, 512], F32) for i in range(2)]
                NT = 9
                chunks = [('conv', c) for c in range(12)] + [('z', h) for h in range(4)] + [('gate', c) for c in range(16)]
                nf = 0
                for ci, (kind, idx) in enumerate(chunks):
                    wi_ = ci % 2
                    col0 = idx * 128 if kind == 'conv' else (1536 + idx * 128 if kind == 'z' else 3408 + idx * 128)
                    K.dma('sp', wfs[wi_].t[:], w_in[:, col0:col0 + 128].rearrange("(k p) c -> p k c", p=128), writes=[wfs[wi_]])
                    K.op('pool', lambda e, wi_=wi_: e.tensor_copy(wfb[wi_].t[:], wfs[wi_].t[:]), reads=[wfs[wi_]], writes=[wfb[wi_]])
                    for t in range(NT):
                        w = 512 if t < 8 else 128
                        tok = slice(t * 512, t * 512 + w)
                        pf = pF[nf % 2]
                        nf += 1
                        for k in range(8):
                            K.op('pe', lambda e, pf=pf, k=k, tok=tok, w=w, wi_=wi_: e.matmul(
                                pf.t[:, 0:w], wfb[wi_].t[:, k, :], xnT.t[:, k, tok], start=(k == 0), stop=(k == 7)),
                                reads=[wfb[wi_], xnT], writes=[pf])
                        if kind == 'conv':
                            K.op('act', lambda e, pf=pf, t=t, w=w: e.copy(uT.t[:, 3 + t * 512:3 + t * 512 + w], pf.t[:, 0:w]), reads=[pf], writes=[uT])
                            if t == 7:
                                K.op('dve', lambda e, pf=pf, idx=idx: e.tensor_copy(cpre.t[:, idx, :], pf.t[:, 509:512]), reads=[pf], writes=[cpre])
                            if t == 8:
                                K.op('dve', lambda e, pf=pf: e.tensor_copy(us.t[:], pf.t[:, 0:4]), reads=[pf], writes=[us])
                        elif kind == 'z':
                            o = co[nf % 2]
                            K.op('act', lambda e, pf=pf, o=o, w=w: e.activation(o.t[:, 0:w], pf.t[:, 0:w], AF.Silu), reads=[pf], writes=[o])
                            K.dma('sp', zT.t.ap()[idx, :, tok], o.t[:, 0:w], reads=[o], writes=[zT])
                        else:
                            o = co[nf % 2]
                            K.op('act', lambda e, pf=pf, o=o, w=w: e.activation(o.t[:, 0:w], pf.t[:, 0:w], AF.Sigmoid), reads=[pf], writes=[o])
                            K.dma('sp', sgT.t.ap()[idx, :, tok], o.t[:, 0:w], reads=[o], writes=[sgT])
                    if kind != 'conv':
                        continue
                    ch = idx
                    isqk = ch < 8
                    scl = 128.0 if ch < 4 else 1.0

                    def finish_tile(src, w, tok, ch=ch, isqk=isqk, scl=scl):
                        if not isqk:
                            K.dma('sp', cT.t.ap()[ch, :, tok], src.t[:, 0:w], reads=[src], writes=[cT])
                            return
                        K.op('dve', lambda e: e.tensor_tensor(sq2.t[:, 0:w], src.t[:, 0:w], src.t[:, 0:w], ALU.mult), reads=[src], writes=[sq2])
                        K.op('pe', lambda e: e.matmul(pN.t[:, 0:w], C('ones'), sq2.t[:, 0:w], start=True, stop=True), reads=[cst, sq2], writes=[pN])
                        K.op('act', lambda e: e.activation(rr.t[:, 0:w], pN.t[:, 0:w], AF.Sqrt, bias=EPS * scl, scale=scl), reads=[pN], writes=[rr])
                        K.op('dve', lambda e: e.reciprocal(rr.t[:, 0:w], rr.t[:, 0:w]), reads=[rr], writes=[rr])
                        o = co[finish_tile.n % 2]
                        finish_tile.n += 1
                        K.op('dve', lambda e: e.tensor_tensor(o.t[:, 0:w], src.t[:, 0:w], rr.t[:, 0:w], ALU.mult), reads=[src, rr], writes=[o])
                        K.dma('sp', cT.t.ap()[ch, :, tok], o.t[:, 0:w], reads=[o], writes=[cT])
                    finish_tile.n = 0
                    for t in range(8):
                        pc = pC[t % 2]
                        for j in range(4):
                            K.op('pe', lambda e, pc=pc, j=j, t=t, ch=ch: e.matmul(
                                pc.t[:], dg.t[:, ch, j, :], uT.t[:, t * 512 + j:t * 512 + j + 512], start=(j == 0), stop=(j == 3)),
                                reads=[dg, uT], writes=[pc])
                        c_ = cv[t % 2]
                        K.op('act', lambda e, pc=pc, c_=c_: e.activation(c_.t[:], pc.t[:], AF.Silu), reads=[pc], writes=[c_])
                        finish_tile(c_, 512, slice(t * 512, (t + 1) * 512))
                    K.op('dve', lambda e, ch=ch: e.tensor_scalar(acc4.t[:], stT.t[:, ch, 0, :], cw.t[:, ch, 0:1], None, op0=ALU.mult),
                         reads=[stT, cw], writes=[acc4])
                    for j in (1, 2):
                        K.op('dve', lambda e, ch=ch, j=j: e.scalar_tensor_tensor(acc4.t[:], stT.t[:, ch, j, :], cw.t[:, ch, j:j + 1], acc4.t[:],
                                                                               op0=ALU.mult, op1=ALU.add), reads=[stT, cw, acc4], writes=[acc4])
                    K.op('dve', lambda e, ch=ch: e.scalar_tensor_tensor(acc4.t[:], us.t[:], cw.t[:, ch, 3:4], acc4.t[:], op0=ALU.mult, op1=ALU.add),
                         reads=[us, cw, acc4], writes=[acc4])
                    c_ = cv[0]
                    K.op('act', lambda e, c_=c_: e.activation(c_.t[:, 0:4], acc4.t[:], AF.Silu), reads=[acc4], writes=[c_])
                    finish_tile(c_, 4, slice(T, T + 4))
                    K.op('dve', lambda e, ch=ch: e.tensor_copy(cso.t[:, ch, :, 0:2], stT.t[:, ch, 1:3, :].rearrange("p j s -> p s j")),
                         reads=[stT], writes=[cso])
                    K.op('dve', lambda e, ch=ch: e.tensor_copy(cso.t[:, ch, :, 2], us.t[:]), reads=[us], writes=[cso])
                for r in range(3):
                    K.dma('sp', convp_out[r].rearrange("(c p) -> p c", p=128), cpre.t[:, :, r], reads=[cpre], allow_slow_non_contiguous=True)
                for s in range(4):
                    for r in range(3):
                        K.dma('sp', convs_out[s, r].rearrange("(c p) -> p c", p=128), cso.t[:, :, s, r], reads=[cso], allow_slow_non_contiguous=True)

        if not P1_ENABLE:
            ini = K.sb("ini", [128, 12, 128], F32)
            K.dma('sp', ini.t[:, 0:8, :].rearrange("p c t -> p (c t)"), x_in[0:128, :], writes=[ini])
            K.dma('sp', ini.t[:, 8:12, :].rearrange("p c t -> p (c t)"), x_in[256:384, 0:512], writes=[ini])
            gi = K.sb("gi", [128, NB, 4], F32)
            K.dma('sp', gi.t[:].rearrange("p b h -> p (b h)"), x_in[128:256, 0:NB * 4], writes=[gi])
            K.op('dve', lambda e: e.tensor_tensor(gtm.t[:], gi.t[:], gi.t[:], ALU.mult), reads=[gi], writes=[gtm])
            K.op('dve', lambda e: e.tensor_scalar(gtm.t[:], gtm.t[:], -1.0, None, op0=ALU.mult), reads=[gtm], writes=[gtm])
            K.op('act', lambda e: e.activation(btm.t[:], gi.t[:], AF.Sigmoid), reads=[gi], writes=[btm])
            for b in range(NB):
                K.dma('sp', cT.t.ap()[:, :, b * 128:(b + 1) * 128].rearrange("c p t -> p c t"), ini.t[:], reads=[ini], writes=[cT])
                K.dma('sp', zT.t.ap()[:, :, b * 128:(b + 1) * 128].rearrange("c p t -> p c t"), ini.t[:, 0:4, :], reads=[ini], writes=[zT])
            K.op('pool', lambda e: e.memset(kbT2.t[:], 0.01), writes=[kbT2])
            K.op('pool', lambda e: e.memset(kiT2.t[:], 0.01), writes=[kiT2])
            K.op('pool', lambda e: e.memset(wab.t[:], 0.3), writes=[wab])
            K.op('pool', lambda e: e.memset(wsg.t[:], 1.0), writes=[wsg])
            qini = K.sb("qini", [128, 8, 128], BF16)
            K.op('pool', lambda e: e.memset(qini.t[:], 0.02), writes=[qini])
            for b in range(NB):
                K.dma('sp', qsT.t.ap()[b], qini.t[:], reads=[qini], writes=[qsT])
            K.barrier()
        oaT = K.dram("oaT_s", [4, 128, TT], BF16, kind=("ExternalOutput" if DEBUG_OUT else "Internal"))
        with K.scope():
            bk = [K.ps(f"bk{i}", [128, 4, 128], F32) for i in range(8)]
            identf = C('ident')
            onesf = C('ones')
            gcol = K.sb("gcol", [128, 1], F32)
            K.dma('sp', gcol.t[:], gdn_g.rearrange("(p o) -> p o", o=1), writes=[gcol], allow_slow_non_contiguous=True)
            qkv = [K.sb(f"qkv{i}", [128, 12, 128], F32) for i in range(2)]
            zt = [K.sb(f"zt{i}", [128, 4, 128], F32) for i in range(2)]
            sc = K.sb("sc", [128, 32], F32)
            gU = K.sb("gU", [128, 4, 128], F32)
            bI = K.sb("bI", [128, 4, 128], F32)
            Erow = K.sb("Erow", [128, 4, 128], F32)
            decT = K.sb("decT", [128, 4, 128], F32)
            dec = K.sb("dec", [128, 4, 128], F32)
            WT = K.sb("WT", [128, 4, 128], F32)
            QW = K.sb("QW", [128, 4, 128], F32)
            QKm = K.sb("QKm", [128, 4, 128], F32)
            Pm = [K.sb(f"Pm{i}", [128, 4, 128], F32) for i in range(2)]
            PT = [K.sb(f"PT{i}", [128, 4, 128], F32) for i in range(2)]
            XT = [K.sb(f"XT{i}", [128, 4, 128], F32) for i in range(2)]
            kbg = K.sb("kbg", [128, 4, 128], F32)
            kd0 = K.sb("kd0", [128, 4, 128], F32)
            kd1 = K.sb("kd1", [128, 4, 128], F32)
            vb = K.sb("vb", [128, 4, 128], F32)
            u_sb = K.sb("u_sb", [128, 4, 128], F32)
            wT_sb = K.sb("wT_sb", [128, 4, 128], F32)
            qd0 = K.sb("qd0", [128, 4, 128], F32)
            qd1 = K.sb("qd1", [128, 4, 128], F32)
            vn = K.sb("vn", [128, 4, 128], F32)
            Sx = [K.sb(f"Sx{i}", [128, 4, 128], F32) for i in range(3)]
            sqo = K.sb("sqo", [128, 4, 128], F32)
            rs = K.sb("rs", [128, 4, 128], F32)
            oa = [K.sb(f"oa{i}", [128, 4, 128], BF16) for i in range(2)]
            K.op('pool', lambda e: e.memset(vn.t[:], 0.0), writes=[vn])
            K.op('pool', lambda e: e.memset(Sx[0].t[:], 0.0), writes=[Sx[0]])

            def bc_h(ap2):
                return ap2.unsqueeze(1).to_broadcast([128, 4, 128])

            def bc_s(ap4):
                return ap4.unsqueeze(2).to_broadcast([128, 4, 128])

            def flat(bufAP):
                return bufAP.rearrange("p h j -> p (h j)")

            def gated_norm(pO_ap, zt_ap, out_ap, w, pObuf, ztbuf, outbuf, pSSbuf):
                K.op('act', lambda e: e.activation(flat(sqo.t[:])[:, 0:w], pO_ap, AF.Square), reads=[pObuf], writes=[sqo])
                K.op('pe', lambda e: e.matmul(flat(pSSbuf.t[:])[:, 0:w], onesf, flat(sqo.t[:])[:, 0:w], start=True, stop=True),
                     reads=[cst, sqo], writes=[pSSbuf])
                K.op('act', lambda e: e.activation(flat(rs.t[:])[:, 0:w], flat(pSSbuf.t[:])[:, 0:w], AF.Sqrt, bias=EPS, scale=1.0 / 128),
                     reads=[pSSbuf], writes=[rs])
                K.op('dve', lambda e: e.reciprocal(flat(rs.t[:])[:, 0:w], flat(rs.t[:])[:, 0:w]), reads=[rs], writes=[rs])
                K.op('dve', lambda e: e.scalar_tensor_tensor(flat(rs.t[:])[:, 0:w], pO_ap, gcol.t[:, 0:1], flat(rs.t[:])[:, 0:w], op0=ALU.mult, op1=ALU.mult),
                     reads=[pObuf, gcol, rs], writes=[rs])
                K.op('dve', lambda e: e.tensor_tensor(out_ap, flat(rs.t[:])[:, 0:w], zt_ap, ALU.mult), reads=[rs, ztbuf], writes=[outbuf])

            def gdn_block(b):
                i = b % 2
                blk = slice(b * 128, (b + 1) * 128)
                K.dma('sp', qkv[i].t[:], cT.t.ap()[:, :, blk].rearrange("c p t -> p c t"), reads=[cT], writes=[qkv[i]])
                K.dma('sp', zt[i].t[:], zT.t.ap()[:, :, blk].rearrange("c p t -> p c t"), reads=[zT], writes=[zt[i]])
                Q, Kk, V = qkv[i].t[:, 0:4, :], qkv[i].t[:, 4:8, :], qkv[i].t[:, 8:12, :]
                g_ap = gtm.t[:, b, :]
                be_ap = btm.t[:, b, :]
                S0, S1, S2 = Sx[(2 * b) % 3], Sx[(2 * b + 1) % 3], Sx[(2 * b + 2) % 3]
                if P2_STAGE < -3:
                    return
                for n_, nm in enumerate(('U', 'BD', 'CM0', 'CM1')):
                    K.op('pe', lambda e, n_=n_, nm=nm: e.matmul(bk[0].t[:, 0, n_ * 4:(n_ + 1) * 4], C(nm), g_ap, start=True, stop=True),
                         reads=[cst, gtm], writes=[bk[0]])
                p1 = bk[0].t[:, 0, :]
                K.op('dve', lambda e: e.tensor_copy(sc.t[:, 0:4], p1[:, 0:4]), reads=[bk[0]], writes=[sc])
                K.op('act', lambda e: e.activation(sc.t[:, 4:8], p1[:, 0:4], AF.Exp), reads=[bk[0]], writes=[sc])
                K.op('dve', lambda e: e.tensor_tensor(sc.t[:, 8:12], p1[:, 4:8], sc.t[:, 0:4], ALU.subtract), reads=[bk[0], sc], writes=[sc])
                K.op('act', lambda e: e.activation(sc.t[:, 8:12], sc.t[:, 8:12], AF.Exp), reads=[sc], writes=[sc])
                K.op('act', lambda e: e.activation(sc.t[:, 12:20], p1[:, 8:16], AF.Exp), reads=[bk[0]], writes=[sc])
                K.op('dve', lambda e: e.tensor_scalar(sc.t[:, 20:24], be_ap, -1.0, None, op0=ALU.mult), reads=[btm], writes=[sc])
                K.op('dve', lambda e: e.tensor_tensor(sc.t[:, 24:28], be_ap, sc.t[:, 4:8], ALU.mult), reads=[btm, sc], writes=[sc])
                if P2_STAGE < -2:
                    return
                K.op('dve', lambda e: e.tensor_tensor(gU.t[:], bc_h(C('U')), bc_s(g_ap), ALU.mult), reads=[cst, gtm], writes=[gU])
                K.op('dve', lambda e: e.tensor_tensor(bI.t[:], bc_h(identf), bc_s(be_ap), ALU.mult), reads=[cst, btm], writes=[bI])
                if P2_STAGE < -1.5:
                    return
                K.op('pe', lambda e: e.matmul(flat(bk[1].t[:]), onesf, flat(gU.t[:]), start=True, stop=True), reads=[cst, gU], writes=[bk[1]])
                if P2_STAGE < -1.25:
                    return
                if DBG == 5:
                    K.op('pe', lambda e: e.matmul(flat(bk[1].t[:]), onesf, flat(gU.t[:]), start=True, stop=True), reads=[cst, gU], writes=[bk[1]])
                elif DBG == 6:
                    K.op('pe', lambda e: e.matmul(bk[0].t[:, 1, 0:4], C('U'), g_ap, start=True, stop=True), reads=[cst, gtm], writes=[bk[0]])
                elif DBG == 2:
                    K.op('pe', lambda e: e.matmul(flat(bk[5].t[:]), onesf, flat(bI.t[:]), start=True, stop=True), reads=[cst, bI, gU], writes=[bk[5]])
                elif DBG == 3:
                    K.op('pe', lambda e: e.matmul(bk[2].t[:, 0, :], onesf, bI.t[:, 0, :], start=True, stop=True), reads=[cst, bI, gU], writes=[bk[2]])
                elif DBG == 4:
                    K.op('pe', lambda e: e.matmul(flat(bk[2].t[:]), C('U'), flat(bI.t[:]), start=True, stop=True), reads=[cst, bI, gU], writes=[bk[2]])
                else:
                    K.op('pe', lambda e: e.matmul(flat(bk[2].t[:]), onesf, flat(bI.t[:]), start=True, stop=True), reads=[cst, bI, gU], writes=[bk[2]])
                if P2_STAGE < -1:
                    return
                if P2_STAGE < -0.9:
                    return
                K.op('act', lambda e: e.activation(Erow.t[:], bk[1].t[:], AF.Exp), reads=[bk[1]], writes=[Erow])
                if P2_STAGE < -0.8:
                    return
                for h in range(4):
                    K.op('dve', lambda e, h=h: e.tensor_scalar(decT.t[:, h, :], bk[1].t[:, h, :], sc.t[:, h:h + 1], 0.0, op0=ALU.subtract, op1=ALU.min),
                         reads=[bk[1], sc], writes=[decT])
                    K.op('dve', lambda e, h=h: e.tensor_scalar(dec.t[:, h, :], bk[1].t[:, h, :], sc.t[:, h:h + 1], 0.0, op0=ALU.subtract, op1=ALU.max),
                         reads=[bk[1], sc], writes=[dec])
                if P2_STAGE < -0.7:
                    return
                K.op('act', lambda e: e.activation(decT.t[:], decT.t[:], AF.Exp), reads=[decT], writes=[decT])
                K.op('act', lambda e: e.activation(dec.t[:], dec.t[:], AF.Exp, scale=-1.0), reads=[dec], writes=[dec])
                if P2_STAGE < -0.6:
                    return
                K.op('dve', lambda e: e.tensor_tensor(WT.t[:], decT.t[:], bc_h(C('MsT')), ALU.mult), reads=[decT, cst], writes=[WT])
                K.op('dve', lambda e: e.tensor_tensor(WT.t[:], WT.t[:], bk[2].t[:], ALU.mult), reads=[WT, bk[2]], writes=[WT])
                if P2_STAGE < -0.5:
                    return
                K.op('dve', lambda e: e.tensor_tensor(QW.t[:], decT.t[:], bc_h(C('MiT')), ALU.mult), reads=[decT, cst], writes=[QW])
                K.op('dve', lambda e: e.tensor_tensor(dec.t[:], dec.t[:], bc_h(C('Ms')), ALU.mult), reads=[dec, cst], writes=[dec])
                if P2_STAGE < 1:
                    return
                for h in range(4):
                    K.op('pe', lambda e, h=h: e.matmul(bk[3].t[:, h, :], Kk[:, h, :], Kk[:, h, :], start=True, stop=True), reads=[qkv[i]], writes=[bk[3]])
                    K.op('pe', lambda e, h=h: e.matmul(bk[4].t[:, h, :], Kk[:, h, :], Q[:, h, :], start=True, stop=True), reads=[qkv[i]], writes=[bk[4]])
                P0, PT0, X0 = Pm[0], PT[0], XT[0]
                for h in range(4):
                    K.op('dve', lambda e, h=h: e.scalar_tensor_tensor(P0.t[:, h, :], bk[3].t[:, h, :], sc.t[:, 20 + h:21 + h], dec.t[:, h, :], op0=ALU.mult, op1=ALU.mult),
                         reads=[bk[3], sc, dec], writes=[P0])
                K.op('dve', lambda e: e.scalar_tensor_tensor(PT0.t[:], bk[3].t[:], -1.0, WT.t[:], op0=ALU.mult, op1=ALU.mult), reads=[bk[3], WT], writes=[PT0])
                K.op('dve', lambda e: e.tensor_tensor(QKm.t[:], bk[4].t[:], QW.t[:], ALU.mult), reads=[bk[4], QW], writes=[QKm])
                K.op('dve', lambda e: e.tensor_tensor(X0.t[:], PT0.t[:], bc_h(identf), ALU.add), reads=[PT0, cst], writes=[X0])
                if P2_STAGE < 2:
                    return
                cur = 0
                for lvl in range(6):
                    Pc, PTc, Xc = Pm[cur], PT[cur], XT[cur]
                    Pn, PTn, Xn = Pm[1 - cur], PT[1 - cur], XT[1 - cur]
                    last = (lvl == 5)
                    for h in range(4):
                        if not last:
                            K.op('pe', lambda e, h=h, Pc=Pc, PTc=PTc: e.matmul(bk[0].t[:, h, :], PTc.t[:, h, :], Pc.t[:, h, :], start=True, stop=True),
                                 reads=[Pc, PTc], writes=[bk[0]])
                            K.op('pe', lambda e, h=h, Pc=Pc, PTc=PTc: e.matmul(bk[1].t[:, h, :], Pc.t[:, h, :], PTc.t[:, h, :], start=True, stop=True),
                                 reads=[Pc, PTc], writes=[bk[1]])
                        if lvl > 0:
                            K.op('pe', lambda e, h=h, Pc=Pc, Xc=Xc: e.matmul(bk[2].t[:, h, :], Pc.t[:, h, :], Xc.t[:, h, :], start=True, stop=True),
                                 reads=[Pc, Xc], writes=[bk[2]])
                    if not last:
                        K.op('act', lambda e, Pn=Pn: e.copy(Pn.t[:], bk[0].t[:]), reads=[bk[0]], writes=[Pn])
                        K.op('dve', lambda e, PTn=PTn: e.tensor_copy(PTn.t[:], bk[1].t[:]), reads=[bk[1]], writes=[PTn])
                    if lvl > 0:
                        K.op('dve', lambda e, Xc=Xc, Xn=Xn: e.tensor_tensor(Xn.t[:], Xc.t[:], bk[2].t[:], ALU.add), reads=[Xc, bk[2]], writes=[Xn])
                    else:
                        K.op('pool', lambda e, Xc=Xc, Xn=Xn: e.tensor_copy(Xn.t[:], Xc.t[:]), reads=[Xc], writes=[Xn])
                    cur = 1 - cur
                Xf = XT[cur]
                if P2_STAGE < 3:
                    return
                for h in range(4):
                    K.op('pe', lambda e, h=h: e.transpose(bk[3].t[:, h, :], Kk[:, h, :], identf), reads=[qkv[i], cst], writes=[bk[3]])
                    K.op('pe', lambda e, h=h: e.transpose(bk[4].t[:, h, :], V[:, h, :], identf), reads=[qkv[i], cst], writes=[bk[4]])
                K.op('dve', lambda e: e.tensor_tensor(kbg.t[:], bk[3].t[:], bc_s(sc.t[:, 24:28]), ALU.mult), reads=[bk[3], sc], writes=[kbg])
                K.op('dve', lambda e: e.tensor_tensor(kd1.t[:], bk[3].t[:], bc_s(sc.t[:, 8:12]), ALU.mult), reads=[bk[3], sc], writes=[kd1])
                K.op('dve', lambda e: e.tensor_scalar(flat(kd0.t[:]), flat(kd1.t[:]), C('CM0')[:, 0:1], None, op0=ALU.mult), reads=[kd1, cst], writes=[kd0])
                K.op('dve', lambda e: e.tensor_scalar(flat(kd1.t[:]), flat(kd1.t[:]), C('CM1')[:, 0:1], None, op0=ALU.mult), reads=[kd1, cst], writes=[kd1])
                K.op('dve', lambda e: e.tensor_tensor(vb.t[:], bk[4].t[:], bc_s(be_ap), ALU.mult), reads=[bk[4], btm], writes=[vb])
                if P2_STAGE < 4:
                    return
                for h in range(4):
                    K.op('pe', lambda e, h=h: e.matmul(bk[5].t[:, h, :], Xf.t[:, h, :], vb.t[:, h, :], start=True, stop=True), reads=[Xf, vb], writes=[bk[5]])
                    K.op('pe', lambda e, h=h: e.matmul(bk[6].t[:, h, :], kbg.t[:, h, :], Xf.t[:, h, :], start=True, stop=True), reads=[Xf, kbg], writes=[bk[6]])
                K.op('act', lambda e: e.copy(u_sb.t[:], bk[5].t[:]), reads=[bk[5]], writes=[u_sb])
                K.op('act', lambda e: e.copy(wT_sb.t[:], bk[6].t[:]), reads=[bk[6]], writes=[wT_sb])
                K.op('dve', lambda e: e.tensor_tensor(qd1.t[:], Q, Erow.t[:], ALU.mult), reads=[qkv[i], Erow], writes=[qd1])
                K.op('dve', lambda e: e.tensor_tensor(qd0.t[:], qd1.t[:], bc_h(C('C0')), ALU.mult), reads=[qd1, cst], writes=[qd0])
                K.op('dve', lambda e: e.tensor_tensor(qd1.t[:], qd1.t[:], bc_h(C('C1')), ALU.mult), reads=[qd1, cst], writes=[qd1])
                if P2_STAGE < 5:
                    return
                for h in range(4):
                    K.op('pe', lambda e, h=h: e.matmul(bk[3].t[:, h, :], wT_sb.t[:, h, :], S0.t[:, h, :], start=True, stop=True), reads=[wT_sb, S0], writes=[bk[3]])
                K.op('dve', lambda e: e.tensor_tensor(vn.t[0:64], u_sb.t[0:64], bk[3].t[0:64], ALU.subtract), reads=[u_sb, bk[3]], writes=[vn])
                if DBG == 9 and b == P2_BLOCKS - 1:
                    K.op('act', lambda e: e.copy(sqo.t[:], bk[3].t[:]), reads=[bk[3]], writes=[sqo])
                    K.dma('sp', ssmp_out.rearrange("h d e -> d h e"), sqo.t[:], reads=[sqo])
                if DBG == 10 and b == P2_BLOCKS - 1:
                    K.dma('sp', ssmp_out.rearrange("h d e -> d h e"), wT_sb.t[:], reads=[wT_sb])
                if DBG == 11 and b == P2_BLOCKS - 1:
                    K.dma('sp', ssmp_out.rearrange("h d e -> d h e"), S0.t[:], reads=[S0])
                for h in range(4):
                    K.op('pe', lambda e, h=h: e.matmul(bk[4].t[:, h, :], kd0.t[:, h, :], vn.t[:, h, :], start=True, stop=True), reads=[kd0, vn], writes=[bk[4]])
                K.op('dve', lambda e: e.tensor_tensor(S1.t[:], S0.t[:], bc_s(sc.t[:, 12:16]), ALU.mult), reads=[S0, sc], writes=[S1])
                K.op('dve', lambda e: e.tensor_tensor(S1.t[:], S1.t[:], bk[4].t[:], ALU.add), reads=[S1, bk[4]], writes=[S1])
                for h in range(4):
                    K.op('pe', lambda e, h=h: e.matmul(bk[3].t[:, h, :], wT_sb.t[:, h, :], S1.t[:, h, :], start=True, stop=True), reads=[wT_sb, S1], writes=[bk[3]])
                K.op('dve', lambda e: e.tensor_tensor(vn.t[64:128], u_sb.t[64:128], bk[3].t[64:128], ALU.subtract), reads=[u_sb, bk[3]], writes=[vn])
                for h in range(4):
                    K.op('pe', lambda e, h=h: e.matmul(bk[4].t[:, h, :], kd1.t[:, h, :], vn.t[:, h, :], start=True, stop=True), reads=[kd1, vn], writes=[bk[4]])
                K.op('dve', lambda e: e.tensor_tensor(S2.t[:], S1.t[:], bc_s(sc.t[:, 16:20]), ALU.mult), reads=[S1, sc], writes=[S2])
                K.op('dve', lambda e: e.tensor_tensor(S2.t[:], S2.t[:], bk[4].t[:], ALU.add), reads=[S2, bk[4]], writes=[S2])
                for h in range(4):
                    K.op('pe', lambda e, h=h: e.matmul(bk[5].t[:, h, :], S0.t[:, h, :], qd0.t[:, h, :], start=True, stop=False), reads=[S0, qd0], writes=[bk[5]])
                    K.op('pe', lambda e, h=h: e.matmul(bk[5].t[:, h, :], S1.t[:, h, :], qd1.t[:, h, :], start=False, stop=False), reads=[S1, qd1], writes=[bk[5]])
                    K.op('pe', lambda e, h=h: e.matmul(bk[5].t[:, h, :], vn.t[:, h, :], QKm.t[:, h, :], start=False, stop=True), reads=[vn, QKm], writes=[bk[5]])
                if P2_STAGE < 6:
                    return
                gated_norm(flat(bk[5].t[:]), flat(zt[i].t[:]), flat(oa[i].t[:]), 512, bk[5], zt[i], oa[i], bk[6])
                K.dma('sp', oaT.t.ap()[:, :, blk].rearrange("c p t -> p c t"), oa[i].t[:], reads=[oa[i]], writes=[oaT])
            for b_ in range(P2_BLOCKS):
                gdn_block(b_)
            Sfin = Sx[(2 * (P2_BLOCKS - 1) + (1 if DBG == 8 else 2)) % 3]
            if DBG not in (9, 10, 11):
                K.dma('sp', ssmp_out.rearrange("h d e -> d h e"), Sfin.t[:], reads=[Sfin])

            if P2_SAMPLE:
                qs = K.sb("qs", [128, 12, 4], F32)
                zs = K.sb("zs", [128, 4, 4], F32)
                K.dma('sp', qs.t[:], cT.t.ap()[:, :, T:T + 4].rearrange("c p t -> p c t"), reads=[cT], writes=[qs], allow_slow_non_contiguous=True)
                K.dma('sp', zs.t[:], zT.t.ap()[:, :, T:T + 4].rearrange("c p t -> p c t"), reads=[zT], writes=[zs], allow_slow_non_contiguous=True)
                Sin = K.sb("Sin", [128, 16, 128], F32)
                Sn = K.sb("Sn", [128, 16, 128], F32)
                K.dma('sp', Sin.t[:], st_ssm.rearrange("s h d e -> d (s h) e"), writes=[Sin])
                Dg = K.sb("Dg", [128, 4, 8], F32)
                i4 = identf[:, 0:4].unsqueeze(2).to_broadcast([128, 4, 4])
                K.op('dve', lambda e: e.tensor_tensor(Dg.t[:, :, 0:4], gtm.t[:, 32, :].unsqueeze(1).to_broadcast([128, 4, 4]), i4, ALU.mult), reads=[gtm, cst], writes=[Dg])
                K.op('dve', lambda e: e.tensor_tensor(Dg.t[:, :, 4:8], btm.t[:, 32, :].unsqueeze(1).to_broadcast([128, 4, 4]), i4, ALU.mult), reads=[btm, cst], writes=[Dg])
                K.op('pe', lambda e: e.matmul(bk[0].t[:, 0, 0:32], onesf, Dg.t[:].rearrange("p s c -> p (s c)"), start=True, stop=True), reads=[cst, Dg], writes=[bk[0]])
                bcs = K.sb("bcs", [128, 4, 8], F32)
                K.op('dve', lambda e: e.tensor_copy(bcs.t[:].rearrange("p s c -> p (s c)"), bk[0].t[:, 0, 0:32]), reads=[bk[0]], writes=[bcs])
                K.op('act', lambda e: e.activation(bcs.t[:, :, 0:4], bcs.t[:, :, 0:4], AF.Exp), reads=[bcs], writes=[bcs])
                Sp = K.sb("Sp", [128, 128], F32)
                dgm = K.sb("dgm", [128, 128], F32)
                dgk = K.sb("dgk", [128, 128], F32)
                krb = K.sb("krb", [128, 128], F32)
                vcol = K.sb("vcol", [128, 2], F32)
                for s in range(4):
                    for h in range(4):
                        sh = s * 4 + h
                        qc, kc, vc = qs.t[:, h, s:s + 1], qs.t[:, 4 + h, s:s + 1], qs.t[:, 8 + h, s:s + 1]
                        K.op('dve', lambda e, sh=sh, s=s, h=h: e.tensor_scalar(Sp.t[:], Sin.t[:, sh, :], bcs.t[:, s, h:h + 1], None, op0=ALU.mult),
                             reads=[Sin, bcs], writes=[Sp])
                        K.op('pe', lambda e, kc=kc: e.matmul(bk[1].t[:, 0, 0:1], Sp.t[:], kc, start=True, stop=True), reads=[Sp, qs], writes=[bk[1]])
                        K.op('dve', lambda e, vc=vc: e.tensor_tensor(vcol.t[:, 0:1], vc, bk[1].t[:, 0, 0:1], ALU.subtract), reads=[qs, bk[1]], writes=[vcol])
                        K.op('dve', lambda e, s=s, h=h: e.tensor_tensor(vcol.t[:, 1:2], vcol.t[:, 0:1], bcs.t[:, s, 4 + h:5 + h], ALU.mult), reads=[vcol, bcs], writes=[vcol])
                        K.op('dve', lambda e: e.tensor_scalar(dgm.t[:], identf, vcol.t[:, 1:2], None, op0=ALU.mult), reads=[cst, vcol], writes=[dgm])
                        K.op('dve', lambda e, kc=kc: e.tensor_scalar(dgk.t[:], identf, kc, None, op0=ALU.mult), reads=[cst, qs], writes=[dgk])
                        K.op('pe', lambda e: e.matmul(bk[2].t[:, 0, :], onesf, dgk.t[:], start=True, stop=True), reads=[cst, dgk], writes=[bk[2]])
                        K.op('act', lambda e: e.copy(krb.t[:], bk[2].t[:, 0, :]), reads=[bk[2]], writes=[krb])
                        K.op('pe', lambda e: e.matmul(bk[3].t[:, 0, :], krb.t[:], dgm.t[:], start=True, stop=True), reads=[krb, dgm], writes=[bk[3]])
                        K.op('dve', lambda e, sh=sh: e.tensor_tensor(Sn.t[:, sh, :], Sp.t[:], bk[3].t[:, 0, :], ALU.add), reads=[Sp, bk[3]], writes=[Sn])
                        K.op('pe', lambda e, sh=sh, s=s, h=h, qc=qc: e.matmul(bk[5].t[:, h, s:s + 1], Sn.t[:, sh, :], qc, start=True, stop=True),
                             reads=[Sn, qs], writes=[bk[5]])
                K.dma('sp', ssms_out.rearrange("s h d e -> d (s h) e"), Sn.t[:], reads=[Sn])
                oas = K.sb("oas", [128, 4, 4], BF16)
                pO4 = K.sb("pO4", [128, 4, 4], F32)
                K.op('dve', lambda e: e.tensor_copy(pO4.t[:], bk[5].t[:, :, 0:4]), reads=[bk[5]], writes=[pO4])
                p16 = pO4.t[:].rearrange("p h s -> p (h s)")
                K.op('act', lambda e: e.activation(flat(sqo.t[:])[:, 0:16], p16, AF.Square), reads=[pO4], writes=[sqo])
                K.op('pe', lambda e: e.matmul(flat(bk[6].t[:])[:, 0:16], onesf, flat(sqo.t[:])[:, 0:16], start=True, stop=True), reads=[cst, sqo], writes=[bk[6]])
                K.op('act', lambda e: e.activation(flat(rs.t[:])[:, 0:16], flat(bk[6].t[:])[:, 0:16], AF.Sqrt, bias=EPS, scale=1.0 / 128), reads=[bk[6]], writes=[rs])
                K.op('dve', lambda e: e.reciprocal(flat(rs.t[:])[:, 0:16], flat(rs.t[:])[:, 0:16]), reads=[rs], writes=[rs])
                K.op('dve', lambda e: e.scalar_tensor_tensor(flat(rs.t[:])[:, 0:16], p16, gcol.t[:, 0:1], flat(rs.t[:])[:, 0:16], op0=ALU.mult, op1=ALU.mult),
                     reads=[pO4, gcol, rs], writes=[rs])
                K.op('dve', lambda e: e.tensor_tensor(oas.t[:].rearrange("p h s -> p (h s)"), flat(rs.t[:])[:, 0:16], zs.t[:].rearrange("p h s -> p (h s)"), ALU.mult),
                     reads=[rs, zs], writes=[oas])
                K.dma('sp', oaT.t.ap()[:, :, T:T + 4].rearrange("c p t -> p c t"), oas.t[:], reads=[oas], writes=[oaT], allow_slow_non_contiguous=True)

        obT = K.dram("obT_s", [4, 128, TT], BF16, kind=("ExternalOutput" if DEBUG_OUT else "Internal"))
        with K.scope():
            NIT = 24
            qT = [K.sb(f"qT{i}", [128, 8, 128], BF16) for i in range(3)]
            score = [K.sb(f"score{i}", [128, T], F32) for i in range(2)]
            cjunk = K.sb("cjunk", [128, T], F32)
            maskq = [K.sb(f"maskq{i}", [128, T], BF16) for i in range(2)]
            maskT = [K.sb(f"maskT{i}", [128, 32, 128], BF16) for i in range(2)]
            rbuf = [K.sb(f"rbuf{i}", [128, 512], BF16) for i in range(4)]
            dsg = K.sb("dsg", [128, 8, 128], BF16)
            bs = K.sb("bs", [128, 16], F32)
            fill = K.sb("fill", [128, 2], F32)
            K.op('pool', lambda e: e.memset(fill.t[:], 0.0), writes=[fill])
            deltas = K.sb("deltas", [128, NIT + 1], F32)
            cvec = K.sb("cvec", [128, NIT + 1], F32)
            for k in range(NIT + 1):
                K.op('pool', lambda e, k=k: e.memset(cvec.t[:, k:k + 1], 2.0 ** -(k + 1)), writes=[cvec])
            E_ = [K.sb(f"E_{i}", [128, 4, 128], BF16) for i in range(2)]
            Pm_ = [K.sb(f"Pm_{i}", [128, 4, 128], BF16) for i in range(2)]
            obn = K.sb("obn", [128, 8, 64], BF16)
            rden = K.sb("rden", [128, 8], F32)
            obst = [K.sb(f"obst{i}", [128, 4, 128], BF16) for i in range(2)]
            pI_ = [K.ps(f"pI_{i}", [128, 512], F32) for i in range(2)]
            pMT = K.ps("pMT", [128, 8, 128], BF16)
            pL = [K.ps(f"pL{i}", [128, 4, 128], F32) for i in range(2)]
            pO = [K.ps(f"pO{i}", [128, 4, 65], F32) for i in range(2)]
            pS = K.ps("pS", [128, 512], F32)
            zb = K.sb("zb", [128, 128], BF16)
            K.op('pool', lambda e: e.memset(zb.t[:], 0.0), writes=[zb])

            def p3_scores(qb):
                i = qb % 2
                i3 = qb % 3
                L = (qb + 1) * 128
                K.dma('sp', qT[i3].t[:], qsT.t.ap()[qb], reads=[qsT], writes=[qT[i3]])
                K.op('dve', lambda e: e.tensor_tensor(dsg.t[:], identb.t[:, :].unsqueeze(1).to_broadcast([128, 8, 128]),
                                                      wsg.t[:, qb, :].unsqueeze(2).to_broadcast([128, 8, 128]), ALU.mult), reads=[identb, wsg], writes=[dsg])
                for kt in range((L + 511) // 512):
                    w = min(512, L - kt * 512)
                    ks = slice(kt * 512, kt * 512 + w)

                    def mmq(h, w=w, ks=ks):
                        pr, hf = h // 2, h % 2
                        rows = slice(hf * 64, (hf + 1) * 64)
                        ps_, rb = pI_[h % 2], rbuf[h % 4]
                        K.op('pe', lambda e: e.matmul(ps_.t[:, 0:w], qT[i3].t[rows, 4 + pr, :], kiT2.t[rows, ks], start=True, stop=True),
                             reads=[qT[i3], kiT2], writes=[ps_])
                        K.op('act', lambda e: e.activation(rb.t[:, 0:w], ps_.t[:, 0:w], AF.Relu, scale=wab.t[:, qb, h:h + 1]),
                             reads=[ps_, wab], writes=[rb])

                    def mmd(h, w=w):
                        rb = rbuf[h % 4]
                        K.op('pe', lambda e: e.matmul(pS.t[:, 0:w], dsg.t[:, h, :], rb.t[:, 0:w], start=(h == 0), stop=(h == 7)),
                             reads=[dsg, rb], writes=[pS])
                    mmq(0)
                    mmq(1)
                    for h in range(8):
                        if h + 2 < 8:
                            mmq(h + 2)
                        mmd(h)
                    K.op('act', lambda e, w=w, ks=ks: e.copy(score[i].t[:, ks], pS.t[:, 0:w]), reads=[pS], writes=[score[i]])

            def p3_mask(qb):
                i = qb % 2
                L = (qb + 1) * 128
                blk = slice(qb * 128, (qb + 1) * 128)
                K.op('dve', lambda e, L=L: e.tensor_reduce(bs.t[:, 0:1], score[i].t[:, 0:L], AX.X, ALU.max), reads=[score[i]], writes=[bs])
                K.op('dve', lambda e, L=L: e.tensor_reduce(bs.t[:, 1:2], score[i].t[:, 0:L], AX.X, ALU.min), reads=[score[i]], writes=[bs])
                K.op('dve', lambda e: e.tensor_scalar(bs.t[:, 1:2], bs.t[:, 1:2], -1.0, None, op0=ALU.add), reads=[bs], writes=[bs])
                K.op('dve', lambda e: e.scalar_tensor_tensor(bs.t[:, 2:3], bs.t[:, 0:1], 1.0, bs.t[:, 1:2], op0=ALU.add, op1=ALU.subtract), reads=[bs], writes=[bs])
                K.op('dve', lambda e: e.tensor_scalar(deltas.t[:], cvec.t[:], bs.t[:, 2:3], None, op0=ALU.mult), reads=[cvec, bs], writes=[deltas])
                K.op('dve', lambda e: e.tensor_tensor(bs.t[:, 3:4], bs.t[:, 1:2], deltas.t[:, 0:1], ALU.add), reads=[bs, deltas], writes=[bs])
                K.op('dve', lambda e, blk=blk: e.tensor_tensor(score[i].t[:, blk], score[i].t[:, blk], C('CA'), ALU.add), reads=[score[i], cst], writes=[score[i]])
                def filler():
                    K.op('dve', lambda e: e.tensor_copy(fill.t[:, 0:1], fill.t[:, 1:2]), self_sync=False)
                for k in range(NIT):
                    mid, midn = bs.t[:, 3 + k % 2:4 + k % 2], bs.t[:, 3 + (k + 1) % 2:4 + (k + 1) % 2]
                    K.op('dve', lambda e, mid=mid, L=L: e.tensor_scalar(cjunk.t[:, 0:L], score[i].t[:, 0:L], mid, 0.0, op0=ALU.is_ge, op1=ALU.add, accum_out=bs.t[:, 5:6]),
                         reads=[score[i], bs], writes=[bs], self_sync=(k == 0))
                    filler()
                    K.op('dve', lambda e: e.tensor_scalar(bs.t[:, 6:7], bs.t[:, 5:6], 255.5, 0.5, op0=ALU.is_ge, op1=ALU.subtract), reads=[bs], writes=[bs], self_sync=False)
                    filler()
                    K.op('dve', lambda e, mid=mid, midn=midn, k=k: e.scalar_tensor_tensor(midn, bs.t[:, 6:7], deltas.t[:, k:k + 1], mid, op0=ALU.mult, op1=ALU.add),
                         reads=[bs, deltas], writes=[bs], self_sync=False)
                    filler()
                midf = bs.t[:, 3 + NIT % 2:4 + NIT % 2]
                K.op('dve', lambda e, midf=midf: e.scalar_tensor_tensor(bs.t[:, 7:8], deltas.t[:, NIT:NIT + 1], -2.0, midf, op0=ALU.mult, op1=ALU.add), reads=[bs, deltas], writes=[bs])
                K.op('dve', lambda e, L=L: e.tensor_scalar(maskq[i].t[:, 0:L], score[i].t[:, 0:L], bs.t[:, 7:8], None, op0=ALU.is_ge), reads=[score[i], bs], writes=[maskq[i]])
                for kb0 in range(0, qb + 1, 8):
                    n = min(8, qb + 1 - kb0)
                    for j in range(n):
                        K.op('pe', lambda e, j=j, kb0=kb0: e.transpose(pMT.t[:, j, :], maskq[i].t[:, (kb0 + j) * 128:(kb0 + j + 1) * 128], identb.t[:]),
                             reads=[maskq[i], identb], writes=[pMT])
                    K.op('act', lambda e, kb0=kb0, n=n: e.copy(maskT[i].t[:, kb0:kb0 + n, :], pMT.t[:, 0:n, :]), reads=[pMT], writes=[maskT[i]])

            def p3_attn(qb):
                i = qb % 2
                i3 = qb % 3
                L = (qb + 1) * 128
                blk = slice(qb * 128, (qb + 1) * 128)
                for g in range(2):
                    K.op('pe', lambda e, g=g: e.matmul(pO[g].t[:].rearrange("p a b -> p (a b)"), zb.t[:],
                                                       v65.t[:, 0:2, :, :].rearrange("p a b c -> p (a b c)"), start=True, stop=False),
                         reads=[zb, v65], writes=[pO[g]])
                steps = [(kb, g) for kb in range(qb + 1) for g in range(2)]

                def front(kb, g):
                    keys = slice(kb * 128, (kb + 1) * 128)
                    rows = slice(g * 64, (g + 1) * 64)
                    for hh in range(4):
                        h = 2 * hh + g
                        K.op('pe', lambda e, g=g, hh=hh, rows=rows, keys=keys, h=h: e.matmul(
                            pL[g].t[:, hh, :], kbT2.t[rows, h // 4, keys], qT[i3].t[rows, h // 2, :], start=True, stop=True), reads=[kbT2, qT[i3]], writes=[pL[g]])
                    K.op('act', lambda e, g=g: e.activation(E_[g].t[:], pL[g].t[:], AF.Exp, scale=0.125), reads=[pL[g]], writes=[E_[g]])
                    K.op('pool', lambda e, g=g, kb=kb: e.tensor_tensor(Pm_[g].t[:], E_[g].t[:], maskT[i].t[:, kb, :].unsqueeze(1).to_broadcast([128, 4, 128]), ALU.mult),
                         reads=[E_[g], maskT[i]], writes=[Pm_[g]])

                def back(kb, g):
                    for hh in range(4):
                        h = 2 * hh + g
                        K.op('pe', lambda e, g=g, hh=hh, kb=kb, h=h: e.matmul(
                            pO[g].t[:, hh, :], Pm_[g].t[:, hh, :], v65.t[:, kb, h // 4, :], start=False, stop=(kb == qb)), reads=[Pm_[g], v65], writes=[pO[g]])
                front(*steps[0])
                for idx, st_ in enumerate(steps):
                    if idx + 1 < len(steps):
                        front(*steps[idx + 1])
                    back(*st_)

            def p3_final(qb):
                i = qb % 2
                L = (qb + 1) * 128
                blk = slice(qb * 128, (qb + 1) * 128)
                obn4 = obn.t[:].rearrange("p (a b) d -> p a b d", b=2)
                for g in range(2):
                    K.op('dve', lambda e, g=g: e.reciprocal(rden.t[:, 4 * g:4 * g + 4], pO[g].t[:, :, 64]), reads=[pO[g]], writes=[rden])
                    K.op('dve', lambda e, g=g: e.tensor_tensor(obn4[:, :, g, :], pO[g].t[:, :, 0:64],
                                                              rden.t[:, 4 * g:4 * g + 4].unsqueeze(2).to_broadcast([128, 4, 64]), ALU.mult),
                         reads=[pO[g], rden], writes=[obn])
                for c in range(4):
                    K.op('pe', lambda e, c=c: e.transpose(pMT.t[:, c, :], obn.t[:, 2 * c:2 * c + 2, :].rearrange("p h d -> p (h d)"), identb.t[:]),
                         reads=[obn, identb], writes=[pMT])
                K.op('act', lambda e, i=i: e.copy(obst[i].t[:], pMT.t[:, 0:4, :]), reads=[pMT], writes=[obst[i]])
                K.dma('sp', obT.t.ap()[:, :, blk].rearrange("c p t -> p c t"), obst[i].t[:], reads=[obst[i]], writes=[obT])

            if P3_BLOCKS > 0:
                p3_scores(0)
                p3_mask(0)
            if P3_BLOCKS > 1:
                p3_scores(1)
            for qb_ in range(P3_BLOCKS):
                if qb_ + 2 < P3_BLOCKS:
                    p3_scores(qb_ + 2)
                p3_attn(qb_)
                if qb_ + 1 < P3_BLOCKS:
                    p3_mask(qb_ + 1)
                p3_final(qb_)

        if P3S_ENABLE:
            with K.scope():
                NITS = 26
                ki_rows = cache_kidx.rearrange("n (c x) -> (n c) x", x=2048)
                kv_rows = cache_kv.rearrange("n (c x) -> (n c) x", x=4096)
                pt_i = K.sb("pt_i", [128, 4], I32)
                K.dma('sp', pt_i.t[:], page_tab.rearrange("s p -> p s"), writes=[pt_i], allow_slow_non_contiguous=True)
                pt_f = K.sb("pt_f", [128, 4], F32)
                K.op('dve', lambda e: e.tensor_copy(pt_f.t[:], pt_i.t[:]), reads=[pt_i], writes=[pt_f])
                cv16 = K.sb("cv16", [128, 8], F32)
                for c in range(8):
                    K.op('pool', lambda e, c=c: e.memset(cv16.t[:, c:c + 1], float(c)), writes=[cv16])
                ixf = K.sb("ixf", [128, 4, 8], F32)
                ixk = K.sb("ixk", [128, 4, 4], I32)
                ixv = K.sb("ixv", [128, 4, 8], I32)
                K.op('dve', lambda e: e.scalar_tensor_tensor(ixf.t[:, :, 0:4], pt_f.t[:, :].unsqueeze(2).to_broadcast([128, 4, 4]), 4.0,
                                                             cv16.t[:, 0:4].unsqueeze(1).to_broadcast([128, 4, 4]), op0=ALU.mult, op1=ALU.add),
                     reads=[pt_f, cv16], writes=[ixf])
                K.op('dve', lambda e: e.tensor_copy(ixk.t[:], ixf.t[:, :, 0:4]), reads=[ixf], writes=[ixk])
                K.op('dve', lambda e: e.scalar_tensor_tensor(ixf.t[:], pt_f.t[:, :].unsqueeze(2).to_broadcast([128, 4, 8]), 8.0,
                                                             cv16.t[:, 0:8].unsqueeze(1).to_broadcast([128, 4, 8]), op0=ALU.mult, op1=ALU.add),
                     reads=[pt_f, cv16], writes=[ixf])
                K.op('dve', lambda e: e.tensor_copy(ixv.t[:], ixf.t[:]), reads=[ixf], writes=[ixv])
                qiS = K.sb("qiS", [128, 8, 4], BF16)
                qbS = K.sb("qbS", [128, 4, 4], BF16)
                q32 = qsT.t.ap()[32]
                for h in range(8):
                    src = q32[(h % 2) * 64:(h % 2 + 1) * 64, 4 + h // 2, 0:4]
                    for hf in range(2):
                        K.dma('sp', qiS.t[hf * 64:(hf + 1) * 64, h, :], src, reads=[qsT], writes=[qiS], allow_slow_non_contiguous=True)
                    n_, i_ = h // 4, h % 4
                    K.dma('sp', qbS.t[n_ * 64:(n_ + 1) * 64, i_, :], q32[(h % 2) * 64:(h % 2 + 1) * 64, h // 2, 0:4], reads=[qsT], writes=[qbS],
                          allow_slow_non_contiguous=True)
                Dw = K.sb("Dw", [128, 4, 8], F32)
                wsig = K.sb("wsig", [128, 4, 8], F32)
                pB = K.ps("pB", [128, 512], F32)
                K.op('dve', lambda e: e.tensor_tensor(Dw.t[:], wab.t[:, 32, :].unsqueeze(1).to_broadcast([128, 4, 8]),
                                                      wsg.t[:, 32, :].unsqueeze(1).to_broadcast([128, 4, 8]), ALU.mult), reads=[wab, wsg], writes=[Dw])
                K.op('dve', lambda e: e.tensor_tensor(Dw.t[:], Dw.t[:], C('ident')[:, 0:4].unsqueeze(2).to_broadcast([128, 4, 8]), ALU.mult), reads=[Dw, cst], writes=[Dw])
                K.op('pe', lambda e: e.matmul(pB.t[:, 0:32], C('ones'), Dw.t[:].rearrange("p s c -> p (s c)"), start=True, stop=True), reads=[cst, Dw], writes=[pB])
                K.op('act', lambda e: e.copy(wsig.t[:].rearrange("p s c -> p (s c)"), pB.t[:, 0:32]), reads=[pB], writes=[wsig])
                vnew = K.sb("vnew", [1, 4, 128], F32)
                K.dma('sp', vnew.t[:], kv_out[T:T + 4, 128:256].rearrange("(o s) v -> o s v", o=1), writes=[vnew])

                scoreS = K.sb("scoreS", [128, 4, 129], F32)
                K.op('pool', lambda e: e.memset(scoreS.t[:, :, 128:129], -1e30), writes=[scoreS])
                KI = [K.sb(f"KI{i}", [128, 32, 64], F32) for i in range(4)]
                kiTs = [K.sb(f"kiTs{i}", [128, 4, 128], BF16) for i in range(2)]
                pTr = [K.ps(f"pTr{i}", [128, 4, 128], F32) for i in range(2)]
                pX = [K.ps(f"pX{i}", [128, 512], F32) for i in range(2)]
                pB2 = K.ps("pB2", [128, 512], F32)
                tmpS = K.sb("tmpS", [128, 32, 8], F32)
                ngath = 0
                for s in range(4):
                    for half in range(2):
                        for cc in range(2):
                            c = half * 2 + cc
                            kib = KI[ngath % 4]
                            ngath += 1
                            K.dma('pool', None, None, reads=[ixk], writes=[kib],
                                  fn=lambda e, kib=kib, s=s, c=c: e.indirect_dma_start(
                                      out=kib.t[:].rearrange("p j d -> p (j d)"), out_offset=None, in_=ki_rows,
                                      in_offset=bass.IndirectOffsetOnAxis(ap=ixk.t[:, s, c:c + 1], axis=0)))
                            for t4 in range(4):
                                ptr, kt = pTr[t4 % 2], kiTs[t4 % 2]
                                for u in range(4):
                                    jj = t4 * 8 + u * 2
                                    K.op('pe', lambda e, ptr=ptr, u=u, kib=kib, jj=jj: e.transpose(
                                        ptr.t[:, u, :], kib.t[:, jj:jj + 2, :].rearrange("p j d -> p (j d)"), C('ident')), reads=[kib, cst], writes=[ptr])
                                K.op('act', lambda e, ptr=ptr, kt=kt: e.copy(kt.t[:], ptr.t[:]), reads=[ptr], writes=[kt])
                                for hf in range(2):
                                    for u in range(4):
                                        jj2 = cc * 16 + t4 * 4 + u
                                        K.op('pe', lambda e, kt=kt, u=u, hf=hf, jj2=jj2, s=s: e.matmul(
                                            pX[hf].t[:, jj2 * 8:(jj2 + 1) * 8], kt.t[hf * 64:(hf + 1) * 64, u, :], qiS.t[hf * 64:(hf + 1) * 64, :, s], start=True, stop=True),
                                            reads=[kt, qiS], writes=[pX[hf]])
                        for hf in range(2):
                            K.op('dve', lambda e, s=s, hf=hf: e.scalar_tensor_tensor(tmpS.t[:], pX[hf].t[:, 0:256].rearrange("p (j h) -> p j h", h=8), 0.0,
                                                                                  wsig.t[:, s, :].unsqueeze(1).to_broadcast([128, 32, 8]),
                                                                                  op0=ALU.max, op1=ALU.mult), reads=[pX[hf], wsig], writes=[tmpS])
                            K.op('dve', lambda e, s=s, half=half, hf=hf: e.tensor_reduce(
                                scoreS.t[:, s, half * 64:(half + 1) * 64].rearrange("p (a b) -> p a b", b=2)[:, :, hf], tmpS.t[:], AX.X, ALU.add),
                                reads=[tmpS], writes=[scoreS])
                    K.op('pe', lambda e, s=s: e.matmul(pB.t[0:1, 64:72], kiT2.t[0:64, T + s:T + s + 1], qiS.t[0:64, :, s], start=True, stop=True),
                         reads=[kiT2, qiS], writes=[pB])
                    K.op('dve', lambda e, s=s: e.scalar_tensor_tensor(tmpS.t[0:1, 0, :], pB.t[0:1, 64:72], 0.0, wsig.t[0:1, s, :], op0=ALU.max, op1=ALU.mult),
                         reads=[pB, wsig, scoreS], writes=[tmpS])
                    K.op('dve', lambda e, s=s: e.tensor_reduce(scoreS.t[0:1, s, 128:129], tmpS.t[0:1, 0, :], AX.X, ALU.add), reads=[tmpS], writes=[scoreS])

                bsS = K.sb("bsS", [128, 64], F32)
                dlS = K.sb("dlS", [128, 4, NITS + 1], F32)
                cvS = K.sb("cvS", [128, NITS + 1], F32)
                for k in range(NITS + 1):
                    K.op('pool', lambda e, k=k: e.memset(cvS.t[:, k:k + 1], 2.0 ** -(k + 1)), writes=[cvS])
                cmpS = K.sb("cmpS", [128, 4, 129], F32)
                mskS = K.sb("mskS", [128, 4, 129], F32)
                K.op('dve', lambda e: e.tensor_reduce(bsS.t[:, 0:4], scoreS.t[:], AX.X, ALU.max), reads=[scoreS], writes=[bsS])
                K.op('dve', lambda e: e.tensor_reduce(bsS.t[:, 4:8], scoreS.t[:, :, 0:128], AX.X, ALU.min), reads=[scoreS], writes=[bsS])
                K.op('dve', lambda e: e.tensor_scalar(bsS.t[:, 4:8], bsS.t[:, 4:8], -1.0, None, op0=ALU.mult), reads=[bsS], writes=[bsS])
                K.op('pe', lambda e: e.transpose(pB.t[0:8, 128:256], bsS.t[:, 0:8], C('ident')), reads=[bsS, cst], writes=[pB])
                K.op('dve', lambda e: e.tensor_reduce(bsS.t[0:8, 8:9], pB.t[0:8, 128:256], AX.X, ALU.max), reads=[pB], writes=[bsS])
                dg8 = K.sb("dg8", [8, 8], F32)
                K.op('dve', lambda e: e.tensor_scalar(dg8.t[:], C('ident')[0:8, 0:8], bsS.t[0:8, 8:9], None, op0=ALU.mult), reads=[bsS, cst], writes=[dg8])
                K.op('pe', lambda e: e.matmul(pB.t[:, 256:264], C('ones')[0:8, :], dg8.t[:], start=True, stop=True), reads=[cst, dg8], writes=[pB])
                K.op('act', lambda e: e.copy(bsS.t[:, 16:24], pB.t[:, 256:264]), reads=[pB], writes=[bsS])
                K.op('dve', lambda e: e.tensor_scalar(bsS.t[:, 20:24], bsS.t[:, 20:24], 1.0, None, op0=ALU.add), reads=[bsS], writes=[bsS])
                K.op('dve', lambda e: e.scalar_tensor_tensor(bsS.t[:, 24:28], bsS.t[:, 16:20], 1.0, bsS.t[:, 20:24], op0=ALU.add, op1=ALU.add), reads=[bsS], writes=[bsS])
                K.op('dve', lambda e: e.tensor_tensor(dlS.t[:], cvS.t[:, :].unsqueeze(1).to_broadcast([128, 4, NITS + 1]),
                                                      bsS.t[:, 24:28].unsqueeze(2).to_broadcast([128, 4, NITS + 1]), ALU.mult), reads=[cvS, bsS], writes=[dlS])
                K.op('dve', lambda e: e.tensor_tensor(bsS.t[:, 32:36], dlS.t[:, :, 0], bsS.t[:, 20:24], ALU.subtract), reads=[dlS, bsS], writes=[bsS])
                for k in range(NITS):
                    mo, mn = 32 + 4 * (k % 2), 32 + 4 * ((k + 1) % 2)
                    K.op('dve', lambda e, mo=mo: e.tensor_tensor(cmpS.t[:], scoreS.t[:], bsS.t[:, mo:mo + 4].unsqueeze(2).to_broadcast([128, 4, 129]), ALU.is_ge),
                         reads=[scoreS, bsS], writes=[cmpS])
                    K.op('dve', lambda e: e.tensor_reduce(bsS.t[:, 40:44], cmpS.t[:], AX.X, ALU.add), reads=[cmpS], writes=[bsS])
                    K.op('pe', lambda e: e.matmul(pB.t[:, 300:304], C('ones'), bsS.t[:, 40:44], start=True, stop=True), reads=[cst, bsS], writes=[pB])
                    K.op('dve', lambda e: e.tensor_scalar(bsS.t[:, 44:48], pB.t[:, 300:304], 255.5, 0.5, op0=ALU.is_ge, op1=ALU.subtract), reads=[pB], writes=[bsS])
                    K.op('dve', lambda e, k=k: e.tensor_tensor(bsS.t[:, 44:48], bsS.t[:, 44:48], dlS.t[:, :, k], ALU.mult), reads=[bsS, dlS], writes=[bsS])
                    K.op('dve', lambda e, mo=mo, mn=mn: e.tensor_tensor(bsS.t[:, mn:mn + 4], bsS.t[:, mo:mo + 4], bsS.t[:, 44:48], ALU.add), reads=[bsS], writes=[bsS])
                mf = 32 + 4 * (NITS % 2)
                K.op('dve', lambda e: e.scalar_tensor_tensor(bsS.t[:, 48:52], dlS.t[:, :, NITS], -2.0, bsS.t[:, mf:mf + 4], op0=ALU.mult, op1=ALU.add), reads=[bsS, dlS], writes=[bsS])
                K.op('dve', lambda e: e.tensor_tensor(mskS.t[:], scoreS.t[:], bsS.t[:, 48:52].unsqueeze(2).to_broadcast([128, 4, 129]), ALU.is_ge),
                     reads=[scoreS, bsS], writes=[mskS])

                KV = [K.sb(f"KV{i}", [128, 16, 256], F32) for i in range(3)]
                kTs = [K.sb(f"kTs{i}", [128, 4, 128], BF16) for i in range(2)]
                pTs = [K.sb(f"pTs{i}", [128, 16, 8], F32) for i in range(2)]
                acc8 = K.sb("acc8", [128, 8], F32)
                red8 = K.sb("red8", [128, 8], F32)
                pOs = K.ps("pOs", [4, 2, 64], F32)
                pDn = K.ps("pDn", [4, 8], F32)
                pnew = K.sb("pnew", [1, 8], F32)
                obs = K.sb("obs", [4, 2, 64], BF16)
                rdn = K.sb("rdn", [4, 2], F32)
                obT_rows = obT.t.ap().rearrange("c p t -> (c p) t")
                zf4 = K.sb("zf4", [128, 4], F32)
                K.op('pool', lambda e: e.memset(zf4.t[:], 0.0), writes=[zf4])
                nkv = 0
                for s in range(4):
                    K.op('pool', lambda e: e.memset(acc8.t[:], 0.0), writes=[acc8])
                    K.op('pe', lambda e: e.matmul(pOs.t[:].rearrange("p a b -> p (a b)"), zf4.t[:], C('ident'), start=True, stop=False), reads=[zf4, cst], writes=[pOs])
                    for c in range(8):
                        kvb = KV[nkv % 3]
                        pts = pTs[nkv % 2]
                        nkv += 1
                        K.dma('pool', None, None, reads=[ixv], writes=[kvb],
                              fn=lambda e, kvb=kvb, s=s, c=c: e.indirect_dma_start(
                                  out=kvb.t[:].rearrange("p j d -> p (j d)"), out_offset=None, in_=kv_rows,
                                  in_offset=bass.IndirectOffsetOnAxis(ap=ixv.t[:, s, c:c + 1], axis=0)))
                        for t4 in range(4):
                            ptr, kt = pTr[t4 % 2], kTs[t4 % 2]
                            for u in range(4):
                                jj = t4 * 4 + u
                                K.op('pe', lambda e, ptr=ptr, u=u, kvb=kvb, jj=jj: e.transpose(ptr.t[:, u, :], kvb.t[:, jj, 0:128], C('ident')),
                                     reads=[kvb, cst], writes=[ptr])
                            K.op('act', lambda e, ptr=ptr, kt=kt: e.copy(kt.t[:], ptr.t[:]), reads=[ptr], writes=[kt])
                            for u in range(4):
                                jj = t4 * 4 + u
                                for n in range(2):
                                    K.op('pe', lambda e, kt=kt, u=u, n=n, jj=jj, s=s: e.matmul(
                                        pX[n].t[:, jj * 4:(jj + 1) * 4], kt.t[n * 64:(n + 1) * 64, u, :], qbS.t[n * 64:(n + 1) * 64, :, s], start=True, stop=True),
                                        reads=[kt, qbS], writes=[pX[n]])
                        for n in range(2):
                            K.op('act', lambda e, pts=pts, n=n: e.activation(pts.t[:, :, 4 * n:4 * n + 4], pX[n].t[:, 0:64].rearrange("p (j h) -> p j h", h=4),
                                                                            AF.Exp, scale=0.125), reads=[pX[n]], writes=[pts])
                        K.op('dve', lambda e, pts=pts, s=s, c=c: e.tensor_tensor(pts.t[:], pts.t[:], mskS.t[:, s, c * 16:(c + 1) * 16].unsqueeze(2).to_broadcast([128, 16, 8]), ALU.mult),
                             reads=[pts, mskS], writes=[pts])
                        K.op('dve', lambda e, pts=pts: e.tensor_reduce(red8.t[:], pts.t[:].rearrange("p j h -> p h j"), AX.X, ALU.add), reads=[pts], writes=[red8])
                        K.op('dve', lambda e: e.tensor_tensor(acc8.t[:], acc8.t[:], red8.t[:], ALU.add), reads=[acc8, red8], writes=[acc8])
                        for jj in range(16):
                            for n in range(2):
                                K.op('pe', lambda e, pts=pts, jj=jj, n=n, kvb=kvb, c=c: e.matmul(
                                    pOs.t[:, n, :], pts.t[:, jj, 4 * n:4 * n + 4], kvb.t[:, jj, 128 + n * 64:128 + (n + 1) * 64],
                                    start=False, stop=False), reads=[pts, kvb], writes=[pOs])
                    for n in range(2):
                        pbn = pB if n == 0 else pB2
                        K.op('pe', lambda e, n=n, s=s, pbn=pbn: e.matmul(pbn.t[0:1, 320:324], kbT2.t[n * 64:(n + 1) * 64, n, T + s:T + s + 1],
                                                                        qbS.t[n * 64:(n + 1) * 64, :, s], start=True, stop=True), reads=[kbT2, qbS], writes=[pbn])
                        K.op('act', lambda e, n=n, pbn=pbn: e.activation(pnew.t[:, 4 * n:4 * n + 4], pbn.t[0:1, 320:324], AF.Exp, scale=0.125), reads=[pbn], writes=[pnew])
                    K.op('dve', lambda e, s=s: e.tensor_scalar(pnew.t[:], pnew.t[:], mskS.t[0:1, s, 128:129], None, op0=ALU.mult), reads=[pnew, mskS], writes=[pnew])
                    K.op('dve', lambda e: e.tensor_tensor(acc8.t[0:1, :], acc8.t[0:1, :], pnew.t[:], ALU.add), reads=[acc8, pnew], writes=[acc8])
                    for n in range(2):
                        K.op('pe', lambda e, n=n, s=s: e.matmul(pOs.t[:, n, :], pnew.t[0:1, 4 * n:4 * n + 4], vnew.t[0:1, s, n * 64:(n + 1) * 64],
                                                               start=False, stop=True), reads=[pnew, vnew], writes=[pOs])
                        K.op('pe', lambda e, n=n: e.matmul(pDn.t[:, n:n + 1], acc8.t[:, 4 * n:4 * n + 4], C('ones')[:, 0:1], start=True, stop=True),
                             reads=[acc8, cst], writes=[pDn])
                    K.op('dve', lambda e: e.reciprocal(rdn.t[:], pDn.t[:, 0:2]), reads=[pDn], writes=[rdn])
                    K.op('dve', lambda e: e.tensor_tensor(obs.t[:], pOs.t[:], rdn.t[:, :].unsqueeze(2).to_broadcast([4, 2, 64]), ALU.mult), reads=[pOs, rdn], writes=[obs])
                    for n in range(2):
                        K.dma('sp', obT_rows[n * 256:(n + 1) * 256, T + s:T + s + 1].rearrange("(r d) o -> r (d o)", d=64), obs.t[:, n, :],
                              reads=[obs], writes=[obT], allow_slow_non_contiguous=True)


        K.pop_scope()

        hS = K.dram("hS_s", [TT, D], F32)
        hnS = K.dram("hnS_s", [8, 128, TT], BF16)
        with K.scope():
            woa = K.sb("woa", [128, 4, D], BF16)
            wob = K.sb("wob", [128, 4, D], BF16)
            wo = K.sb("wo", [128, 8, D], BF16)
            g2T = K.sb("g2T", [128, 8], F32)
            K.dma('sp', g2T.t[:], norm2.rearrange("(k p) -> p k", p=128), writes=[g2T], allow_slow_non_contiguous=True)
            wst4 = [K.sb(f"wst4{i}", [128, D], F32) for i in range(2)]
            nld = 0
            for (src, dst, nk) in ((w_oa, woa, 4), (w_ob, wob, 4), (w_oo, wo, 8)):
                for k in range(nk):
                    st_ = wst4[nld % 2]
                    K.dma('sp', st_.t[:], src[k * 128:(k + 1) * 128, :], writes=[st_])
                    K.op('pool' if nld % 2 else 'act', (lambda e, st_=st_, dst=dst, k=k: e.tensor_copy(dst.t[:, k, :], st_.t[:])) if nld % 2 else
                         (lambda e, st_=st_, dst=dst, k=k: e.copy(dst.t[:, k, :], st_.t[:])), reads=[st_], writes=[dst])
                    nld += 1
            oat = [K.sb(f"oat{i}", [128, 4, 512], BF16) for i in range(2)]
            obt = [K.sb(f"obt{i}", [128, 4, 512], BF16) for i in range(2)]
            sga = [K.sb(f"sga{i}", [128, 512], F32) for i in range(2)]
            sgb = [K.sb(f"sgb{i}", [128, 512], F32) for i in range(2)]
            m1 = K.sb("m1", [128, 512], F32)
            m2 = K.sb("m2", [128, 512], F32)
            mixT = K.sb("mixT", [128, 8, 512], BF16)
            pMa = [K.ps(f"pMa{i}", [128, 512], F32) for i in range(2)]
            pMb = [K.ps(f"pMb{i}", [128, 512], F32) for i in range(2)]
            pH = [K.ps(f"pH{i}", [128, 512], F32) for i in range(2)]
            pT4 = K.ps("pT4", [128, 8, 128], BF16)
            xb = [K.sb(f"xb{i}", [128, D], F32) for i in range(2)]
            hb = [K.sb(f"hb{i}", [128, D], F32) for i in range(2)]
            junk4 = K.sb("junk4", [128, D], F32)
            st4 = K.sb("st4", [128, 4], F32)
            hs4 = K.sb("hs4", [128, D], BF16)
            hnt = [K.sb(f"hnt{i}", [128, 8, 128], BF16) for i in range(2)]
            nblk = 0
            for t in range(P4A_TILES):
                w = 512 if t < 8 else 128
                tok = slice(t * 512, t * 512 + w)
                i = t % 2
                K.dma('sp', oat[i].t[:, :, 0:w], oaT.t.ap()[:, :, tok].rearrange("c p t -> p c t"), reads=[oaT], writes=[oat[i]])
                K.dma('sp', obt[i].t[:, :, 0:w], obT.t.ap()[:, :, tok].rearrange("c p t -> p c t"), reads=[obT], writes=[obt[i]])
                for oc in range(8):
                    j = oc % 2
                    K.dma('sp', sga[j].t[:, 0:w], sgT.t.ap()[oc, :, tok], reads=[sgT], writes=[sga[j]])
                    K.dma('sp', sgb[j].t[:, 0:w], sgT.t.ap()[8 + oc, :, tok], reads=[sgT], writes=[sgb[j]])
                    for k in range(4):
                        K.op('pe', lambda e, j=j, k=k, oc=oc, i=i, w=w: e.matmul(pMa[j].t[:, 0:w], woa.t[:, k, oc * 128:(oc + 1) * 128], oat[i].t[:, k, 0:w],
                                                                              start=(k == 0), stop=(k == 3)), reads=[woa, oat[i]], writes=[pMa[j]])
                    for k in range(4):
                        K.op('pe', lambda e, j=j, k=k, oc=oc, i=i, w=w: e.matmul(pMb[j].t[:, 0:w], wob.t[:, k, oc * 128:(oc + 1) * 128], obt[i].t[:, k, 0:w],
                                                                              start=(k == 0), stop=(k == 3)), reads=[wob, obt[i]], writes=[pMb[j]])
                    K.op('dve', lambda e, j=j, w=w: e.tensor_tensor(m1.t[:, 0:w], pMa[j].t[:, 0:w], sga[j].t[:, 0:w], ALU.mult), reads=[pMa[j], sga[j]], writes=[m1])
                    K.op('dve', lambda e, j=j, w=w: e.tensor_tensor(m2.t[:, 0:w], pMb[j].t[:, 0:w], sgb[j].t[:, 0:w], ALU.mult), reads=[pMb[j], sgb[j]], writes=[m2])
                    K.op('pool', lambda e, oc=oc, w=w: e.tensor_tensor(mixT.t[:, oc, 0:w], m1.t[:, 0:w], m2.t[:, 0:w], ALU.add), reads=[m1, m2], writes=[mixT])
                for bb in range(w // 128):
                    b = t * 4 + bb
                    blk = slice(b * 128, (b + 1) * 128)
                    i2 = nblk % 2
                    nblk += 1
                    K.dma('sp', xb[i2].t[:], x_in[blk, :], writes=[xb[i2]])
                    for half in range(2):
                        ph = pH[half]
                        for k in range(8):
                            K.op('pe', lambda e, ph=ph, k=k, bb=bb, half=half: e.matmul(ph.t[:], mixT.t[:, k, bb * 128:(bb + 1) * 128], wo.t[:, k, half * 512:(half + 1) * 512],
                                                                                  start=(k == 0), stop=(k == 7)), reads=[mixT, wo], writes=[ph])
                        K.op('dve', lambda e, ph=ph, half=half, i2=i2: e.tensor_tensor(hb[i2].t[:, half * 512:(half + 1) * 512], ph.t[:], xb[i2].t[:, half * 512:(half + 1) * 512], ALU.add),
                             reads=[ph, xb[i2]], writes=[hb[i2]])
                    K.dma('sp', hS.t.ap()[blk, :], hb[i2].t[:], reads=[hb[i2]], writes=[hS])
                    K.op('act', lambda e, i2=i2: e.activation(junk4.t[:], hb[i2].t[:], AF.Square, accum_out=st4.t[:, 0:1]), reads=[hb[i2]], writes=[junk4, st4])
                    K.op('act', lambda e: e.activation(st4.t[:, 1:2], st4.t[:, 0:1], AF.Sqrt, bias=EPS, scale=1.0 / D), reads=[st4], writes=[st4])
                    K.op('dve', lambda e: e.reciprocal(st4.t[:, 2:3], st4.t[:, 1:2]), reads=[st4], writes=[st4])
                    K.op('dve', lambda e, i2=i2: e.tensor_scalar(hs4.t[:], hb[i2].t[:], st4.t[:, 2:3], None, op0=ALU.mult), reads=[hb[i2], st4], writes=[hs4])
                    for k in range(8):
                        K.op('pe', lambda e, k=k: e.transpose(pT4.t[:, k, :], hs4.t[:, k * 128:(k + 1) * 128], identb.t[:]), reads=[hs4, identb], writes=[pT4])
                    K.op('dve', lambda e, i2=i2: e.tensor_tensor(hnt[i2].t[:], pT4.t[:], g2T.t[:, :].unsqueeze(2).to_broadcast([128, 8, 128]), ALU.mult),
                         reads=[pT4, g2T], writes=[hnt[i2]])
                    K.dma('sp', hnS.t.ap()[:, :, blk].rearrange("c p t -> p c t"), hnt[i2].t[:], reads=[hnt[i2]], writes=[hnS])

        with K.scope():
            wup = K.sb("wup", [128, 8, 4 * D], BF16)
            wdn = K.sb("wdn", [128, 32, D], BF16)
            wsu = [K.sb(f"wsu{i}", [128, 2048], F32) for i in range(2)]
            nld = 0
            engs = ('act', 'pool', 'dve')
            for k in range(8):
                for hf in range(2):
                    st_ = wsu[nld % 2]
                    K.dma('sp', st_.t[:], w_up_in[k * 128:(k + 1) * 128, hf * 2048:(hf + 1) * 2048], writes=[st_])
                    en = engs[nld % 3]
                    if en == 'act':
                        K.op('act', lambda e, st_=st_, k=k, hf=hf: e.copy(wup.t[:, k, hf * 2048:(hf + 1) * 2048], st_.t[:]), reads=[st_], writes=[wup])
                    else:
                        K.op(en, lambda e, st_=st_, k=k, hf=hf: e.tensor_copy(wup.t[:, k, hf * 2048:(hf + 1) * 2048], st_.t[:]), reads=[st_], writes=[wup])
                    nld += 1
            for kk in range(16):
                st_ = wsu[nld % 2]
                K.dma('sp', st_.t[:].rearrange("p (a c) -> p a c", a=2), w_dn_in[kk * 256:(kk + 1) * 256, :].rearrange("(a p) c -> p a c", p=128), writes=[st_])
                en = engs[nld % 3]
                dstap = wdn.t[:, 2 * kk:2 * kk + 2, :].rearrange("p a c -> p (a c)")
                if en == 'act':
                    K.op('act', lambda e, st_=st_, dstap=dstap: e.copy(dstap, st_.t[:]), reads=[st_], writes=[wdn])
                else:
                    K.op(en, lambda e, st_=st_, dstap=dstap: e.tensor_copy(dstap, st_.t[:]), reads=[st_], writes=[wdn])
                nld += 1
            hn_t = [K.sb(f"hn_t{i}", [128, 8, 256], BF16) for i in range(2)]
            actT = K.sb("actT", [128, 32, 256], BF16)
            rl = [K.sb(f"rl{i}", [128, 256], BF16) for i in range(2)]
            pUp = [K.ps(f"pUp{i}", [128, 256], F32) for i in range(3)]
            pY = [K.ps(f"pY{i}", [128, 512], F32) for i in range(4)]
            hb2 = [K.sb(f"hb2{i}", [128, D], F32) for i in range(2)]
            yb = [K.sb(f"yb{i}", [128, D], F32) for i in range(2)]
            nblk = 0
            npy = 0
            for t in range(P4B_TILES):
                w = 256 if t < 16 else 128
                tok = slice(t * 256, t * 256 + w)
                i = t % 2
                K.dma('sp', hn_t[i].t[:, :, 0:w], hnS.t.ap()[:, :, tok].rearrange("c p t -> p c t"), reads=[hnS], writes=[hn_t[i]])
                for f in range(32):
                    pu = pUp[f % 3]
                    for k in range(8):
                        K.op('pe', lambda e, pu=pu, k=k, f=f, i=i, w=w: e.matmul(pu.t[:, 0:w], wup.t[:, k, f * 128:(f + 1) * 128], hn_t[i].t[:, k, 0:w],
                                                                              start=(k == 0), stop=(k == 7)), reads=[wup, hn_t[i]], writes=[pu])
                    r_ = rl[f % 2]
                    K.op('act', lambda e, pu=pu, r_=r_, w=w: e.activation(r_.t[:, 0:w], pu.t[:, 0:w], AF.Relu), reads=[pu], writes=[r_])
                    K.op('pool' if f % 2 else 'dve', lambda e, r_=r_, f=f, w=w: e.tensor_tensor(actT.t[:, f, 0:w], r_.t[:, 0:w], r_.t[:, 0:w], ALU.mult), reads=[r_], writes=[actT])
                for bb in range(w // 128):
                    b = t * 2 + bb
                    blk = slice(b * 128, (b + 1) * 128)
                    i2 = nblk % 2
                    nblk += 1
                    K.dma('sp', hb2[i2].t[:], hS.t.ap()[blk, :], reads=[hS], writes=[hb2[i2]])
                    for half in range(2):
                        py = pY[npy % 4]
                        npy += 1
                        for f in range(32):
                            K.op('pe', lambda e, py=py, f=f, bb=bb, half=half: e.matmul(py.t[:], actT.t[:, f, bb * 128:(bb + 1) * 128], wdn.t[:, f, half * 512:(half + 1) * 512],
                                                                                  start=(f == 0), stop=(f == 31)), reads=[actT, wdn], writes=[py])
                        K.op('dve', lambda e, py=py, half=half, i2=i2: e.tensor_tensor(yb[i2].t[:, half * 512:(half + 1) * 512], py.t[:], hb2[i2].t[:, half * 512:(half + 1) * 512], ALU.add),
                             reads=[py, hb2[i2]], writes=[yb[i2]])
                    K.dma('sp', y_out[blk, :], yb[i2].t[:], reads=[yb[i2]])

        K.finish()
    return nc


_CACHE = {}


def kernel(x_prompt, x_sample, cache_kv, cache_kidx, page_table, state_ssm, state_conv,
           norm1_g, w_in, conv_w, a_log, dt_bias, gdn_norm_g, q_norm_g, k_norm_g,
           w_out_a, w_out_b, w_o, norm2_g, w_up, w_down):
    if 'nc' not in _CACHE:
        _CACHE['nc'] = build_program()
    nc = _CACHE['nc']
    f = lambda a: np.ascontiguousarray(np.asarray(a, dtype=np.float32))
    cst = make_consts()
    cs = rope_table()
    ckv = np.ascontiguousarray(np.asarray(cache_kv[0], dtype=np.float32).reshape(NPOOL, 128 * 256))
    cki = np.ascontiguousarray(np.asarray(cache_kidx[0], dtype=np.float32).reshape(NPOOL, 128 * 64))
    ptab = np.ascontiguousarray(np.asarray(page_table, dtype=np.int32))
    in_maps = []
    for c in range(NCR):
        x = np.zeros((TT, D), np.float32)
        x[:T] = x_prompt[c]
        x[T:T + 4] = np.asarray(x_sample)[4 * c:4 * c + 4, 0]
        in_maps.append({
            "x": x, "cs": cs, "cst": cst,
            "w_in": f(w_in[0]), "norm1_g": f(norm1_g[0]), "conv_w": f(conv_w[0]),
            "a_log": f(a_log[0]), "dt_bias": f(dt_bias[0]), "gdn_norm_g": f(gdn_norm_g[0]),
            "q_norm_g": f(q_norm_g[0]), "k_norm_g": f(k_norm_g[0]),
            "state_conv": f(state_conv[0][4 * c:4 * c + 4]),
            "state_ssm": f(state_ssm[0][4 * c:4 * c + 4]),
            "w_out_a": f(w_out_a[0]), "w_out_b": f(w_out_b[0]), "w_o": f(w_o[0]), "norm2_g": f(norm2_g[0]),
            "w_up": f(w_up[0]), "w_down": f(w_down[0]),
            "cache_kidx": cki, "cache_kv": ckv, "page_tab": np.ascontiguousarray(ptab[4 * c:4 * c + 4]),
        })
    res = run_bass_kernel_spmd(nc, in_maps, core_ids=list(range(NCR)))
    R = list(res.results)
    while len(R) < NCORES:
        R.append(R[0])
    if DEBUG_OUT:
        _DBG['r0'] = R[0]
    y = np.stack([r["y"] for r in R])
    kv = np.stack([r["kv"] for r in R])
    kidx = np.stack([r["kidx"] for r in R])
    y_prompt = y[:, :T]
    y_sample = y[:, T:T + 4].reshape(32, 1, D)
    kv_prompt = kv[:, :T].reshape(1, 8, T, 2, 2, 64)
    kv_sample = kv[:, T:T + 4].reshape(1, 32, 1, 2, 2, 64)
    kidx_prompt = kidx[:, :T].reshape(1, 8, T, 64)
    kidx_sample = kidx[:, T:T + 4].reshape(1, 32, 1, 64)
    conv_prompt = np.stack([r["conv_p"] for r in R]).reshape(1, 8, 3, 1536)
    conv_sample = np.stack([r["conv_s"] for r in R]).reshape(1, 32, 3, 1536)
    ssm_prompt = np.stack([r["ssm_p"] for r in R]).reshape(1, 8, 4, 128, 128)
    ssm_sample = np.stack([r["ssm_s"] for r in R]).reshape(1, 32, 4, 128, 128)
    return (np.ascontiguousarray(y_prompt), y_sample, kv_prompt, kidx_prompt, ssm_prompt, conv_prompt,
            kv_sample, kidx_sample, ssm_sample, conv_sample)
```

```python
import contextlib
import numpy as np
import concourse.bass as bass
import concourse.mybir as mybir
from concourse.bass_utils import run_bass_kernel_spmd

F32 = mybir.dt.float32
BF16 = mybir.dt.bfloat16
I32 = mybir.dt.int32
AF = mybir.ActivationFunctionType
ALU = mybir.AluOpType
AX = mybir.AxisListType

NCORES = 8
NCR = 8
T = 4096
NB = 33
TT = NB * 128
D = 1024
NIN = 5456
EPS = 1e-6
PAST = 16384
NPOOL = 5120
P2_BLOCKS = 32
P1_ENABLE = True
P2_STAGE = 99
DBG = 0
P2_SAMPLE = True
P3_BLOCKS = 32
P4A_TILES = 9
DEBUG_OUT = False
_DBG = {}
P3S_ENABLE = True
P3_STAGE = 99
P4B_TILES = 17


class Buf:
    def __init__(self, t, name):
        self.t = t
        self.name = name
        self.w = {}
        self.r = {}
        self.excl = False


class Kern:
    STREAMS = ('pe', 'act', 'dve', 'pool', 'sp')
    NROT = {'sp': 24, 'pool': 8}
    SEMS = ('pe', 'act', 'dve', 'pool') + tuple(f'sp_dma{i}' for i in range(24)) + tuple(f'pool_dma{i}' for i in range(8))

    def __init__(self, nc):
        self.nc = nc
        self.stack = [contextlib.ExitStack()]
        self.rec = {s: [] for s in self.STREAMS}
        self.cnt = {}
        self.seen = {s: {} for s in self.STREAMS}
        self.sems = {}
        self.rr = {'sp': 0, 'pool': 0}

    def __enter__(self):
        self.stack[0].__enter__()
        for k in self.SEMS:
            self.sems[k] = self.stack[0].enter_context(self.nc.semaphore("s_" + k))
            self.cnt[k] = 0
        return self

    def __exit__(self, *a):
        return self.stack[0].__exit__(*a)

    @contextlib.contextmanager
    def scope(self):
        es = contextlib.ExitStack()
        es.__enter__()
        self.stack.append(es)
        try:
            yield
        finally:
            self.barrier()
            self.stack.pop()
            es.__exit__(None, None, None)

    def push_scope(self):
        es = contextlib.ExitStack()
        es.__enter__()
        self.stack.append(es)

    def pop_scope(self):
        self.barrier()
        es = self.stack.pop()
        es.__exit__(None, None, None)

    def sb(self, name, shape, dt):
        return Buf(self.stack[-1].enter_context(self.nc.sbuf_tensor("sb_" + name, list(shape), dt)), name)

    def ps(self, name, shape, dt):
        n = 2048 // (2 if dt == BF16 else 4)
        h = self.stack[-1].enter_context(self.nc.psum_tensor("ps_" + name, [128, n], dt))
        free = 1
        for d in shape[1:]:
            free *= d
        ap = h[0:shape[0], 0:free]
        if len(shape) == 3:
            ap = ap.rearrange("p (a b) -> p a b", a=shape[1])
        elif len(shape) == 4:
            ap = ap.rearrange("p (a b c) -> p a b c", a=shape[1], b=shape[2])
        b = Buf(ap, name)
        b.excl = True
        return b

    def dram(self, name, shape, dt, kind="Internal"):
        return Buf(self.nc.dram_tensor(name, list(shape), dt, kind=kind), name)

    def _wait(self, stream, k, v):
        if self.seen[stream].get(k, 0) < v:
            self.seen[stream][k] = v
            self.rec[stream].append(('wait', k, v))

    def _deps(self, stream, reads, writes, self_sync=True):
        deps = {}

        def add(k, v):
            if k == 'pe' and stream == 'pe':
                return
            if k == stream and not self_sync:
                return
            if deps.get(k, 0) < v:
                deps[k] = v
        for b in reads:
            for k, v in b.w.items():
                add(k, v)
            if b.excl:
                for k, v in b.r.items():
                    if k != stream:
                        add(k, v)
        for b in writes:
            for k, v in b.w.items():
                add(k, v)
            for k, v in b.r.items():
                add(k, v)
        for k, v in deps.items():
            self._wait(stream, k, v)

    def _mark(self, key, val, reads, writes):
        for b in writes:
            b.w[key] = val
        for b in reads:
            b.r[key] = val

    def op(self, stream, fn, reads=(), writes=(), self_sync=True):
        reads, writes = list(reads), list(writes)
        self._deps(stream, reads, writes, self_sync)
        self.cnt[stream] += 1
        self.rec[stream].append(('op', fn, stream, 1))
        self._mark(stream, self.cnt[stream], reads, writes)

    def dma(self, stream, out, in_, reads=(), writes=(), fn=None, **kw):
        reads, writes = list(reads), list(writes)
        self._deps(stream, reads, writes)
        key = f"{stream}_dma{self.rr[stream] % self.NROT[stream]}"
        self.rr[stream] += 1
        self.cnt[key] += 16
        if fn is None:
            fn = (lambda e, out=out, in_=in_, kw=kw: e.dma_start(out=out, in_=in_, **kw))
        self.rec[stream].append(('op', fn, key, 16))
        self._mark(key, self.cnt[key], reads, writes)

    def barrier(self):
        for s in self.STREAMS:
            for k in self.SEMS:
                if self.cnt[k] > 0:
                    self._wait(s, k, self.cnt[k])

    def finish(self):
        self.barrier()
        nc, rec, sems = self.nc, self.rec, self.sems

        def play(eng, items):
            for it in items:
                if it[0] == 'wait':
                    eng.wait_ge(sems[it[1]], it[2])
                else:
                    it[1](eng).then_inc(sems[it[2]], it[3])

        with nc.Block() as block:
            @block.tensor
            def _(e):
                play(e, rec['pe'])

            @block.scalar
            def _(e):
                play(e, rec['act'])

            @block.vector
            def _(e):
                play(e, rec['dve'])

            @block.gpsimd
            def _(e):
                play(e, rec['pool'])

            @block.sync
            def _(e):
                play(e, rec['sp'])


CST_NAMES = ['ident', 'ones', 'U', 'BD', 'CM0', 'CM1', 'MsT', 'Ms', 'MiT', 'C0', 'C1', 'CA']


def make_consts():
    i = np.arange(128)
    same = (i[:, None] // 64) == (i[None, :] // 64)
    c = {}
    c['ident'] = np.eye(128)
    c['ones'] = np.ones((128, 128))
    c['U'] = same & (i[:, None] <= i[None, :])
    c['BD'] = same
    c['CM0'] = np.broadcast_to((i[:, None] < 64), (128, 128))
    c['CM1'] = np.broadcast_to((i[:, None] >= 64), (128, 128))
    c['MsT'] = same & (i[None, :] > i[:, None])
    c['Ms'] = same & (i[:, None] > i[None, :])
    c['MiT'] = same & (i[None, :] >= i[:, None])
    c['C0'] = np.broadcast_to((i[None, :] < 64), (128, 128))
    c['C1'] = np.broadcast_to((i[None, :] >= 64), (128, 128))
    c['CA'] = np.where(i[None, :] <= i[:, None], 0.0, -1e30)
    return np.stack([np.asarray(c[n], dtype=np.float32) for n in CST_NAMES], axis=1)


def rope_table():
    half = 32
    inv = (np.float32(10000.0) ** (-np.arange(half, dtype=np.float32) / np.float32(half))).astype(np.float32)
    pos = np.zeros(TT, dtype=np.float32)
    pos[:T] = np.arange(T, dtype=np.float32)
    pos[T:T + 4] = np.float32(PAST)
    ang = (pos[:, None] * inv[None, :]).astype(np.float32)
    return np.concatenate([np.cos(ang), np.sin(ang)], axis=1).astype(np.float32)


def build_program():
    nc = bass.Bass("TRN2", target_bir_lowering=False)

    def din(name, shape, dt=F32):
        return nc.dram_tensor(name, list(shape), dt, kind="ExternalInput").ap()

    def dout(name, shape, dt=F32):
        return nc.dram_tensor(name, list(shape), dt, kind="ExternalOutput").ap()

    x_in = din("x", [TT, D])
    cs_in = din("cs", [TT, 64])
    cst_in = din("cst", [128, len(CST_NAMES), 128])
    w_in = din("w_in", [D, NIN])
    norm1 = din("norm1_g", [D])
    conv_w = din("conv_w", [4, 1536])
    a_log = din("a_log", [4])
    dt_bias = din("dt_bias", [4])
    gdn_g = din("gdn_norm_g", [128])
    q_g = din("q_norm_g", [64])
    k_g = din("k_norm_g", [64])
    st_conv = din("state_conv", [4, 3, 1536])
    st_ssm = din("state_ssm", [4, 4, 128, 128])
    cache_kidx = din("cache_kidx", [NPOOL, 128 * 64])
    cache_kv = din("cache_kv", [NPOOL, 128 * 256])
    page_tab = din("page_tab", [4, 128], I32)
    w_oa = din("w_out_a", [512, D])
    w_ob = din("w_out_b", [512, D])
    w_oo = din("w_o", [D, D])
    norm2 = din("norm2_g", [D])
    w_up_in = din("w_up", [D, 4 * D])
    w_dn_in = din("w_down", [4 * D, D])

    y_out = dout("y", [TT, D])
    kv_out = dout("kv", [TT, 256])
    kidx_out = dout("kidx", [TT, 64])
    convp_out = dout("conv_p", [3, 1536])
    convs_out = dout("conv_s", [4, 3, 1536])
    ssmp_out = dout("ssm_p", [4, 128, 128])
    ssms_out = dout("ssm_s", [4, 4, 128, 128])

    K = Kern(nc)
    with K:
        cst = K.sb("cst", [128, len(CST_NAMES), 128], F32)
        K.dma('sp', cst.t[:], cst_in[:, :, :], writes=[cst])

        def C(name):
            return cst.t[:, CST_NAMES.index(name), :]
        identb = K.sb("identb", [128, 128], BF16)
        K.op('dve', lambda e: e.tensor_copy(identb.t[:], C('ident')), reads=[cst], writes=[identb])

        K.push_scope()
        kbT2 = K.sb("kbT2", [128, 2, TT], BF16)
        kiT2 = K.sb("kiT2", [128, TT], BF16)
        v65 = K.sb("v65", [128, NB, 2, 65], BF16)
        gtm = K.sb("gtm", [128, NB, 4], F32)
        btm = K.sb("btm", [128, NB, 4], F32)
        wab = K.sb("wab", [128, NB, 8], F32)
        wsg = K.sb("wsg", [128, NB, 8], F32)
        K.op('pool', lambda e: e.memset(v65.t[:], 1.0), writes=[v65])

        prm = K.sb("prm", [128, 4 + 4 + 128 + 64 + 64], F32)
        K.dma('sp', prm.t[:, 0:4], a_log.partition_broadcast(128), writes=[prm])
        K.dma('sp', prm.t[:, 4:8], dt_bias.partition_broadcast(128), writes=[prm])
        K.dma('sp', prm.t[:, 8:136], gdn_g.partition_broadcast(128), writes=[prm])
        K.dma('sp', prm.t[:, 136:200], q_g.partition_broadcast(128), writes=[prm])
        K.dma('sp', prm.t[:, 200:264], k_g.partition_broadcast(128), writes=[prm])
        negA = K.sb("negA", [128, 4], F32)
        K.op('act', lambda e: e.activation(negA.t[:], prm.t[:, 0:4], AF.Exp), reads=[prm], writes=[negA])
        K.op('dve', lambda e: e.tensor_scalar(negA.t[:], negA.t[:], -1.0, None, op0=ALU.mult), reads=[negA], writes=[negA])
        g1T = K.sb("g1T", [128, 8], F32)
        K.dma('sp', g1T.t[:], norm1.rearrange("(k p) -> p k", p=128), writes=[g1T], allow_slow_non_contiguous=True)
        cw = K.sb("cw", [128, 12, 4], F32)
        for j in range(4):
            K.dma('sp', cw.t[:, :, j], conv_w[j].rearrange("(c p) -> p c", p=128), writes=[cw], allow_slow_non_contiguous=True)

        cT = K.dram("cT_s", [12, 128, TT], F32)
        zT = K.dram("zT_s", [4, 128, TT], F32)
        sgT = K.dram("sgT_s", [16, 128, TT], F32)
        qsT = K.dram("qsT_s", [NB, 128, 8, 128], BF16)

        with (K.scope() if P1_ENABLE else contextlib.nullcontext()):
          if P1_ENABLE:
            xnT = K.sb("xnT", [128, 8, TT], BF16)
            with K.scope():
                xt = [K.sb(f"xt{i}", [128, D], F32) for i in range(2)]
                junk = K.sb("junk", [128, D], F32)
                ssq = [K.sb(f"ssq{i}", [128, 2], F32) for i in range(2)]
                xs = [K.sb(f"xs{i}", [128, D], BF16) for i in range(2)]
                pT = [K.ps(f"pT{i}", [128, 8, 128], BF16) for i in range(2)]
                for b in range(NB):
                    i = b % 2
                    K.dma('sp', xt[i].t[:], x_in[b * 128:(b + 1) * 128, :], writes=[xt[i]])
                    K.op('act', lambda e, i=i: e.activation(junk.t[:], xt[i].t[:], AF.Square, accum_out=ssq[i].t[:, 0:1]),
                         reads=[xt[i]], writes=[junk, ssq[i]])
                    K.op('act', lambda e, i=i: e.activation(ssq[i].t[:, 1:2], ssq[i].t[:, 0:1], AF.Sqrt, bias=EPS, scale=1.0 / D),
                         reads=[ssq[i]], writes=[ssq[i]])
                    K.op('dve', lambda e, i=i: e.reciprocal(ssq[i].t[:, 0:1], ssq[i].t[:, 1:2]), reads=[ssq[i]], writes=[ssq[i]])
                    K.op('dve', lambda e, i=i: e.tensor_scalar(xs[i].t[:], xt[i].t[:], ssq[i].t[:, 0:1], None, op0=ALU.mult),
                         reads=[xt[i], ssq[i]], writes=[xs[i]])
                    for k in range(8):
                        K.op('pe', lambda e, i=i, k=k: e.transpose(pT[i].t[:, k, :], xs[i].t[:, k * 128:(k + 1) * 128], identb.t[:]),
                             reads=[xs[i], identb], writes=[pT[i]])
                    K.op('dve', lambda e, i=i, b=b: e.tensor_tensor(
                        xnT.t[:, :, b * 128:(b + 1) * 128], pT[i].t[:], g1T.t[:, :].unsqueeze(2).to_broadcast([128, 8, 128]), ALU.mult),
                        reads=[pT[i], g1T], writes=[xnT])

            with K.scope():
                wtm = K.sb("wtm", [128, 8, 1360], BF16)
                wst = [K.sb(f"wst{i}", [128, 1360], F32) for i in range(2)]
                for k in range(8):
                    i = k % 2
                    K.dma('sp', wst[i].t[:], w_in[k * 128:(k + 1) * 128, 2048:3408], writes=[wst[i]])
                    K.op('act', lambda e, i=i, k=k: e.copy(wtm.t[:, k, :], wst[i].t[:]), reads=[wst[i]], writes=[wtm])
                pSm = K.ps("pSm", [128, 512], F32)
                pQ = K.ps("pQ", [128, 512], F32)
                pKV = K.ps("pKV", [128, 512], F32)
                pI = K.ps("pI", [128, 512], F32)
                pTq = K.ps("pTq", [128, 8, 128], BF16)
                pTk = K.ps("pTk", [128, 4, 128], BF16)
                cs = [K.sb(f"cs{i}", [128, 64], F32) for i in range(2)]
                sm = K.sb("sm", [128, 16], F32)
                sq = K.sb("sq", [128, 512], F32)
                qn = K.sb("qn", [128, 512], F32)
                rn = K.sb("rn", [128, 16], F32)
                t1 = K.sb("t1", [128, 8, 32], F32)
                t2 = K.sb("t2", [128, 8, 32], F32)
                qr = K.sb("qr", [128, 512], BF16)
                qir = K.sb("qir", [128, 512], BF16)
                kvo = [K.sb(f"kvo{i}", [128, 256], F32) for i in range(2)]
                kio = [K.sb(f"kio{i}", [128, 64], F32) for i in range(2)]
                kdup = K.sb("kdup", [128, 3, 128], BF16)
                qst = [K.sb(f"qst{i}", [128, 8, 128], BF16) for i in range(2)]

                def rope(src3, dst3, nh, cst_b, srcbufs, dstbufs):
                    cosb = cst_b.t[:, 0:32].unsqueeze(1).to_broadcast([128, nh, 32])
                    sinb = cst_b.t[:, 32:64].unsqueeze(1).to_broadcast([128, nh, 32])
                    a1, a2 = t1.t[:, 0:nh, :], t2.t[:, 0:nh, :]
                    x1, x2 = src3[:, :, 0:32], src3[:, :, 32:64]
                    K.op('dve', lambda e: e.tensor_tensor(a1, x1, cosb, ALU.mult), reads=srcbufs + [cst_b], writes=[t1])
                    K.op('dve', lambda e: e.tensor_tensor(a2, x2, sinb, ALU.mult), reads=srcbufs + [cst_b], writes=[t2])
                    K.op('dve', lambda e: e.tensor_tensor(dst3[:, :, 0:32], a1, a2, ALU.subtract), reads=[t1, t2], writes=dstbufs)
                    K.op('dve', lambda e: e.tensor_tensor(a1, x2, cosb, ALU.mult), reads=srcbufs + [cst_b], writes=[t1])
                    K.op('dve', lambda e: e.tensor_tensor(a2, x1, sinb, ALU.mult), reads=srcbufs + [cst_b], writes=[t2])
                    K.op('dve', lambda e: e.tensor_tensor(dst3[:, :, 32:64], a1, a2, ALU.add), reads=[t1, t2], writes=dstbufs)

                def qknorm(psrc, nh, gofs, psbuf):
                    n = nh * 64
                    K.op('act', lambda e: e.activation(sq.t[:, 0:n], psrc, AF.Square), reads=[psbuf], writes=[sq])
                    K.op('dve', lambda e: e.tensor_reduce(rn.t[:, 0:nh], sq.t[:, 0:n].rearrange("p (h d) -> p h d", d=64), AX.X, ALU.add),
                         reads=[sq], writes=[rn])
                    K.op('act', lambda e: e.activation(rn.t[:, 8:8 + nh], rn.t[:, 0:nh], AF.Sqrt, bias=EPS, scale=1.0 / 64),
                         reads=[rn], writes=[rn])
                    K.op('dve', lambda e: e.reciprocal(rn.t[:, 0:nh], rn.t[:, 8:8 + nh]), reads=[rn], writes=[rn])
                    q3 = qn.t[:, 0:n].rearrange("p (h d) -> p h d", d=64)
                    K.op('dve', lambda e: e.tensor_tensor(q3, psrc.rearrange("p (h d) -> p h d", d=64),
                                                          rn.t[:, 0:nh].unsqueeze(2).to_broadcast([128, nh, 64]), ALU.mult),
                         reads=[psbuf, rn], writes=[qn])
                    K.op('dve', lambda e: e.tensor_tensor(q3, q3, prm.t[:, gofs:gofs + 64].unsqueeze(1).to_broadcast([128, nh, 64]), ALU.mult),
                         reads=[qn, prm], writes=[qn])
                    return q3

                for b in range(NB):
                    i = b % 2
                    blk = slice(b * 128, (b + 1) * 128)
                    K.dma('sp', cs[i].t[:], cs_in[blk, :], writes=[cs[i]])
                    for (pt, c0, c1, o0) in ((pQ, 8, 520, 0), (pKV, 520, 776, 0), (pI, 776, 1288, 0), (pSm, 0, 8, 0), (pSm, 1288, 1360, 8)):
                        for k in range(8):
                            K.op('pe', lambda e, pt=pt, c0=c0, c1=c1, o0=o0, k=k, blk=blk: e.matmul(
                                pt.t[:, o0:o0 + (c1 - c0)], xnT.t[:, k, blk], wtm.t[:, k, c0:c1], start=(k == 0), stop=(k == 7)),
                                reads=[xnT, wtm], writes=[pt])
                    q3 = qknorm(pQ.t[:, 0:512], 8, 136, pQ)
                    rope(q3, qr.t[:].rearrange("p (h d) -> p h d", d=64), 8, cs[i], [qn], [qr])
                    k3 = qknorm(pKV.t[:, 0:128], 2, 200, pKV)
                    rope(k3, kvo[i].t[:, 0:128].rearrange("p (h d) -> p h d", d=64), 2, cs[i], [qn], [kvo[i]])
                    K.op('act', lambda e, i=i: e.copy(kvo[i].t[:, 128:256], pKV.t[:, 128:256]), reads=[pKV], writes=[kvo[i]])
                    K.dma('sp', kv_out[blk, :], kvo[i].t[:], reads=[kvo[i]])
                    K.op('act', lambda e, i=i, b=b: e.copy(v65.t[:, b, :, 0:64], kvo[i].t[:, 128:256].rearrange("p (n d) -> p n d", d=64)),
                         reads=[kvo[i]], writes=[v65])
                    for n in range(2):
                        K.op('dve', lambda e, i=i, n=n: e.tensor_copy(
                            kdup.t[:, n, :].rearrange("p (r d) -> p r d", d=64),
                            kvo[i].t[:, n * 64:(n + 1) * 64].unsqueeze(1).to_broadcast([128, 2, 64])), reads=[kvo[i]], writes=[kdup])
                    rope(pI.t[:, 0:512].rearrange("p (h d) -> p h d", d=64), qir.t[:].rearrange("p (h d) -> p h d", d=64), 8, cs[i], [pI], [qir])
                    K.op('act', lambda e, b=b: e.activation(btm.t[:, b, :], pSm.t[:, 0:4], AF.Sigmoid), reads=[pSm], writes=[btm])
                    K.op('dve', lambda e: e.tensor_tensor(sm.t[:, 0:4], pSm.t[:, 4:8], prm.t[:, 4:8], ALU.add), reads=[pSm, prm], writes=[sm])
                    K.op('act', lambda e: e.activation(sm.t[:, 4:8], sm.t[:, 0:4], AF.Exp), reads=[sm], writes=[sm])
                    K.op('act', lambda e: e.activation(sm.t[:, 8:12], sm.t[:, 4:8], AF.Ln, bias=1.0), reads=[sm], writes=[sm])
                    K.op('dve', lambda e, b=b: e.tensor_tensor(gtm.t[:, b, :], sm.t[:, 8:12], negA.t[:], ALU.mult), reads=[sm, negA], writes=[gtm])
                    K.op('act', lambda e, b=b: e.activation(wab.t[:, b, :], pSm.t[:, 72:80], AF.Abs, scale=8.0 ** -0.5 / 8.0),
                         reads=[pSm], writes=[wab])
                    K.op('act', lambda e, b=b: e.sign(wsg.t[:, b, :], pSm.t[:, 72:80]), reads=[pSm], writes=[wsg])
                    rope(pSm.t[:, 8:72].rearrange("p (h d) -> p h d", d=64), kio[i].t[:].rearrange("p (h d) -> p h d", d=64), 1, cs[i], [pSm], [kio[i]])
                    K.dma('sp', kidx_out[blk, :], kio[i].t[:], reads=[kio[i]])
                    K.op('dve', lambda e, i=i: e.tensor_copy(kdup.t[:, 2, :].rearrange("p (r d) -> p r d", d=64),
                                                             kio[i].t[:].unsqueeze(1).to_broadcast([128, 2, 64])), reads=[kio[i]], writes=[kdup])
                    for pr in range(4):
                        K.op('pe', lambda e, pr=pr: e.transpose(pTq.t[:, pr, :], qr.t[:, pr * 128:(pr + 1) * 128], identb.t[:]),
                             reads=[qr, identb], writes=[pTq])
                        K.op('pe', lambda e, pr=pr: e.transpose(pTq.t[:, 4 + pr, :], qir.t[:, pr * 128:(pr + 1) * 128], identb.t[:]),
                             reads=[qir, identb], writes=[pTq])
                    K.op('act', lambda e, i=i: e.copy(qst[i].t[:], pTq.t[:]), reads=[pTq], writes=[qst[i]])
                    K.dma('sp', qsT.t.ap()[b], qst[i].t[:], reads=[qst[i]], writes=[qsT])
                    for n in range(3):
                        K.op('pe', lambda e, n=n: e.transpose(pTk.t[:, n, :], kdup.t[:, n, :], identb.t[:]), reads=[kdup, identb], writes=[pTk])
                    K.op('act', lambda e, blk=blk: e.copy(kbT2.t[:, :, blk], pTk.t[:, 0:2, :]), reads=[pTk], writes=[kbT2])
                    K.op('act', lambda e, blk=blk: e.copy(kiT2.t[:, blk], pTk.t[:, 2, :]), reads=[pTk], writes=[kiT2])

            with K.scope():
                uT = K.sb("uT", [128, 3 + TT], BF16)
                dg = K.sb("dg", [128, 12, 4, 128], BF16)
                cwb = K.sb("cwb", [128, 12, 4], F32)
                for ch in range(12):
                    for j in range(4):
                        K.op('dve', lambda e, ch=ch, j=j: e.tensor_scalar(dg.t[:, ch, j, :], C('ident'), cw.t[:, ch, j:j + 1], None, op0=ALU.mult),
                             reads=[cst, cw], writes=[dg])
                K.op('pool', lambda e: e.memset(uT.t[:, 0:3], 0.0), writes=[uT])
                stT = K.sb("stT", [128, 12, 3, 4], F32)
                for s in range(4):
                    for j in range(3):
                        K.dma('sp', stT.t[:, :, j, s], st_conv[s, j].rearrange("(c p) -> p c", p=128), writes=[stT], allow_slow_non_contiguous=True)
                cpre = K.sb("cpre", [128, 12, 3], F32)
                cso = K.sb("cso", [128, 12, 4, 3], F32)
                us = K.sb("us", [128, 4], F32)
                acc4 = K.sb("acc4", [128, 4], F32)
                wfs = [K.sb(f"wfs{i}", [128, 8, 128], F32) for i in range(2)]
                wfb = [K.sb(f"wfb{i}", [128, 8, 128], BF16) for i in range(2)]
                pF = [K.ps(f"pF{i}", [128, 512], F32) for i in range(2)]
                pC = [K.ps(f"pC{i}", [128, 512], F32) for i in range(2)]
                pN = K.ps("pN", [128, 512], F32)
                cv = [K.sb(f"cv{i}", [128, 512], F32) for i in range(2)]
                sq2 = K.sb("sq2", [128, 512], F32)
                rr = K.sb("rr", [128, 512], F32)
                co = [K.sb(f"co{i}", [128, 512], F32) for i in range(2)]
                NT = 9
                chunks = [('conv', c) for c in range(12)] + [('z', h) for h in range(4)] + [('gate', c) for c in range(16)]
                nf = 0
                for ci, (kind, idx) in enumerate(chunks):
                    wi_ = ci % 2
                    col0 = idx * 128 if kind == 'conv' else (1536 + idx * 128 if kind == 'z' else 3408 + idx * 128)
                    K.dma('sp', wfs[wi_].t[:], w_in[:, col0:col0 + 128].rearrange("(k p) c -> p k c", p=128), writes=[wfs[wi_]])
                    K.op('pool', lambda e, wi_=wi_: e.tensor_copy(wfb[wi_].t[:], wfs[wi_].t[:]), reads=[wfs[wi_]], writes=[wfb[wi_]])
                    for t in range(NT):
                        w = 512 if t < 8 else 128
                        tok = slice(t * 512, t * 512 + w)
                        pf = pF[nf % 2]
                        nf += 1
                        for k in range(8):
                            K.op('pe', lambda e, pf=pf, k=k, tok=tok, w=w, wi_=wi_: e.matmul(
                                pf.t[:, 0:w], wfb[wi_].t[:, k, :], xnT.t[:, k, tok], start=(k == 0), stop=(k == 7)),
                                reads=[wfb[wi_], xnT], writes=[pf])
                        if kind == 'conv':
                            K.op('act', lambda e, pf=pf, t=t, w=w: e.copy(uT.t[:, 3 + t * 512:3 + t * 512 + w], pf.t[:, 0:w]), reads=[pf], writes=[uT])
                            if t == 7:
                                K.op('dve', lambda e, pf=pf, idx=idx: e.tensor_copy(cpre.t[:, idx, :], pf.t[:, 509:512]), reads=[pf], writes=[cpre])
                            if t == 8:
                                K.op('dve', lambda e, pf=pf: e.tensor_copy(us.t[:], pf.t[:, 0:4]), reads=[pf], writes=[us])
                        elif kind == 'z':
                            o = co[nf % 2]
                            K.op('act', lambda e, pf=pf, o=o, w=w: e.activation(o.t[:, 0:w], pf.t[:, 0:w], AF.Silu), reads=[pf], writes=[o])
                            K.dma('sp', zT.t.ap()[idx, :, tok], o.t[:, 0:w], reads=[o], writes=[zT])
                        else:
                            o = co[nf % 2]
                            K.op('act', lambda e, pf=pf, o=o, w=w: e.activation(o.t[:, 0:w], pf.t[:, 0:w], AF.Sigmoid), reads=[pf], writes=[o])
                            K.dma('sp', sgT.t.ap()[idx, :, tok], o.t[:, 0:w], reads=[o], writes=[sgT])
                    if kind != 'conv':
                        continue
                    ch = idx
                    isqk = ch < 8
                    scl = 128.0 if ch < 4 else 1.0

                    def finish_tile(src, w, tok, ch=ch, isqk=isqk, scl=scl):
                        if not isqk:
                            K.dma('sp', cT.t.ap()[ch, :, tok], src.t[:, 0:w], reads=[src], writes=[cT])
                            return
                        K.op('dve', lambda e: e.tensor_tensor(sq2.t[:, 0:w], src.t[:, 0:w], src.t[:, 0:w], ALU.mult), reads=[src], writes=[sq2])
                        K.op('pe', lambda e: e.matmul(pN.t[:, 0:w], C('ones'), sq2.t[:, 0:w], start=True, stop=True), reads=[cst, sq2], writes=[pN])
                        K.op('act', lambda e: e.activation(rr.t[:, 0:w], pN.t[:, 0:w], AF.Sqrt, bias=EPS * scl, scale=scl), reads=[pN], writes=[rr])
                        K.op('dve', lambda e: e.reciprocal(rr.t[:, 0:w], rr.t[:, 0:w]), reads=[rr], writes=[rr])
                        o = co[finish_tile.n % 2]
                        finish_tile.n += 1
                        K.op('dve', lambda e: e.tensor_tensor(o.t[:, 0:w], src.t[:, 0:w], rr.t[:, 0:w], ALU.mult), reads=[src, rr], writes=[o])
                        K.dma('sp', cT.t.ap()[ch, :, tok], o.t[:, 0:w], reads=[o], writes=[cT])
                    finish_tile.n = 0
                    for t in range(8):
                        pc = pC[t % 2]
                        for j in range(4):
                            K.op('pe', lambda e, pc=pc, j=j, t=t, ch=ch: e.matmul(
                                pc.t[:], dg.t[:, ch, j, :], uT.t[:, t * 512 + j:t * 512 + j + 512], start=(j == 0), stop=(j == 3)),
                                reads=[dg, uT], writes=[pc])
                        c_ = cv[t % 2]
                        K.op('act', lambda e, pc=pc, c_=c_: e.activation(c_.t[:], pc.t[:], AF.Silu), reads=[pc], writes=[c_])
                        finish_tile(c_, 512, slice(t * 512, (t + 1) * 512))
                    K.op('dve', lambda e, ch=ch: e.tensor_scalar(acc4.t[:], stT.t[:, ch, 0, :], cw.t[:, ch, 0:1], None, op0=ALU.mult),
                         reads=[stT, cw], writes=[acc4])
                    for j in (1, 2):
                        K.op('dve', lambda e, ch=ch, j=j: e.scalar_tensor_tensor(acc4.t[:], stT.t[:, ch, j, :], cw.t[:, ch, j:j + 1], acc4.t[:],
                                                                               op0=ALU.mult, op1=ALU.add), reads=[stT, cw, acc4], writes=[acc4])
                    K.op('dve', lambda e, ch=ch: e.scalar_tensor_tensor(acc4.t[:], us.t[:], cw.t[:, ch, 3:4], acc4.t[:], op0=ALU.mult, op1=ALU.add),
                         reads=[us, cw, acc4], writes=[acc4])
                    c_ = cv[0]
                    K.op('act', lambda e, c_=c_: e.activation(c_.t[:, 0:4], acc4.t[:], AF.Silu), reads=[acc4], writes=[c_])
                    finish_tile(c_, 4, slice(T, T + 4))
                    K.op('dve', lambda e, ch=ch: e.tensor_copy(cso.t[:, ch, :, 0:2], stT.t[:, ch, 1:3, :].rearrange("p j s -> p s j")),
                         reads=[stT], writes=[cso])
                    K.op('dve', lambda e, ch=ch: e.tensor_copy(cso.t[:, ch, :, 2], us.t[:]), reads=[us], writes=[cso])
                for r in range(3):
                    K.dma('sp', convp_out[r].rearrange("(c p) -> p c", p=128), cpre.t[:, :, r], reads=[cpre], allow_slow_non_contiguous=True)
                for s in range(4):
                    for r in range(3):
                        K.dma('sp', convs_out[s, r].rearrange("(c p) -> p c", p=128), cso.t[:, :, s, r], reads=[cso], allow_slow_non_contiguous=True)

        if not P1_ENABLE:
            ini = K.sb("ini", [128, 12, 128], F32)
            K.dma('sp', ini.t[:, 0:8, :].rearrange("p c t -> p (c t)"), x_in[0:128, :], writes=[ini])
            K.dma('sp', ini.t[:, 8:12, :].rearrange("p c t -> p (c t)"), x_in[256:384, 0:512], writes=[ini])
            gi = K.sb("gi", [128, NB, 4], F32)
            K.dma('sp', gi.t[:].rearrange("p b h -> p (b h)"), x_in[128:256, 0:NB * 4], writes=[gi])
            K.op('dve', lambda e: e.tensor_tensor(gtm.t[:], gi.t[:], gi.t[:], ALU.mult), reads=[gi], writes=[gtm])
            K.op('dve', lambda e: e.tensor_scalar(gtm.t[:], gtm.t[:], -1.0, None, op0=ALU.mult), reads=[gtm], writes=[gtm])
            K.op('act', lambda e: e.activation(btm.t[:], gi.t[:], AF.Sigmoid), reads=[gi], writes=[btm])
            for b in range(NB):
                K.dma('sp', cT.t.ap()[:, :, b * 128:(b + 1) * 128].rearrange("c p t -> p c t"), ini.t[:], reads=[ini], writes=[cT])
                K.dma('sp', zT.t.ap()[:, :, b * 128:(b + 1) * 128].rearrange("c p t -> p c t"), ini.t[:, 0:4, :], reads=[ini], writes=[zT])
            K.op('pool', lambda e: e.memset(kbT2.t[:], 0.01), writes=[kbT2])
            K.op('pool', lambda e: e.memset(kiT2.t[:], 0.01), writes=[kiT2])
            K.op('pool', lambda e: e.memset(wab.t[:], 0.3), writes=[wab])
            K.op('pool', lambda e: e.memset(wsg.t[:], 1.0), writes=[wsg])
            qini = K.sb("qini", [128, 8, 128], BF16)
            K.op('pool', lambda e: e.memset(qini.t[:], 0.02), writes=[qini])
            for b in range(NB):
                K.dma('sp', qsT.t.ap()[b], qini.t[:], reads=[qini], writes=[qsT])
            K.barrier()
        oaT = K.dram("oaT_s", [4, 128, TT], BF16, kind=("ExternalOutput" if DEBUG_OUT else "Internal"))
        with K.scope():
            bk = [K.ps(f"bk{i}", [128, 4, 128], F32) for i in range(8)]
            identf = C('ident')
            onesf = C('ones')
            gcol = K.sb("gcol", [128, 1], F32)
            K.dma('sp', gcol.t[:], gdn_g.rearrange("(p o) -> p o", o=1), writes=[gcol], allow_slow_non_contiguous=True)
            qkv = [K.sb(f"qkv{i}", [128, 12, 128], F32) for i in range(2)]
            zt = [K.sb(f"zt{i}", [128, 4, 128], F32) for i in range(2)]
            sc = K.sb("sc", [128, 32], F32)
            gU = K.sb("gU", [128, 4, 128], F32)
            bI = K.sb("bI", [128, 4, 128], F32)
            Erow = K.sb("Erow", [128, 4, 128], F32)
            decT = K.sb("decT", [128, 4, 128], F32)
            dec = K.sb("dec", [128, 4, 128], F32)
            WT = K.sb("WT", [128, 4, 128], F32)
            QW = K.sb("QW", [128, 4, 128], F32)
            QKm = K.sb("QKm", [128, 4, 128], F32)
            Pm = [K.sb(f"Pm{i}", [128, 4, 128], F32) for i in range(2)]
            PT = [K.sb(f"PT{i}", [128, 4, 128], F32) for i in range(2)]
            XT = [K.sb(f"XT{i}", [128, 4, 128], F32) for i in range(2)]
            kbg = K.sb("kbg", [128, 4, 128], F32)
            kd0 = K.sb("kd0", [128, 4, 128], F32)
            kd1 = K.sb("kd1", [128, 4, 128], F32)
            vb = K.sb("vb", [128, 4, 128], F32)
            u_sb = K.sb("u_sb", [128, 4, 128], F32)
            wT_sb = K.sb("wT_sb", [128, 4, 128], F32)
            qd0 = K.sb("qd0", [128, 4, 128], F32)
            qd1 = K.sb("qd1", [128, 4, 128], F32)
            vn = K.sb("vn", [128, 4, 128], F32)
            Sx = [K.sb(f"Sx{i}", [128, 4, 128], F32) for i in range(3)]
            sqo = K.sb("sqo", [128, 4, 128], F32)
            rs = K.sb("rs", [128, 4, 128], F32)
            oa = [K.sb(f"oa{i}", [128, 4, 128], BF16) for i in range(2)]
            K.op('pool', lambda e: e.memset(vn.t[:], 0.0), writes=[vn])
            K.op('pool', lambda e: e.memset(Sx[0].t[:], 0.0), writes=[Sx[0]])

            def bc_h(ap2):
                return ap2.unsqueeze(1).to_broadcast([128, 4, 128])

            def bc_s(ap4):
                return ap4.unsqueeze(2).to_broadcast([128, 4, 128])

            def flat(bufAP):
                return bufAP.rearrange("p h j -> p (h j)")

            def gated_norm(pO_ap, zt_ap, out_ap, w, pObuf, ztbuf, outbuf, pSSbuf):
                K.op('act', lambda e: e.activation(flat(sqo.t[:])[:, 0:w], pO_ap, AF.Square), reads=[pObuf], writes=[sqo])
                K.op('pe', lambda e: e.matmul(flat(pSSbuf.t[:])[:, 0:w], onesf, flat(sqo.t[:])[:, 0:w], start=True, stop=True),
                     reads=[cst, sqo], writes=[pSSbuf])
                K.op('act', lambda e: e.activation(flat(rs.t[:])[:, 0:w], flat(pSSbuf.t[:])[:, 0:w], AF.Ln, bias=EPS, scale=1.0 / 128),
                     reads=[pSSbuf], writes=[rs])
                K.op('act', lambda e: e.activation(flat(rs.t[:])[:, 0:w], flat(rs.t[:])[:, 0:w], AF.Exp, scale=-0.5), reads=[rs], writes=[rs])
                K.op('dve', lambda e: e.scalar_tensor_tensor(flat(rs.t[:])[:, 0:w], pO_ap, gcol.t[:, 0:1], flat(rs.t[:])[:, 0:w], op0=ALU.mult, op1=ALU.mult),
                     reads=[pObuf, gcol, rs], writes=[rs])
                K.op('dve', lambda e: e.tensor_tensor(out_ap, flat(rs.t[:])[:, 0:w], zt_ap, ALU.mult), reads=[rs, ztbuf], writes=[outbuf])

            def gdn_block(b):
                i = b % 2
                blk = slice(b * 128, (b + 1) * 128)
                K.dma('sp', qkv[i].t[:], cT.t.ap()[:, :, blk].rearrange("c p t -> p c t"), reads=[cT], writes=[qkv[i]])
                K.dma('sp', zt[i].t[:], zT.t.ap()[:, :, blk].rearrange("c p t -> p c t"), reads=[zT], writes=[zt[i]])
                Q, Kk, V = qkv[i].t[:, 0:4, :], qkv[i].t[:, 4:8, :], qkv[i].t[:, 8:12, :]
                g_ap = gtm.t[:, b, :]
                be_ap = btm.t[:, b, :]
                S0, S1, S2 = Sx[(2 * b) % 3], Sx[(2 * b + 1) % 3], Sx[(2 * b + 2) % 3]
                if P2_STAGE < -3:
                    return
                for n_, nm in enumerate(('U', 'BD', 'CM0', 'CM1')):
                    K.op('pe', lambda e, n_=n_, nm=nm: e.matmul(bk[0].t[:, 0, n_ * 4:(n_ + 1) * 4], C(nm), g_ap, start=True, stop=True),
                         reads=[cst, gtm], writes=[bk[0]])
                p1 = bk[0].t[:, 0, :]
                K.op('dve', lambda e: e.tensor_copy(sc.t[:, 0:4], p1[:, 0:4]), reads=[bk[0]], writes=[sc])
                K.op('act', lambda e: e.activation(sc.t[:, 4:8], p1[:, 0:4], AF.Exp), reads=[bk[0]], writes=[sc])
                K.op('dve', lambda e: e.tensor_tensor(sc.t[:, 8:12], p1[:, 4:8], sc.t[:, 0:4], ALU.subtract), reads=[bk[0], sc], writes=[sc])
                K.op('act', lambda e: e.activation(sc.t[:, 8:12], sc.t[:, 8:12], AF.Exp), reads=[sc], writes=[sc])
                K.op('act', lambda e: e.activation(sc.t[:, 12:20], p1[:, 8:16], AF.Exp), reads=[bk[0]], writes=[sc])
                K.op('dve', lambda e: e.tensor_scalar(sc.t[:, 20:24], be_ap, -1.0, None, op0=ALU.mult), reads=[btm], writes=[sc])
                K.op('dve', lambda e: e.tensor_tensor(sc.t[:, 24:28], be_ap, sc.t[:, 4:8], ALU.mult), reads=[btm, sc], writes=[sc])
                if P2_STAGE < -2:
                    return
                K.op('dve', lambda e: e.tensor_tensor(gU.t[:], bc_h(C('U')), bc_s(g_ap), ALU.mult), reads=[cst, gtm], writes=[gU])
                K.op('dve', lambda e: e.tensor_tensor(bI.t[:], bc_h(identf), bc_s(be_ap), ALU.mult), reads=[cst, btm], writes=[bI])
                if P2_STAGE < -1.5:
                    return
                K.op('pe', lambda e: e.matmul(flat(bk[1].t[:]), onesf, flat(gU.t[:]), start=True, stop=True), reads=[cst, gU], writes=[bk[1]])
                if P2_STAGE < -1.25:
                    return
                if DBG == 5:
                    K.op('pe', lambda e: e.matmul(flat(bk[1].t[:]), onesf, flat(gU.t[:]), start=True, stop=True), reads=[cst, gU], writes=[bk[1]])
                elif DBG == 6:
                    K.op('pe', lambda e: e.matmul(bk[0].t[:, 1, 0:4], C('U'), g_ap, start=True, stop=True), reads=[cst, gtm], writes=[bk[0]])
                elif DBG == 2:
                    K.op('pe', lambda e: e.matmul(flat(bk[5].t[:]), onesf, flat(bI.t[:]), start=True, stop=True), reads=[cst, bI, gU], writes=[bk[5]])
                elif DBG == 3:
                    K.op('pe', lambda e: e.matmul(bk[2].t[:, 0, :], onesf, bI.t[:, 0, :], start=True, stop=True), reads=[cst, bI, gU], writes=[bk[2]])
                elif DBG == 4:
                    K.op('pe', lambda e: e.matmul(flat(bk[2].t[:]), C('U'), flat(bI.t[:]), start=True, stop=True), reads=[cst, bI, gU], writes=[bk[2]])
                else:
                    K.op('pe', lambda e: e.matmul(flat(bk[2].t[:]), onesf, flat(bI.t[:]), start=True, stop=True), reads=[cst, bI, gU], writes=[bk[2]])
                if P2_STAGE < -1:
                    return
                if P2_STAGE < -0.9:
                    return
                K.op('act', lambda e: e.activation(Erow.t[:], bk[1].t[:], AF.Exp), reads=[bk[1]], writes=[Erow])
                if P2_STAGE < -0.8:
                    return
                for h in range(4):
                    K.op('dve', lambda e, h=h: e.tensor_scalar(decT.t[:, h, :], bk[1].t[:, h, :], sc.t[:, h:h + 1], 0.0, op0=ALU.subtract, op1=ALU.min),
                         reads=[bk[1], sc], writes=[decT])
                    K.op('dve', lambda e, h=h: e.tensor_scalar(dec.t[:, h, :], bk[1].t[:, h, :], sc.t[:, h:h + 1], 0.0, op0=ALU.subtract, op1=ALU.max),
                         reads=[bk[1], sc], writes=[dec])
                if P2_STAGE < -0.7:
                    return
                K.op('act', lambda e: e.activation(decT.t[:], decT.t[:], AF.Exp), reads=[decT], writes=[decT])
                K.op('act', lambda e: e.activation(dec.t[:], dec.t[:], AF.Exp, scale=-1.0), reads=[dec], writes=[dec])
                if P2_STAGE < -0.6:
                    return
                K.op('dve', lambda e: e.tensor_tensor(WT.t[:], decT.t[:], bc_h(C('MsT')), ALU.mult), reads=[decT, cst], writes=[WT])
                K.op('dve', lambda e: e.tensor_tensor(WT.t[:], WT.t[:], bk[2].t[:], ALU.mult), reads=[WT, bk[2]], writes=[WT])
                if P2_STAGE < -0.5:
                    return
                K.op('dve', lambda e: e.tensor_tensor(QW.t[:], decT.t[:], bc_h(C('MiT')), ALU.mult), reads=[decT, cst], writes=[QW])
                K.op('dve', lambda e: e.tensor_tensor(dec.t[:], dec.t[:], bc_h(C('Ms')), ALU.mult), reads=[dec, cst], writes=[dec])
                if P2_STAGE < 1:
                    return
                for h in range(4):
                    K.op('pe', lambda e, h=h: e.matmul(bk[3].t[:, h, :], Kk[:, h, :], Kk[:, h, :], start=True, stop=True), reads=[qkv[i]], writes=[bk[3]])
                    K.op('pe', lambda e, h=h: e.matmul(bk[4].t[:, h, :], Kk[:, h, :], Q[:, h, :], start=True, stop=True), reads=[qkv[i]], writes=[bk[4]])
                P0, PT0, X0 = Pm[0], PT[0], XT[0]
                for h in range(4):
                    K.op('dve', lambda e, h=h: e.scalar_tensor_tensor(P0.t[:, h, :], bk[3].t[:, h, :], sc.t[:, 20 + h:21 + h], dec.t[:, h, :], op0=ALU.mult, op1=ALU.mult),
                         reads=[bk[3], sc, dec], writes=[P0])
                K.op('dve', lambda e: e.scalar_tensor_tensor(PT0.t[:], bk[3].t[:], -1.0, WT.t[:], op0=ALU.mult, op1=ALU.mult), reads=[bk[3], WT], writes=[PT0])
                K.op('dve', lambda e: e.tensor_tensor(QKm.t[:], bk[4].t[:], QW.t[:], ALU.mult), reads=[bk[4], QW], writes=[QKm])
                K.op('dve', lambda e: e.tensor_tensor(X0.t[:], PT0.t[:], bc_h(identf), ALU.add), reads=[PT0, cst], writes=[X0])
                if P2_STAGE < 2:
                    return
                cur = 0
                for lvl in range(6):
                    Pc, PTc, Xc = Pm[cur], PT[cur], XT[cur]
                    Pn, PTn, Xn = Pm[1 - cur], PT[1 - cur], XT[1 - cur]
                    last = (lvl == 5)
                    for h in range(4):
                        if not last:
                            K.op('pe', lambda e, h=h, Pc=Pc, PTc=PTc: e.matmul(bk[0].t[:, h, :], PTc.t[:, h, :], Pc.t[:, h, :], start=True, stop=True),
                                 reads=[Pc, PTc], writes=[bk[0]])
                            K.op('pe', lambda e, h=h, Pc=Pc, PTc=PTc: e.matmul(bk[1].t[:, h, :], Pc.t[:, h, :], PTc.t[:, h, :], start=True, stop=True),
                                 reads=[Pc, PTc], writes=[bk[1]])
                        if lvl > 0:
                            K.op('pe', lambda e, h=h, Pc=Pc, Xc=Xc: e.matmul(bk[2].t[:, h, :], Pc.t[:, h, :], Xc.t[:, h, :], start=True, stop=True),
                                 reads=[Pc, Xc], writes=[bk[2]])
                    if not last:
                        K.op('act', lambda e, Pn=Pn: e.copy(Pn.t[:], bk[0].t[:]), reads=[bk[0]], writes=[Pn])
                        K.op('dve', lambda e, PTn=PTn: e.tensor_copy(PTn.t[:], bk[1].t[:]), reads=[bk[1]], writes=[PTn])
                    if lvl > 0:
                        K.op('dve', lambda e, Xc=Xc, Xn=Xn: e.tensor_tensor(Xn.t[:], Xc.t[:], bk[2].t[:], ALU.add), reads=[Xc, bk[2]], writes=[Xn])
                    else:
                        K.op('pool', lambda e, Xc=Xc, Xn=Xn: e.tensor_copy(Xn.t[:], Xc.t[:]), reads=[Xc], writes=[Xn])
                    cur = 1 - cur
                Xf = XT[cur]
                if P2_STAGE < 3:
                    return
                for h in range(4):
                    K.op('pe', lambda e, h=h: e.transpose(bk[3].t[:, h, :], Kk[:, h, :], identf), reads=[qkv[i], cst], writes=[bk[3]])
                    K.op('pe', lambda e, h=h: e.transpose(bk[4].t[:, h, :], V[:, h, :], identf), reads=[qkv[i], cst], writes=[bk[4]])
                K.op('dve', lambda e: e.tensor_tensor(kbg.t[:], bk[3].t[:], bc_s(sc.t[:, 24:28]), ALU.mult), reads=[bk[3], sc], writes=[kbg])
                K.op('dve', lambda e: e.tensor_tensor(kd1.t[:], bk[3].t[:], bc_s(sc.t[:, 8:12]), ALU.mult), reads=[bk[3], sc], writes=[kd1])
                K.op('dve', lambda e: e.tensor_scalar(flat(kd0.t[:]), flat(kd1.t[:]), C('CM0')[:, 0:1], None, op0=ALU.mult), reads=[kd1, cst], writes=[kd0])
                K.op('dve', lambda e: e.tensor_scalar(flat(kd1.t[:]), flat(kd1.t[:]), C('CM1')[:, 0:1], None, op0=ALU.mult), reads=[kd1, cst], writes=[kd1])
                K.op('dve', lambda e: e.tensor_tensor(vb.t[:], bk[4].t[:], bc_s(be_ap), ALU.mult), reads=[bk[4], btm], writes=[vb])
                if P2_STAGE < 4:
                    return
                for h in range(4):
                    K.op('pe', lambda e, h=h: e.matmul(bk[5].t[:, h, :], Xf.t[:, h, :], vb.t[:, h, :], start=True, stop=True), reads=[Xf, vb], writes=[bk[5]])
                    K.op('pe', lambda e, h=h: e.matmul(bk[6].t[:, h, :], kbg.t[:, h, :], Xf.t[:, h, :], start=True, stop=True), reads=[Xf, kbg], writes=[bk[6]])
                K.op('act', lambda e: e.copy(u_sb.t[:], bk[5].t[:]), reads=[bk[5]], writes=[u_sb])
                K.op('act', lambda e: e.copy(wT_sb.t[:], bk[6].t[:]), reads=[bk[6]], writes=[wT_sb])
                K.op('dve', lambda e: e.tensor_tensor(qd1.t[:], Q, Erow.t[:], ALU.mult), reads=[qkv[i], Erow], writes=[qd1])
                K.op('dve', lambda e: e.tensor_tensor(qd0.t[:], qd1.t[:], bc_h(C('C0')), ALU.mult), reads=[qd1, cst], writes=[qd0])
                K.op('dve', lambda e: e.tensor_tensor(qd1.t[:], qd1.t[:], bc_h(C('C1')), ALU.mult), reads=[qd1, cst], writes=[qd1])
                if P2_STAGE < 5:
                    return
                for h in range(4):
                    K.op('pe', lambda e, h=h: e.matmul(bk[3].t[:, h, :], wT_sb.t[:, h, :], S0.t[:, h, :], start=True, stop=True), reads=[wT_sb, S0], writes=[bk[3]])
                K.op('dve', lambda e: e.tensor_tensor(vn.t[0:64], u_sb.t[0:64], bk[3].t[0:64], ALU.subtract), reads=[u_sb, bk[3]], writes=[vn])
                if DBG == 9 and b == P2_BLOCKS - 1:
                    K.op('act', lambda e: e.copy(sqo.t[:], bk[3].t[:]), reads=[bk[3]], writes=[sqo])
                    K.dma('sp', ssmp_out.rearrange("h d e -> d h e"), sqo.t[:], reads=[sqo])
                if DBG == 10 and b == P2_BLOCKS - 1:
                    K.dma('sp', ssmp_out.rearrange("h d e -> d h e"), wT_sb.t[:], reads=[wT_sb])
                if DBG == 11 and b == P2_BLOCKS - 1:
                    K.dma('sp', ssmp_out.rearrange("h d e -> d h e"), S0.t[:], reads=[S0])
                for h in range(4):
                    K.op('pe', lambda e, h=h: e.matmul(bk[4].t[:, h, :], kd0.t[:, h, :], vn.t[:, h, :], start=True, stop=True), reads=[kd0, vn], writes=[bk[4]])
                K.op('dve', lambda e: e.tensor_tensor(S1.t[:], S0.t[:], bc_s(sc.t[:, 12:16]), ALU.mult), reads=[S0, sc], writes=[S1])
                K.op('dve', lambda e: e.tensor_tensor(S1.t[:], S1.t[:], bk[4].t[:], ALU.add), reads=[S1, bk[4]], writes=[S1])
                for h in range(4):
                    K.op('pe', lambda e, h=h: e.matmul(bk[3].t[:, h, :], wT_sb.t[:, h, :], S1.t[:, h, :], start=True, stop=True), reads=[wT_sb, S1], writes=[bk[3]])
                K.op('dve', lambda e: e.tensor_tensor(vn.t[64:128], u_sb.t[64:128], bk[3].t[64:128], ALU.subtract), reads=[u_sb, bk[3]], writes=[vn])
                for h in range(4):
                    K.op('pe', lambda e, h=h: e.matmul(bk[4].t[:, h, :], kd1.t[:, h, :], vn.t[:, h, :], start=True, stop=True), reads=[kd1, vn], writes=[bk[4]])
                K.op('dve', lambda e: e.tensor_tensor(S2.t[:], S1.t[:], bc_s(sc.t[:, 16:20]), ALU.mult), reads=[S1, sc], writes=[S2])
                K.op('dve', lambda e: e.tensor_tensor(S2.t[:], S2.t[:], bk[4].t[:], ALU.add), reads=[S2, bk[4]], writes=[S2])
                for h in range(4):
                    K.op('pe', lambda e, h=h: e.matmul(bk[5].t[:, h, :], S0.t[:, h, :], qd0.t[:, h, :], start=True, stop=False), reads=[S0, qd0], writes=[bk[5]])
                    K.op('pe', lambda e, h=h: e.matmul(bk[5].t[:, h, :], S1.t[:, h, :], qd1.t[:, h, :], start=False, stop=False), reads=[S1, qd1], writes=[bk[5]])
                    K.op('pe', lambda e, h=h: e.matmul(bk[5].t[:, h, :], vn.t[:, h, :], QKm.t[:, h, :], start=False, stop=True), reads=[vn, QKm], writes=[bk[5]])
                if P2_STAGE < 6:
                    return
                gated_norm(flat(bk[5].t[:]), flat(zt[i].t[:]), flat(oa[i].t[:]), 512, bk[5], zt[i], oa[i], bk[6])
                K.dma('sp', oaT.t.ap()[:, :, blk].rearrange("c p t -> p c t"), oa[i].t[:], reads=[oa[i]], writes=[oaT])
            for b_ in range(P2_BLOCKS):
                gdn_block(b_)
            Sfin = Sx[(2 * (P2_BLOCKS - 1) + (1 if DBG == 8 else 2)) % 3]
            if DBG not in (9, 10, 11):
                K.dma('sp', ssmp_out.rearrange("h d e -> d h e"), Sfin.t[:], reads=[Sfin])

            if P2_SAMPLE:
                qs = K.sb("qs", [128, 12, 4], F32)
                zs = K.sb("zs", [128, 4, 4], F32)
                K.dma('sp', qs.t[:], cT.t.ap()[:, :, T:T + 4].rearrange("c p t -> p c t"), reads=[cT], writes=[qs], allow_slow_non_contiguous=True)
                K.dma('sp', zs.t[:], zT.t.ap()[:, :, T:T + 4].rearrange("c p t -> p c t"), reads=[zT], writes=[zs], allow_slow_non_contiguous=True)
                Sin = K.sb("Sin", [128, 16, 128], F32)
                Sn = K.sb("Sn", [128, 16, 128], F32)
                K.dma('sp', Sin.t[:], st_ssm.rearrange("s h d e -> d (s h) e"), writes=[Sin])
                Dg = K.sb("Dg", [128, 4, 8], F32)
                i4 = identf[:, 0:4].unsqueeze(2).to_broadcast([128, 4, 4])
                K.op('dve', lambda e: e.tensor_tensor(Dg.t[:, :, 0:4], gtm.t[:, 32, :].unsqueeze(1).to_broadcast([128, 4, 4]), i4, ALU.mult), reads=[gtm, cst], writes=[Dg])
                K.op('dve', lambda e: e.tensor_tensor(Dg.t[:, :, 4:8], btm.t[:, 32, :].unsqueeze(1).to_broadcast([128, 4, 4]), i4, ALU.mult), reads=[btm, cst], writes=[Dg])
                K.op('pe', lambda e: e.matmul(bk[0].t[:, 0, 0:32], onesf, Dg.t[:].rearrange("p s c -> p (s c)"), start=True, stop=True), reads=[cst, Dg], writes=[bk[0]])
                bcs = K.sb("bcs", [128, 4, 8], F32)
                K.op('dve', lambda e: e.tensor_copy(bcs.t[:].rearrange("p s c -> p (s c)"), bk[0].t[:, 0, 0:32]), reads=[bk[0]], writes=[bcs])
                K.op('act', lambda e: e.activation(bcs.t[:, :, 0:4], bcs.t[:, :, 0:4], AF.Exp), reads=[bcs], writes=[bcs])
                Sp = K.sb("Sp", [128, 128], F32)
                dgm = K.sb("dgm", [128, 128], F32)
                dgk = K.sb("dgk", [128, 128], F32)
                krb = K.sb("krb", [128, 128], F32)
                vcol = K.sb("vcol", [128, 2], F32)
                for s in range(4):
                    for h in range(4):
                        sh = s * 4 + h
                        qc, kc, vc = qs.t[:, h, s:s + 1], qs.t[:, 4 + h, s:s + 1], qs.t[:, 8 + h, s:s + 1]
                        K.op('dve', lambda e, sh=sh, s=s, h=h: e.tensor_scalar(Sp.t[:], Sin.t[:, sh, :], bcs.t[:, s, h:h + 1], None, op0=ALU.mult),
                             reads=[Sin, bcs], writes=[Sp])
                        K.op('pe', lambda e, kc=kc: e.matmul(bk[1].t[:, 0, 0:1], Sp.t[:], kc, start=True, stop=True), reads=[Sp, qs], writes=[bk[1]])
                        K.op('dve', lambda e, vc=vc: e.tensor_tensor(vcol.t[:, 0:1], vc, bk[1].t[:, 0, 0:1], ALU.subtract), reads=[qs, bk[1]], writes=[vcol])
                        K.op('dve', lambda e, s=s, h=h: e.tensor_tensor(vcol.t[:, 1:2], vcol.t[:, 0:1], bcs.t[:, s, 4 + h:5 + h], ALU.mult), reads=[vcol, bcs], writes=[vcol])
                        K.op('dve', lambda e: e.tensor_scalar(dgm.t[:], identf, vcol.t[:, 1:2], None, op0=ALU.mult), reads=[cst, vcol], writes=[dgm])
                        K.op('dve', lambda e, kc=kc: e.tensor_scalar(dgk.t[:], identf, kc, None, op0=ALU.mult), reads=[cst, qs], writes=[dgk])
                        K.op('pe', lambda e: e.matmul(bk[2].t[:, 0, :], onesf, dgk.t[:], start=True, stop=True), reads=[cst, dgk], writes=[bk[2]])
                        K.op('act', lambda e: e.copy(krb.t[:], bk[2].t[:, 0, :]), reads=[bk[2]], writes=[krb])
                        K.op('pe', lambda e: e.matmul(bk[3].t[:, 0, :], krb.t[:], dgm.t[:], start=True, stop=True), reads=[krb, dgm], writes=[bk[3]])
                        K.op('dve', lambda e, sh=sh: e.tensor_tensor(Sn.t[:, sh, :], Sp.t[:], bk[3].t[:, 0, :], ALU.add), reads=[Sp, bk[3]], writes=[Sn])
                        K.op('pe', lambda e, sh=sh, s=s, h=h, qc=qc: e.matmul(bk[5].t[:, h, s:s + 1], Sn.t[:, sh, :], qc, start=True, stop=True),
                             reads=[Sn, qs], writes=[bk[5]])
                K.dma('sp', ssms_out.rearrange("s h d e -> d (s h) e"), Sn.t[:], reads=[Sn])
                oas = K.sb("oas", [128, 4, 4], BF16)
                pO4 = K.sb("pO4", [128, 4, 4], F32)
                K.op('dve', lambda e: e.tensor_copy(pO4.t[:], bk[5].t[:, :, 0:4]), reads=[bk[5]], writes=[pO4])
                p16 = pO4.t[:].rearrange("p h s -> p (h s)")
                K.op('act', lambda e: e.activation(flat(sqo.t[:])[:, 0:16], p16, AF.Square), reads=[pO4], writes=[sqo])
                K.op('pe', lambda e: e.matmul(flat(bk[6].t[:])[:, 0:16], onesf, flat(sqo.t[:])[:, 0:16], start=True, stop=True), reads=[cst, sqo], writes=[bk[6]])
                K.op('act', lambda e: e.activation(flat(rs.t[:])[:, 0:16], flat(bk[6].t[:])[:, 0:16], AF.Sqrt, bias=EPS, scale=1.0 / 128), reads=[bk[6]], writes=[rs])
                K.op('dve', lambda e: e.reciprocal(flat(rs.t[:])[:, 0:16], flat(rs.t[:])[:, 0:16]), reads=[rs], writes=[rs])
                K.op('dve', lambda e: e.scalar_tensor_tensor(flat(rs.t[:])[:, 0:16], p16, gcol.t[:, 0:1], flat(rs.t[:])[:, 0:16], op0=ALU.mult, op1=ALU.mult),
                     reads=[pO4, gcol, rs], writes=[rs])
                K.op('dve', lambda e: e.tensor_tensor(oas.t[:].rearrange("p h s -> p (h s)"), flat(rs.t[:])[:, 0:16], zs.t[:].rearrange("p h s -> p (h s)"), ALU.mult),
                     reads=[rs, zs], writes=[oas])
                K.dma('sp', oaT.t.ap()[:, :, T:T + 4].rearrange("c p t -> p c t"), oas.t[:], reads=[oas], writes=[oaT], allow_slow_non_contiguous=True)

        obT = K.dram("obT_s", [4, 128, TT], BF16, kind=("ExternalOutput" if DEBUG_OUT else "Internal"))
        with K.scope():
            NIT = 24
            qT = [K.sb(f"qT{i}", [128, 8, 128], BF16) for i in range(3)]
            score = [K.sb(f"score{i}", [128, T], F32) for i in range(2)]
            cjunk = K.sb("cjunk", [128, T], F32)
            maskq = [K.sb(f"maskq{i}", [128, T], BF16) for i in range(2)]
            maskT = [K.sb(f"maskT{i}", [128, 32, 128], BF16) for i in range(2)]
            rbuf = [K.sb(f"rbuf{i}", [128, 512], BF16) for i in range(4)]
            dsg = K.sb("dsg", [128, 8, 128], BF16)
            bs = K.sb("bs", [128, 16], F32)
            fill = K.sb("fill", [128, 2], F32)
            K.op('pool', lambda e: e.memset(fill.t[:], 0.0), writes=[fill])
            deltas = K.sb("deltas", [128, NIT + 1], F32)
            cvec = K.sb("cvec", [128, NIT + 1], F32)
            for k in range(NIT + 1):
                K.op('pool', lambda e, k=k: e.memset(cvec.t[:, k:k + 1], 2.0 ** -(k + 1)), writes=[cvec])
            E_ = [K.sb(f"E_{i}", [128, 4, 128], BF16) for i in range(2)]
            Pm_ = [K.sb(f"Pm_{i}", [128, 4, 128], BF16) for i in range(2)]
            obn = K.sb("obn", [128, 8, 64], BF16)
            rden = K.sb("rden", [128, 8], F32)
            obst = [K.sb(f"obst{i}", [128, 4, 128], BF16) for i in range(2)]
            pI_ = [K.ps(f"pI_{i}", [128, 512], F32) for i in range(2)]
            pMT = K.ps("pMT", [128, 8, 128], BF16)
            pL = [K.ps(f"pL{i}", [128, 4, 128], F32) for i in range(2)]
            pO = [K.ps(f"pO{i}", [128, 4, 65], F32) for i in range(2)]
            pS = K.ps("pS", [128, 512], F32)
            zb = K.sb("zb", [128, 128], BF16)
            K.op('pool', lambda e: e.memset(zb.t[:], 0.0), writes=[zb])

            def p3_scores(qb):
                i = qb % 2
                i3 = qb % 3
                L = (qb + 1) * 128
                K.dma('sp', qT[i3].t[:], qsT.t.ap()[qb], reads=[qsT], writes=[qT[i3]])
                K.op('dve', lambda e: e.tensor_tensor(dsg.t[:], identb.t[:, :].unsqueeze(1).to_broadcast([128, 8, 128]),
                                                      wsg.t[:, qb, :].unsqueeze(2).to_broadcast([128, 8, 128]), ALU.mult), reads=[identb, wsg], writes=[dsg])
                for kt in range((L + 511) // 512):
                    w = min(512, L - kt * 512)
                    ks = slice(kt * 512, kt * 512 + w)

                    def mmq(h, w=w, ks=ks):
                        pr, hf = h // 2, h % 2
                        rows = slice(hf * 64, (hf + 1) * 64)
                        ps_, rb = pI_[h % 2], rbuf[h % 4]
                        K.op('pe', lambda e: e.matmul(ps_.t[:, 0:w], qT[i3].t[rows, 4 + pr, :], kiT2.t[rows, ks], start=True, stop=True),
                             reads=[qT[i3], kiT2], writes=[ps_])
                        K.op('act', lambda e: e.activation(rb.t[:, 0:w], ps_.t[:, 0:w], AF.Relu, scale=wab.t[:, qb, h:h + 1]),
                             reads=[ps_, wab], writes=[rb])

                    def mmd(h, w=w):
                        rb = rbuf[h % 4]
                        K.op('pe', lambda e: e.matmul(pS.t[:, 0:w], dsg.t[:, h, :], rb.t[:, 0:w], start=(h == 0), stop=(h == 7)),
                             reads=[dsg, rb], writes=[pS])
                    mmq(0)
                    mmq(1)
                    for h in range(8):
                        if h + 2 < 8:
                            mmq(h + 2)
                        mmd(h)
                    K.op('act', lambda e, w=w, ks=ks: e.copy(score[i].t[:, ks], pS.t[:, 0:w]), reads=[pS], writes=[score[i]])

            def p3_mask(qb):
                i = qb % 2
                L = (qb + 1) * 128
                blk = slice(qb * 128, (qb + 1) * 128)
                K.op('dve', lambda e, L=L: e.tensor_reduce(bs.t[:, 0:1], score[i].t[:, 0:L], AX.X, ALU.max), reads=[score[i]], writes=[bs])
                K.op('dve', lambda e, L=L: e.tensor_reduce(bs.t[:, 1:2], score[i].t[:, 0:L], AX.X, ALU.min), reads=[score[i]], writes=[bs])
                K.op('dve', lambda e: e.tensor_scalar(bs.t[:, 1:2], bs.t[:, 1:2], -1.0, None, op0=ALU.add), reads=[bs], writes=[bs])
                K.op('dve', lambda e: e.scalar_tensor_tensor(bs.t[:, 2:3], bs.t[:, 0:1], 1.0, bs.t[:, 1:2], op0=ALU.add, op1=ALU.subtract), reads=[bs], writes=[bs])
                K.op('dve', lambda e: e.tensor_scalar(deltas.t[:], cvec.t[:], bs.t[:, 2:3], None, op0=ALU.mult), reads=[cvec, bs], writes=[deltas])
                K.op('dve', lambda e: e.tensor_tensor(bs.t[:, 3:4], bs.t[:, 1:2], deltas.t[:, 0:1], ALU.add), reads=[bs, deltas], writes=[bs])
                K.op('dve', lambda e, blk=blk: e.tensor_tensor(score[i].t[:, blk], score[i].t[:, blk], C('CA'), ALU.add), reads=[score[i], cst], writes=[score[i]])
                def filler():
                    K.op('dve', lambda e: e.tensor_copy(fill.t[:, 0:1], fill.t[:, 1:2]), self_sync=False)
                for k in range(NIT):
                    mid, midn = bs.t[:, 3 + k % 2:4 + k % 2], bs.t[:, 3 + (k + 1) % 2:4 + (k + 1) % 2]
                    K.op('dve', lambda e, mid=mid, L=L: e.tensor_scalar(cjunk.t[:, 0:L], score[i].t[:, 0:L], mid, 0.0, op0=ALU.is_ge, op1=ALU.add, accum_out=bs.t[:, 5:6]),
                         reads=[score[i], bs], writes=[bs], self_sync=(k == 0))
                    filler()
                    K.op('dve', lambda e: e.tensor_scalar(bs.t[:, 6:7], bs.t[:, 5:6], 255.5, 0.5, op0=ALU.is_ge, op1=ALU.subtract), reads=[bs], writes=[bs], self_sync=False)
                    filler()
                    K.op('dve', lambda e, mid=mid, midn=midn, k=k: e.scalar_tensor_tensor(midn, bs.t[:, 6:7], deltas.t[:, k:k + 1], mid, op0=ALU.mult, op1=ALU.add),
                         reads=[bs, deltas], writes=[bs], self_sync=False)
                    filler()
                midf = bs.t[:, 3 + NIT % 2:4 + NIT % 2]
                K.op('dve', lambda e, midf=midf: e.scalar_tensor_tensor(bs.t[:, 7:8], deltas.t[:, NIT:NIT + 1], -2.0, midf, op0=ALU.mult, op1=ALU.add), reads=[bs, deltas], writes=[bs])
                K.op('dve', lambda e, L=L: e.tensor_scalar(maskq[i].t[:, 0:L], score[i].t[:, 0:L], bs.t[:, 7:8], None, op0=ALU.is_ge), reads=[score[i], bs], writes=[maskq[i]])
                for kb0 in range(0, qb + 1, 8):
                    n = min(8, qb + 1 - kb0)
                    for j in range(n):
                        K.op('pe', lambda e, j=j, kb0=kb0: e.transpose(pMT.t[:, j, :], maskq[i].t[:, (kb0 + j) * 128:(kb0 + j + 1) * 128], identb.t[:]),
                             reads=[maskq[i], identb], writes=[pMT])
                    K.op('act', lambda e, kb0=kb0, n=n: e.copy(maskT[i].t[:, kb0:kb0 + n, :], pMT.t[:, 0:n, :]), reads=[pMT], writes=[maskT[i]])

            def p3_attn(qb):
                i = qb % 2
                i3 = qb % 3
                L = (qb + 1) * 128
                blk = slice(qb * 128, (qb + 1) * 128)
                for g in range(2):
                    K.op('pe', lambda e, g=g: e.matmul(pO[g].t[:].rearrange("p a b -> p (a b)"), zb.t[:],
                                                       v65.t[:, 0:2, :, :].rearrange("p a b c -> p (a b c)"), start=True, stop=False),
                         reads=[zb, v65], writes=[pO[g]])
                steps = [(kb, g) for kb in range(qb + 1) for g in range(2)]

                def front(kb, g):
                    keys = slice(kb * 128, (kb + 1) * 128)
                    rows = slice(g * 64, (g + 1) * 64)
                    for hh in range(4):
                        h = 2 * hh + g
                        K.op('pe', lambda e, g=g, hh=hh, rows=rows, keys=keys, h=h: e.matmul(
                            pL[g].t[:, hh, :], kbT2.t[rows, h // 4, keys], qT[i3].t[rows, h // 2, :], start=True, stop=True), reads=[kbT2, qT[i3]], writes=[pL[g]])
                    K.op('act', lambda e, g=g: e.activation(E_[g].t[:], pL[g].t[:], AF.Exp, scale=0.125), reads=[pL[g]], writes=[E_[g]])
                    K.op('pool', lambda e, g=g, kb=kb: e.tensor_tensor(Pm_[g].t[:], E_[g].t[:], maskT[i].t[:, kb, :].unsqueeze(1).to_broadcast([128, 4, 128]), ALU.mult),
                         reads=[E_[g], maskT[i]], writes=[Pm_[g]])

                def back(kb, g):
                    for hh in range(4):
                        h = 2 * hh + g
                        K.op('pe', lambda e, g=g, hh=hh, kb=kb, h=h: e.matmul(
                            pO[g].t[:, hh, :], Pm_[g].t[:, hh, :], v65.t[:, kb, h // 4, :], start=False, stop=(kb == qb)), reads=[Pm_[g], v65], writes=[pO[g]])
                front(*steps[0])
                for idx, st_ in enumerate(steps):
                    if idx + 1 < len(steps):
                        front(*steps[idx + 1])
                    back(*st_)

            def p3_final(qb):
                i = qb % 2
                L = (qb + 1) * 128
                blk = slice(qb * 128, (qb + 1) * 128)
                obn4 = obn.t[:].rearrange("p (a b) d -> p a b d", b=2)
                for g in range(2):
                    K.op('dve', lambda e, g=g: e.reciprocal(rden.t[:, 4 * g:4 * g + 4], pO[g].t[:, :, 64]), reads=[pO[g]], writes=[rden])
                    K.op('dve', lambda e, g=g: e.tensor_tensor(obn4[:, :, g, :], pO[g].t[:, :, 0:64],
                                                              rden.t[:, 4 * g:4 * g + 4].unsqueeze(2).to_broadcast([128, 4, 64]), ALU.mult),
                         reads=[pO[g], rden], writes=[obn])
                for c in range(4):
                    K.op('pe', lambda e, c=c: e.transpose(pMT.t[:, c, :], obn.t[:, 2 * c:2 * c + 2, :].rearrange("p h d -> p (h d)"), identb.t[:]),
                         reads=[obn, identb], writes=[pMT])
                K.op('act', lambda e, i=i: e.copy(obst[i].t[:], pMT.t[:, 0:4, :]), reads=[pMT], writes=[obst[i]])
                K.dma('sp', obT.t.ap()[:, :, blk].rearrange("c p t -> p c t"), obst[i].t[:], reads=[obst[i]], writes=[obT])

            if P3_BLOCKS > 0:
                p3_scores(0)
                p3_mask(0)
            if P3_BLOCKS > 1:
                p3_scores(1)
            for qb_ in range(P3_BLOCKS):
                if qb_ + 2 < P3_BLOCKS:
                    p3_scores(qb_ + 2)
                p3_attn(qb_)
                if qb_ + 1 < P3_BLOCKS:
                    p3_mask(qb_ + 1)
                p3_final(qb_)

        if P3S_ENABLE:
            with K.scope():
                NITS = 26
                ki_rows = cache_kidx.rearrange("n (c x) -> (n c) x", x=2048)
                kv_rows = cache_kv.rearrange("n (c x) -> (n c) x", x=4096)
                pt_i = K.sb("pt_i", [128, 4], I32)
                K.dma('sp', pt_i.t[:], page_tab.rearrange("s p -> p s"), writes=[pt_i], allow_slow_non_contiguous=True)
                pt_f = K.sb("pt_f", [128, 4], F32)
                K.op('dve', lambda e: e.tensor_copy(pt_f.t[:], pt_i.t[:]), reads=[pt_i], writes=[pt_f])
                cv16 = K.sb("cv16", [128, 8], F32)
                for c in range(8):
                    K.op('pool', lambda e, c=c: e.memset(cv16.t[:, c:c + 1], float(c)), writes=[cv16])
                ixf = K.sb("ixf", [128, 4, 8], F32)
                ixk = K.sb("ixk", [128, 4, 4], I32)
                ixv = K.sb("ixv", [128, 4, 8], I32)
                K.op('dve', lambda e: e.scalar_tensor_tensor(ixf.t[:, :, 0:4], pt_f.t[:, :].unsqueeze(2).to_broadcast([128, 4, 4]), 4.0,
                                                             cv16.t[:, 0:4].unsqueeze(1).to_broadcast([128, 4, 4]), op0=ALU.mult, op1=ALU.add),
                     reads=[pt_f, cv16], writes=[ixf])
                K.op('dve', lambda e: e.tensor_copy(ixk.t[:], ixf.t[:, :, 0:4]), reads=[ixf], writes=[ixk])
                K.op('dve', lambda e: e.scalar_tensor_tensor(ixf.t[:], pt_f.t[:, :].unsqueeze(2).to_broadcast([128, 4, 8]), 8.0,
                                                             cv16.t[:, 0:8].unsqueeze(1).to_broadcast([128, 4, 8]), op0=ALU.mult, op1=ALU.add),
                     reads=[pt_f, cv16], writes=[ixf])
                K.op('dve', lambda e: e.tensor_copy(ixv.t[:], ixf.t[:]), reads=[ixf], writes=[ixv])
                qiS = K.sb("qiS", [128, 8, 4], BF16)
                qbS = K.sb("qbS", [128, 4, 4], BF16)
                q32 = qsT.t.ap()[32]
                for h in range(8):
                    src = q32[(h % 2) * 64:(h % 2 + 1) * 64, 4 + h // 2, 0:4]
                    for hf in range(2):
                        K.dma('sp', qiS.t[hf * 64:(hf + 1) * 64, h, :], src, reads=[qsT], writes=[qiS], allow_slow_non_contiguous=True)
                    n_, i_ = h // 4, h % 4
                    K.dma('sp', qbS.t[n_ * 64:(n_ + 1) * 64, i_, :], q32[(h % 2) * 64:(h % 2 + 1) * 64, h // 2, 0:4], reads=[qsT], writes=[qbS],
                          allow_slow_non_contiguous=True)
                Dw = K.sb("Dw", [128, 4, 8], F32)
                wsig = K.sb("wsig", [128, 4, 8], F32)
                pB = K.ps("pB", [128, 512], F32)
                K.op('dve', lambda e: e.tensor_tensor(Dw.t[:], wab.t[:, 32, :].unsqueeze(1).to_broadcast([128, 4, 8]),
                                                      wsg.t[:, 32, :].unsqueeze(1).to_broadcast([128, 4, 8]), ALU.mult), reads=[wab, wsg], writes=[Dw])
                K.op('dve', lambda e: e.tensor_tensor(Dw.t[:], Dw.t[:], C('ident')[:, 0:4].unsqueeze(2).to_broadcast([128, 4, 8]), ALU.mult), reads=[Dw, cst], writes=[Dw])
                K.op('pe', lambda e: e.matmul(pB.t[:, 0:32], C('ones'), Dw.t[:].rearrange("p s c -> p (s c)"), start=True, stop=True), reads=[cst, Dw], writes=[pB])
                K.op('act', lambda e: e.copy(wsig.t[:].rearrange("p s c -> p (s c)"), pB.t[:, 0:32]), reads=[pB], writes=[wsig])
                vnew = K.sb("vnew", [1, 4, 128], F32)
                K.dma('sp', vnew.t[:], kv_out[T:T + 4, 128:256].rearrange("(o s) v -> o s v", o=1), writes=[vnew])

                scoreS = K.sb("scoreS", [128, 4, 129], F32)
                K.op('pool', lambda e: e.memset(scoreS.t[:, :, 128:129], -1e30), writes=[scoreS])
                KI = [K.sb(f"KI{i}", [128, 32, 64], F32) for i in range(2)]
                kiTs = [K.sb(f"kiTs{i}", [128, 4, 128], BF16) for i in range(2)]
                pTr = [K.ps(f"pTr{i}", [128, 4, 128], F32) for i in range(2)]
                pX = [K.ps(f"pX{i}", [128, 512], F32) for i in range(2)]
                pB2 = K.ps("pB2", [128, 512], F32)
                tmpS = K.sb("tmpS", [128, 32, 8], F32)
                ngath = 0
                for s in range(4):
                    for half in range(2):
                        for cc in range(2):
                            c = half * 2 + cc
                            kib = KI[ngath % 2]
                            ngath += 1
                            K.dma('pool', None, None, reads=[ixk], writes=[kib],
                                  fn=lambda e, kib=kib, s=s, c=c: e.indirect_dma_start(
                                      out=kib.t[:].rearrange("p j d -> p (j d)"), out_offset=None, in_=ki_rows,
                                      in_offset=bass.IndirectOffsetOnAxis(ap=ixk.t[:, s, c:c + 1], axis=0)))
                            for t4 in range(4):
                                ptr, kt = pTr[t4 % 2], kiTs[t4 % 2]
                                for u in range(4):
                                    jj = t4 * 8 + u * 2
                                    K.op('pe', lambda e, ptr=ptr, u=u, kib=kib, jj=jj: e.transpose(
                                        ptr.t[:, u, :], kib.t[:, jj:jj + 2, :].rearrange("p j d -> p (j d)"), C('ident')), reads=[kib, cst], writes=[ptr])
                                K.op('act', lambda e, ptr=ptr, kt=kt: e.copy(kt.t[:], ptr.t[:]), reads=[ptr], writes=[kt])
                                for hf in range(2):
                                    for u in range(4):
                                        jj2 = cc * 16 + t4 * 4 + u
                                        K.op('pe', lambda e, kt=kt, u=u, hf=hf, jj2=jj2, s=s: e.matmul(
                                            pX[hf].t[:, jj2 * 8:(jj2 + 1) * 8], kt.t[hf * 64:(hf + 1) * 64, u, :], qiS.t[hf * 64:(hf + 1) * 64, :, s], start=True, stop=True),
                                            reads=[kt, qiS], writes=[pX[hf]])
                        for hf in range(2):
                            K.op('dve', lambda e, s=s, hf=hf: e.scalar_tensor_tensor(tmpS.t[:], pX[hf].t[:, 0:256].rearrange("p (j h) -> p j h", h=8), 0.0,
                                                                                  wsig.t[:, s, :].unsqueeze(1).to_broadcast([128, 32, 8]),
                                                                                  op0=ALU.max, op1=ALU.mult), reads=[pX[hf], wsig], writes=[tmpS])
                            K.op('dve', lambda e, s=s, half=half, hf=hf: e.tensor_reduce(
                                scoreS.t[:, s, half * 64:(half + 1) * 64].rearrange("p (a b) -> p a b", b=2)[:, :, hf], tmpS.t[:], AX.X, ALU.add),
                                reads=[tmpS], writes=[scoreS])
                    K.op('pe', lambda e, s=s: e.matmul(pB.t[0:1, 64:72], kiT2.t[0:64, T + s:T + s + 1], qiS.t[0:64, :, s], start=True, stop=True),
                         reads=[kiT2, qiS], writes=[pB])
                    K.op('dve', lambda e, s=s: e.scalar_tensor_tensor(tmpS.t[0:1, 0, :], pB.t[0:1, 64:72], 0.0, wsig.t[0:1, s, :], op0=ALU.max, op1=ALU.mult),
                         reads=[pB, wsig, scoreS], writes=[tmpS])
                    K.op('dve', lambda e, s=s: e.tensor_reduce(scoreS.t[0:1, s, 128:129], tmpS.t[0:1, 0, :], AX.X, ALU.add), reads=[tmpS], writes=[scoreS])

                bsS = K.sb("bsS", [128, 64], F32)
                dlS = K.sb("dlS", [128, 4, NITS + 1], F32)
                cvS = K.sb("cvS", [128, NITS + 1], F32)
                for k in range(NITS + 1):
                    K.op('pool', lambda e, k=k: e.memset(cvS.t[:, k:k + 1], 2.0 ** -(k + 1)), writes=[cvS])
                cmpS = K.sb("cmpS", [128, 4, 129], F32)
                mskS = K.sb("mskS", [128, 4, 129], F32)
                K.op('dve', lambda e: e.tensor_reduce(bsS.t[:, 0:4], scoreS.t[:], AX.X, ALU.max), reads=[scoreS], writes=[bsS])
                K.op('dve', lambda e: e.tensor_reduce(bsS.t[:, 4:8], scoreS.t[:, :, 0:128], AX.X, ALU.min), reads=[scoreS], writes=[bsS])
                K.op('dve', lambda e: e.tensor_scalar(bsS.t[:, 4:8], bsS.t[:, 4:8], -1.0, None, op0=ALU.mult), reads=[bsS], writes=[bsS])
                K.op('pe', lambda e: e.transpose(pB.t[0:8, 128:256], bsS.t[:, 0:8], C('ident')), reads=[bsS, cst], writes=[pB])
                K.op('dve', lambda e: e.tensor_reduce(bsS.t[0:8, 8:9], pB.t[0:8, 128:256], AX.X, ALU.max), reads=[pB], writes=[bsS])
                dg8 = K.sb("dg8", [8, 8], F32)
                K.op('dve', lambda e: e.tensor_scalar(dg8.t[:], C('ident')[0:8, 0:8], bsS.t[0:8, 8:9], None, op0=ALU.mult), reads=[bsS, cst], writes=[dg8])
                K.op('pe', lambda e: e.matmul(pB.t[:, 256:264], C('ones')[0:8, :], dg8.t[:], start=True, stop=True), reads=[cst, dg8], writes=[pB])
                K.op('act', lambda e: e.copy(bsS.t[:, 16:24], pB.t[:, 256:264]), reads=[pB], writes=[bsS])
                K.op('dve', lambda e: e.tensor_scalar(bsS.t[:, 20:24], bsS.t[:, 20:24], 1.0, None, op0=ALU.add), reads=[bsS], writes=[bsS])
                K.op('dve', lambda e: e.scalar_tensor_tensor(bsS.t[:, 24:28], bsS.t[:, 16:20], 1.0, bsS.t[:, 20:24], op0=ALU.add, op1=ALU.add), reads=[bsS], writes=[bsS])
                K.op('dve', lambda e: e.tensor_tensor(dlS.t[:], cvS.t[:, :].unsqueeze(1).to_broadcast([128, 4, NITS + 1]),
                                                      bsS.t[:, 24:28].unsqueeze(2).to_broadcast([128, 4, NITS + 1]), ALU.mult), reads=[cvS, bsS], writes=[dlS])
                K.op('dve', lambda e: e.tensor_tensor(bsS.t[:, 32:36], dlS.t[:, :, 0], bsS.t[:, 20:24], ALU.subtract), reads=[dlS, bsS], writes=[bsS])
                for k in range(NITS):
                    mo, mn = 32 + 4 * (k % 2), 32 + 4 * ((k + 1) % 2)
                    K.op('dve', lambda e, mo=mo: e.tensor_tensor(cmpS.t[:], scoreS.t[:], bsS.t[:, mo:mo + 4].unsqueeze(2).to_broadcast([128, 4, 129]), ALU.is_ge),
                         reads=[scoreS, bsS], writes=[cmpS])
                    K.op('dve', lambda e: e.tensor_reduce(bsS.t[:, 40:44], cmpS.t[:], AX.X, ALU.add), reads=[cmpS], writes=[bsS])
                    K.op('pe', lambda e: e.matmul(pB.t[:, 300:304], C('ones'), bsS.t[:, 40:44], start=True, stop=True), reads=[cst, bsS], writes=[pB])
                    K.op('dve', lambda e: e.tensor_scalar(bsS.t[:, 44:48], pB.t[:, 300:304], 255.5, 0.5, op0=ALU.is_ge, op1=ALU.subtract), reads=[pB], writes=[bsS])
                    K.op('dve', lambda e, k=k: e.tensor_tensor(bsS.t[:, 44:48], bsS.t[:, 44:48], dlS.t[:, :, k], ALU.mult), reads=[bsS, dlS], writes=[bsS])
                    K.op('dve', lambda e, mo=mo, mn=mn: e.tensor_tensor(bsS.t[:, mn:mn + 4], bsS.t[:, mo:mo + 4], bsS.t[:, 44:48], ALU.add), reads=[bsS], writes=[bsS])
                mf = 32 + 4 * (NITS % 2)
                K.op('dve', lambda e: e.scalar_tensor_tensor(bsS.t[:, 48:52], dlS.t[:, :, NITS], -2.0, bsS.t[:, mf:mf + 4], op0=ALU.mult, op1=ALU.add), reads=[bsS, dlS], writes=[bsS])
                K.op('dve', lambda e: e.tensor_tensor(mskS.t[:], scoreS.t[:], bsS.t[:, 48:52].unsqueeze(2).to_broadcast([128, 4, 129]), ALU.is_ge),
                     reads=[scoreS, bsS], writes=[mskS])

                KV = [K.sb(f"KV{i}", [128, 16, 256], F32) for i in range(2)]
                kTs = [K.sb(f"kTs{i}", [128, 4, 128], BF16) for i in range(2)]
                pTs = [K.sb(f"pTs{i}", [128, 16, 8], F32) for i in range(2)]
                acc8 = K.sb("acc8", [128, 8], F32)
                red8 = K.sb("red8", [128, 8], F32)
                pOs = K.ps("pOs", [4, 2, 64], F32)
                pDn = K.ps("pDn", [4, 8], F32)
                pnew = K.sb("pnew", [1, 8], F32)
                obs = K.sb("obs", [4, 2, 64], BF16)
                rdn = K.sb("rdn", [4, 2], F32)
                obT_rows = obT.t.ap().rearrange("c p t -> (c p) t")
                zf4 = K.sb("zf4", [128, 4], F32)
                K.op('pool', lambda e: e.memset(zf4.t[:], 0.0), writes=[zf4])
                nkv = 0
                for s in range(4):
                    K.op('pool', lambda e: e.memset(acc8.t[:], 0.0), writes=[acc8])
                    K.op('pe', lambda e: e.matmul(pOs.t[:].rearrange("p a b -> p (a b)"), zf4.t[:], C('ident'), start=True, stop=False), reads=[zf4, cst], writes=[pOs])
                    for c in range(8):
                        kvb = KV[nkv % 2]
                        pts = pTs[nkv % 2]
                        nkv += 1
                        K.dma('pool', None, None, reads=[ixv], writes=[kvb],
                              fn=lambda e, kvb=kvb, s=s, c=c: e.indirect_dma_start(
                                  out=kvb.t[:].rearrange("p j d -> p (j d)"), out_offset=None, in_=kv_rows,
                                  in_offset=bass.IndirectOffsetOnAxis(ap=ixv.t[:, s, c:c + 1], axis=0)))
                        for t4 in range(4):
                            ptr, kt = pTr[t4 % 2], kTs[t4 % 2]
                            for u in range(4):
                                jj = t4 * 4 + u
                                K.op('pe', lambda e, ptr=ptr, u=u, kvb=kvb, jj=jj: e.transpose(ptr.t[:, u, :], kvb.t[:, jj, 0:128], C('ident')),
                                     reads=[kvb, cst], writes=[ptr])
                            K.op('act', lambda e, ptr=ptr, kt=kt: e.copy(kt.t[:], ptr.t[:]), reads=[ptr], writes=[kt])
                            for u in range(4):
                                jj = t4 * 4 + u
                                for n in range(2):
                                    K.op('pe', lambda e, kt=kt, u=u, n=n, jj=jj, s=s: e.matmul(
                                        pX[n].t[:, jj * 4:(jj + 1) * 4], kt.t[n * 64:(n + 1) * 64, u, :], qbS.t[n * 64:(n + 1) * 64, :, s], start=True, stop=True),
                                        reads=[kt, qbS], writes=[pX[n]])
                        for n in range(2):
                            K.op('act', lambda e, pts=pts, n=n: e.activation(pts.t[:, :, 4 * n:4 * n + 4], pX[n].t[:, 0:64].rearrange("p (j h) -> p j h", h=4),
                                                                            AF.Exp, scale=0.125), reads=[pX[n]], writes=[pts])
                        K.op('dve', lambda e, pts=pts, s=s, c=c: e.tensor_tensor(pts.t[:], pts.t[:], mskS.t[:, s, c * 16:(c + 1) * 16].unsqueeze(2).to_broadcast([128, 16, 8]), ALU.mult),
                             reads=[pts, mskS], writes=[pts])
                        K.op('dve', lambda e, pts=pts: e.tensor_reduce(red8.t[:], pts.t[:].rearrange("p j h -> p h j"), AX.X, ALU.add), reads=[pts], writes=[red8])
                        K.op('dve', lambda e: e.tensor_tensor(acc8.t[:], acc8.t[:], red8.t[:], ALU.add), reads=[acc8, red8], writes=[acc8])
                        for jj in range(16):
                            for n in range(2):
                                K.op('pe', lambda e, pts=pts, jj=jj, n=n, kvb=kvb, c=c: e.matmul(
                                    pOs.t[:, n, :], pts.t[:, jj, 4 * n:4 * n + 4], kvb.t[:, jj, 128 + n * 64:128 + (n + 1) * 64],
                                    start=False, stop=False), reads=[pts, kvb], writes=[pOs])
                    for n in range(2):
                        pbn = pB if n == 0 else pB2
                        K.op('pe', lambda e, n=n, s=s, pbn=pbn: e.matmul(pbn.t[0:1, 320:324], kbT2.t[n * 64:(n + 1) * 64, n, T + s:T + s + 1],
                                                                        qbS.t[n * 64:(n + 1) * 64, :, s], start=True, stop=True), reads=[kbT2, qbS], writes=[pbn])
                        K.op('act', lambda e, n=n, pbn=pbn: e.activation(pnew.t[:, 4 * n:4 * n + 4], pbn.t[0:1, 320:324], AF.Exp, scale=0.125), reads=[pbn], writes=[pnew])
                    K.op('dve', lambda e, s=s: e.tensor_scalar(pnew.t[:], pnew.t[:], mskS.t[0:1, s, 128:129], None, op0=ALU.mult), reads=[pnew, mskS], writes=[pnew])
                    K.op('dve', lambda e: e.tensor_tensor(acc8.t[0:1, :], acc8.t[0:1, :], pnew.t[:], ALU.add), reads=[acc8, pnew], writes=[acc8])
                    for n in range(2):
                        K.op('pe', lambda e, n=n, s=s: e.matmul(pOs.t[:, n, :], pnew.t[0:1, 4 * n:4 * n + 4], vnew.t[0:1, s, n * 64:(n + 1) * 64],
                                                               start=False, stop=True), reads=[pnew, vnew], writes=[pOs])
                        K.op('pe', lambda e, n=n: e.matmul(pDn.t[:, n:n + 1], acc8.t[:, 4 * n:4 * n + 4], C('ones')[:, 0:1], start=True, stop=True),
                             reads=[acc8, cst], writes=[pDn])
                    K.op('dve', lambda e: e.reciprocal(rdn.t[:], pDn.t[:, 0:2]), reads=[pDn], writes=[rdn])
                    K.op('dve', lambda e: e.tensor_tensor(obs.t[:], pOs.t[:], rdn.t[:, :].unsqueeze(2).to_broadcast([4, 2, 64]), ALU.mult), reads=[pOs, rdn], writes=[obs])
                    for n in range(2):
                        K.dma('sp', obT_rows[n * 256:(n + 1) * 256, T + s:T + s + 1].rearrange("(r d) o -> r (d o)", d=64), obs.t[:, n, :],
                              reads=[obs], writes=[obT], allow_slow_non_contiguous=True)


        K.pop_scope()

        hS = K.dram("hS_s", [TT, D], F32)
        hnS = K.dram("hnS_s", [8, 128, TT], BF16)
        with K.scope():
            woa = K.sb("woa", [128, 4, D], BF16)
            wob = K.sb("wob", [128, 4, D], BF16)
            wo = K.sb("wo", [128, 8, D], BF16)
            g2T = K.sb("g2T", [128, 8], F32)
            K.dma('sp', g2T.t[:], norm2.rearrange("(k p) -> p k", p=128), writes=[g2T], allow_slow_non_contiguous=True)
            wst4 = [K.sb(f"wst4{i}", [128, D], F32) for i in range(2)]
            nld = 0
            for (src, dst, nk) in ((w_oa, woa, 4), (w_ob, wob, 4), (w_oo, wo, 8)):
                for k in range(nk):
                    st_ = wst4[nld % 2]
                    K.dma('sp', st_.t[:], src[k * 128:(k + 1) * 128, :], writes=[st_])
                    K.op('pool' if nld % 2 else 'act', (lambda e, st_=st_, dst=dst, k=k: e.tensor_copy(dst.t[:, k, :], st_.t[:])) if nld % 2 else
                         (lambda e, st_=st_, dst=dst, k=k: e.copy(dst.t[:, k, :], st_.t[:])), reads=[st_], writes=[dst])
                    nld += 1
            oat = [K.sb(f"oat{i}", [128, 4, 512], BF16) for i in range(2)]
            obt = [K.sb(f"obt{i}", [128, 4, 512], BF16) for i in range(2)]
            sga = [K.sb(f"sga{i}", [128, 512], F32) for i in range(2)]
            sgb = [K.sb(f"sgb{i}", [128, 512], F32) for i in range(2)]
            m1 = K.sb("m1", [128, 512], F32)
            m2 = K.sb("m2", [128, 512], F32)
            mixT = K.sb("mixT", [128, 8, 512], BF16)
            pMa = [K.ps(f"pMa{i}", [128, 512], F32) for i in range(2)]
            pMb = [K.ps(f"pMb{i}", [128, 512], F32) for i in range(2)]
            pH = [K.ps(f"pH{i}", [128, 512], F32) for i in range(2)]
            pT4 = K.ps("pT4", [128, 8, 128], BF16)
            xb = [K.sb(f"xb{i}", [128, D], F32) for i in range(2)]
            hb = [K.sb(f"hb{i}", [128, D], F32) for i in range(2)]
            junk4 = K.sb("junk4", [128, D], F32)
            st4 = K.sb("st4", [128, 4], F32)
            hs4 = K.sb("hs4", [128, D], BF16)
            hnt = [K.sb(f"hnt{i}", [128, 8, 128], BF16) for i in range(2)]
            nblk = 0
            for t in range(P4A_TILES):
                w = 512 if t < 8 else 128
                tok = slice(t * 512, t * 512 + w)
                i = t % 2
                K.dma('sp', oat[i].t[:, :, 0:w], oaT.t.ap()[:, :, tok].rearrange("c p t -> p c t"), reads=[oaT], writes=[oat[i]])
                K.dma('sp', obt[i].t[:, :, 0:w], obT.t.ap()[:, :, tok].rearrange("c p t -> p c t"), reads=[obT], writes=[obt[i]])
                for oc in range(8):
                    j = oc % 2
                    K.dma('sp', sga[j].t[:, 0:w], sgT.t.ap()[oc, :, tok], reads=[sgT], writes=[sga[j]])
                    K.dma('sp', sgb[j].t[:, 0:w], sgT.t.ap()[8 + oc, :, tok], reads=[sgT], writes=[sgb[j]])
                    for k in range(4):
                        K.op('pe', lambda e, j=j, k=k, oc=oc, i=i, w=w: e.matmul(pMa[j].t[:, 0:w], woa.t[:, k, oc * 128:(oc + 1) * 128], oat[i].t[:, k, 0:w],
                                                                              start=(k == 0), stop=(k == 3)), reads=[woa, oat[i]], writes=[pMa[j]])
                    for k in range(4):
                        K.op('pe', lambda e, j=j, k=k, oc=oc, i=i, w=w: e.matmul(pMb[j].t[:, 0:w], wob.t[:, k, oc * 128:(oc + 1) * 128], obt[i].t[:, k, 0:w],
                                                                              start=(k == 0), stop=(k == 3)), reads=[wob, obt[i]], writes=[pMb[j]])
                    K.op('dve', lambda e, j=j, w=w: e.tensor_tensor(m1.t[:, 0:w], pMa[j].t[:, 0:w], sga[j].t[:, 0:w], ALU.mult), reads=[pMa[j], sga[j]], writes=[m1])
                    K.op('dve', lambda e, j=j, w=w: e.tensor_tensor(m2.t[:, 0:w], pMb[j].t[:, 0:w], sgb[j].t[:, 0:w], ALU.mult), reads=[pMb[j], sgb[j]], writes=[m2])
                    K.op('pool', lambda e, oc=oc, w=w: e.tensor_tensor(mixT.t[:, oc, 0:w], m1.t[:, 0:w], m2.t[:, 0:w], ALU.add), reads=[m1, m2], writes=[mixT])
                for bb in range(w // 128):
                    b = t * 4 + bb
                    blk = slice(b * 128, (b + 1) * 128)
                    i2 = nblk % 2
                    nblk += 1
                    K.dma('sp', xb[i2].t[:], x_in[blk, :], writes=[xb[i2]])
                    for half in range(2):
                        ph = pH[half]
                        for k in range(8):
                            K.op('pe', lambda e, ph=ph, k=k, bb=bb, half=half: e.matmul(ph.t[:], mixT.t[:, k, bb * 128:(bb + 1) * 128], wo.t[:, k, half * 512:(half + 1) * 512],
                                                                                  start=(k == 0), stop=(k == 7)), reads=[mixT, wo], writes=[ph])
                        K.op('dve', lambda e, ph=ph, half=half, i2=i2: e.tensor_tensor(hb[i2].t[:, half * 512:(half + 1) * 512], ph.t[:], xb[i2].t[:, half * 512:(half + 1) * 512], ALU.add),
                             reads=[ph, xb[i2]], writes=[hb[i2]])
                    K.dma('sp', hS.t.ap()[blk, :], hb[i2].t[:], reads=[hb[i2]], writes=[hS])
                    K.op('act', lambda e, i2=i2: e.activation(junk4.t[:], hb[i2].t[:], AF.Square, accum_out=st4.t[:, 0:1]), reads=[hb[i2]], writes=[junk4, st4])
                    K.op('act', lambda e: e.activation(st4.t[:, 1:2], st4.t[:, 0:1], AF.Sqrt, bias=EPS, scale=1.0 / D), reads=[st4], writes=[st4])
                    K.op('dve', lambda e: e.reciprocal(st4.t[:, 2:3], st4.t[:, 1:2]), reads=[st4], writes=[st4])
                    K.op('dve', lambda e, i2=i2: e.tensor_scalar(hs4.t[:], hb[i2].t[:], st4.t[:, 2:3], None, op0=ALU.mult), reads=[hb[i2], st4], writes=[hs4])
                    for k in range(8):
                        K.op('pe', lambda e, k=k: e.transpose(pT4.t[:, k, :], hs4.t[:, k * 128:(k + 1) * 128], identb.t[:]), reads=[hs4, identb], writes=[pT4])
                    K.op('dve', lambda e, i2=i2: e.tensor_tensor(hnt[i2].t[:], pT4.t[:], g2T.t[:, :].unsqueeze(2).to_broadcast([128, 8, 128]), ALU.mult),
                         reads=[pT4, g2T], writes=[hnt[i2]])
                    K.dma('sp', hnS.t.ap()[:, :, blk].rearrange("c p t -> p c t"), hnt[i2].t[:], reads=[hnt[i2]], writes=[hnS])

        with K.scope():
            wup = K.sb("wup", [128, 8, 4 * D], BF16)
            wdn = K.sb("wdn", [128, 32, D], BF16)
            wsu = [K.sb(f"wsu{i}", [128, 2048], F32) for i in range(2)]
            nld = 0
            engs = ('act', 'pool', 'dve')
            for k in range(8):
                for hf in range(2):
                    st_ = wsu[nld % 2]
                    K.dma('sp', st_.t[:], w_up_in[k * 128:(k + 1) * 128, hf * 2048:(hf + 1) * 2048], writes=[st_])
                    en = engs[nld % 3]
                    if en == 'act':
                        K.op('act', lambda e, st_=st_, k=k, hf=hf: e.copy(wup.t[:, k, hf * 2048:(hf + 1) * 2048], st_.t[:]), reads=[st_], writes=[wup])
                    else:
                        K.op(en, lambda e, st_=st_, k=k, hf=hf: e.tensor_copy(wup.t[:, k, hf * 2048:(hf + 1) * 2048], st_.t[:]), reads=[st_], writes=[wup])
                    nld += 1
            for kk in range(16):
                st_ = wsu[nld % 2]
                K.dma('sp', st_.t[:].rearrange("p (a c) -> p a c", a=2), w_dn_in[kk * 256:(kk + 1) * 256, :].rearrange("(a p) c -> p a c", p=128), writes=[st_])
                en = engs[nld % 3]
                dstap = wdn.t[:, 2 * kk:2 * kk + 2, :].rearrange("p a c -> p (a c)")
                if en == 'act':
                    K.op('act', lambda e, st_=st_, dstap=dstap: e.copy(dstap, st_.t[:]), reads=[st_], writes=[wdn])
                else:
                    K.op(en, lambda e, st_=st_, dstap=dstap: e.tensor_copy(dstap, st_.t[:]), reads=[st_], writes=[wdn])
                nld += 1
            hn_t = [K.sb(f"hn_t{i}", [128, 8, 256], BF16) for i in range(2)]
            actT = K.sb("actT", [128, 32, 256], BF16)
            rl = [K.sb(f"rl{i}", [128, 256], BF16) for i in range(2)]
            pUp = [K.ps(f"pUp{i}", [128, 256], F32) for i in range(3)]
            pY = [K.ps(f"pY{i}", [128, 512], F32) for i in range(4)]
            hb2 = [K.sb(f"hb2{i}", [128, D], F32) for i in range(2)]
            yb = [K.sb(f"yb{i}", [128, D], F32) for i in range(2)]
            nblk = 0
            npy = 0
            for t in range(P4B_TILES):
                w = 256 if t < 16 else 128
                tok = slice(t * 256, t * 256 + w)
                i = t % 2
                K.dma('sp', hn_t[i].t[:, :, 0:w], hnS.t.ap()[:, :, tok].rearrange("c p t -> p c t"), reads=[hnS], writes=[hn_t[i]])
                for f in range(32):
                    pu = pUp[f % 3]
                    for k in range(8):
                        K.op('pe', lambda e, pu=pu, k=k, f=f, i=i, w=w: e.matmul(pu.t[:, 0:w], wup.t[:, k, f * 128:(f + 1) * 128], hn_t[i].t[:, k, 0:w],
                                                                              start=(k == 0), stop=(k == 7)), reads=[wup, hn_t[i]], writes=[pu])
                    r_ = rl[f % 2]
                    K.op('act', lambda e, pu=pu, r_=r_, w=w: e.activation(r_.t[:, 0:w], pu.t[:, 0:w], AF.Relu), reads=[pu], writes=[r_])
                    K.op('pool' if f % 2 else 'dve', lambda e, r_=r_, f=f, w=w: e.tensor_tensor(actT.t[:, f, 0:w], r_.t[:, 0:w], r_.t[:, 0:w], ALU.mult), reads=[r_], writes=[actT])
                for bb in range(w // 128):
                    b = t * 2 + bb
                    blk = slice(b * 128, (b + 1) * 128)
                    i2 = nblk % 2
                    nblk += 1
                    K.dma('sp', hb2[i2].t[:], hS.t.ap()[blk, :], reads=[hS], writes=[hb2[i2]])
                    for half in range(2):
                        py = pY[npy % 4]
                        npy += 1
                        for f in range(32):
                            K.op('pe', lambda e, py=py, f=f, bb=bb, half=half: e.matmul(py.t[:], actT.t[:, f, bb * 128:(bb + 1) * 128], wdn.t[:, f, half * 512:(half + 1) * 512],
                                                                                  start=(f == 0), stop=(f == 31)), reads=[actT, wdn], writes=[py])
                        K.op('dve', lambda e, py=py, half=half, i2=i2: e.tensor_tensor(yb[i2].t[:, half * 512:(half + 1) * 512], py.t[:], hb2[i2].t[:, half * 512:(half + 1) * 512], ALU.add),
                             reads=[py, hb2[i2]], writes=[yb[i2]])
                    K.dma('sp', y_out[blk, :], yb[i2].t[:], reads=[yb[i2]])

        K.finish()
    return nc


_CACHE = {}


def kernel(x_prompt, x_sample, cache_kv, cache_kidx, page_table, state_ssm, state_conv,
           norm1_g, w_in, conv_w, a_log, dt_bias, gdn_norm_g, q_norm_g, k_norm_g,
           w_out_a, w_out_b, w_o, norm2_g, w_up, w_down):
    if 'nc' not in _CACHE:
        _CACHE['nc'] = build_program()
    nc = _CACHE['nc']
    f = lambda a: np.ascontiguousarray(np.asarray(a, dtype=np.float32))
    cst = make_consts()
    cs = rope_table()
    ckv = np.ascontiguousarray(np.asarray(cache_kv[0], dtype=np.float32).reshape(NPOOL, 128 * 256))
    cki = np.ascontiguousarray(np.asarray(cache_kidx[0], dtype=np.float32).reshape(NPOOL, 128 * 64))
    ptab = np.ascontiguousarray(np.asarray(page_table, dtype=np.int32))
    in_maps = []
    for c in range(NCR):
        x = np.zeros((TT, D), np.float32)
        x[:T] = x_prompt[c]
        x[T:T + 4] = np.asarray(x_sample)[4 * c:4 * c + 4, 0]
        in_maps.append({
            "x": x, "cs": cs, "cst": cst,
            "w_in": f(w_in[0]), "norm1_g": f(norm1_g[0]), "conv_w": f(conv_w[0]),
            "a_log": f(a_log[0]), "dt_bias": f(dt_bias[0]), "gdn_norm_g": f(gdn_norm_g[0]),
            "q_norm_g": f(q_norm_g[0]), "k_norm_g": f(k_norm_g[0]),
            "state_conv": f(state_conv[0][4 * c:4 * c + 4]),
            "state_ssm": f(state_ssm[0][4 * c:4 * c + 4]),
            "w_out_a": f(w_out_a[0]), "w_out_b": f(w_out_b[0]), "w_o": f(w_o[0]), "norm2_g": f(norm2_g[0]),
            "w_up": f(w_up[0]), "w_down": f(w_down[0]),
            "cache_kidx": cki, "cache_kv": ckv, "page_tab": np.ascontiguousarray(ptab[4 * c:4 * c + 4]),
        })
    res = run_bass_kernel_spmd(nc, in_maps, core_ids=list(range(NCR)))
    R = list(res.results)
    while len(R) < NCORES:
        R.append(R[0])
    if DEBUG_OUT:
        _DBG['r0'] = R[0]
    y = np.stack([r["y"] for r in R])
    kv = np.stack([r["kv"] for r in R])
    kidx = np.stack([r["kidx"] for r in R])
    y_prompt = y[:, :T]
    y_sample = y[:, T:T + 4].reshape(32, 1, D)
    kv_prompt = kv[:, :T].reshape(1, 8, T, 2, 2, 64)
    kv_sample = kv[:, T:T + 4].reshape(1, 32, 1, 2, 2, 64)
    kidx_prompt = kidx[:, :T].reshape(1, 8, T, 64)
    kidx_sample = kidx[:, T:T + 4].reshape(1, 32, 1, 64)
    conv_prompt = np.stack([r["conv_p"] for r in R]).reshape(1, 8, 3, 1536)
    conv_sample = np.stack([r["conv_s"] for r in R]).reshape(1, 32, 3, 1536)
    ssm_prompt = np.stack([r["ssm_p"] for r in R]).reshape(1, 8, 4, 128, 128)
    ssm_sample = np.stack([r["ssm_s"] for r in R]).reshape(1, 32, 4, 128, 128)
    return (np.ascontiguousarray(y_prompt), y_sample, kv_prompt, kidx_prompt, ssm_prompt, conv_prompt,
            kv_sample, kidx_sample, ssm_sample, conv_sample)
```
